# Optimizing a Trainium2 kernel written in Bass

```python
import jax, jax.numpy as jnp
from jax import lax
import numpy as np

D_MODEL = 1024
BATCH = 8
SEQ = 4096
DEPTH = 4

N_EVEN = (DEPTH + 1) // 2
N_ODD = DEPTH // 2
NORM_EPS = 1e-6
D_FF = 2816

CONV_WIDTH = D_MODEL // 2
CONV_GROUPS = 8
CONV_TAPS = 3
RWKV_WIDTH = D_MODEL - CONV_WIDTH
RWKV_HEAD = 64
RWKV_HEADS = RWKV_WIDTH // RWKV_HEAD
DECAY_RANK = 32
ICLR_RANK = 32
GATE_RANK = 96
RWKV_GN_EPS = 64e-5
RWKV_SHIFT_COLS = 3 * RWKV_WIDTH + DECAY_RANK + ICLR_RANK + GATE_RANK
EVEN_IN = 3 * CONV_WIDTH + RWKV_SHIFT_COLS
_RWKV_SPLITS = (RWKV_WIDTH, 2 * RWKV_WIDTH, 3 * RWKV_WIDTH,
                3 * RWKV_WIDTH + DECAY_RANK, 3 * RWKV_WIDTH + DECAY_RANK + ICLR_RANK)

MLA_HEADS = 8
Q_RANK = 384
KV_RANK = 256
NOPE_DIM = 128
ROPE_DIM = 64
V_DIM = 128
QK_DIM = NOPE_DIM + ROPE_DIM
ODD_IN = Q_RANK + KV_RANK + ROPE_DIM
ROPE_THETA = 10000.0
Q_BLOCK = 128
ATTN_SCALE = QK_DIM ** -0.5

kernel_name = "hybrid_conv_rwkv7_mla_macaron"


def _rmsnorm(x, gain, eps=NORM_EPS):
    xf = x.astype(jnp.float32)
    y = xf * lax.rsqrt(jnp.mean(xf * xf, axis=-1, keepdims=True) + eps)
    return (y * gain.astype(jnp.float32)).astype(x.dtype)


def _swiglu(h, w_gate, w_up, w_down):
    return (jax.nn.silu(h @ w_gate) * (h @ w_up)) @ w_down


def _shift(u):
    return jnp.pad(u, ((0, 0), (1, 0), (0, 0)))[:, :-1]


def _short_conv(u, w):
    T = u.shape[1]
    up = jnp.pad(u, ((0, 0), (CONV_TAPS - 1, 0), (0, 0)))
    return sum(up[:, j:j + T] * w[j] for j in range(CONV_TAPS))


def _rwkv7_scan(r, decay, k, v, a_vec, b_vec):
    B, T, H, N = r.shape

    def step(S, inp):
        r_t, w_t, k_t, v_t, a_t, b_t = inp
        Sa = jnp.einsum('bhvk,bhk->bhv', S, a_t)
        S = S * w_t[:, :, None, :] + Sa[..., None] * b_t[:, :, None, :] + v_t[..., None] * k_t[:, :, None, :]
        return S, jnp.einsum('bhvk,bhk->bhv', S, r_t)

    xs = tuple(jnp.moveaxis(t, 1, 0) for t in (r, decay, k, v, a_vec, b_vec))
    S0 = jnp.zeros((B, H, N, N), jnp.float32)
    _, y = lax.scan(step, S0, xs)
    return jnp.moveaxis(y, 0, 1)


def _conv_rwkv_mixer(h, w_in, conv_w, mu_shift, w0, w_up, a0, a_up, g_up,
                     k_k, k_a, r_k, ln_w, ln_b, w_out):
    B, T, _ = h.shape
    f32 = jnp.float32
    p = h @ w_in
    p_conv, p_rwkv = p[..., :3 * CONV_WIDTH], p[..., 3 * CONV_WIDTH:]
    gate_b, gate_c, h_in = jnp.split(p_conv, 3, axis=-1)
    y_conv = gate_b * _short_conv(gate_c * h_in, conv_w)
    p_rwkv = p_rwkv + (_shift(p_rwkv) - p_rwkv) * mu_shift
    r, k, v, dw, da, dg = jnp.split(p_rwkv, _RWKV_SPLITS, axis=-1)
    log_w = -jax.nn.softplus(-(w0 + jnp.tanh(dw) @ w_up).astype(f32)) - 0.5
    decay = jnp.exp(-jnp.exp(log_w))
    a = jax.nn.sigmoid((a0 + da @ a_up).astype(f32))
    g = jax.nn.sigmoid(dg) @ g_up

    def heads(t):
        return t.astype(f32).reshape(B, T, RWKV_HEADS, RWKV_HEAD)

    r_h, v_h, a_h, decay_h = heads(r), heads(v), heads(a), heads(decay)
    kk = heads(k * k_k)
    kk = kk * lax.rsqrt(jnp.maximum(jnp.sum(kk * kk, axis=-1, keepdims=True), 1e-24))
    k_h = heads(k) * (1.0 + (a_h - 1.0) * k_a.astype(f32).reshape(RWKV_HEADS, RWKV_HEAD))
    y = _rwkv7_scan(r_h, decay_h, k_h, v_h, -kk, kk * a_h)
    mu = jnp.mean(y, axis=-1, keepdims=True)
    var = jnp.mean(jnp.square(y - mu), axis=-1, keepdims=True)
    y = (y - mu) * lax.rsqrt(var + RWKV_GN_EPS) * ln_w.astype(f32).reshape(RWKV_HEADS, RWKV_HEAD) \
        + ln_b.astype(f32).reshape(RWKV_HEADS, RWKV_HEAD)
    y = y + jnp.sum(r_h * k_h * r_k.astype(f32), axis=-1, keepdims=True) * v_h
    y_rwkv = y.reshape(B, T, RWKV_WIDTH).astype(h.dtype) * g
    return jnp.concatenate([y_conv, y_rwkv], axis=-1) @ w_out


def _rope(t, cos, sin):
    t1, t2 = t[..., :ROPE_DIM // 2], t[..., ROPE_DIM // 2:]
    return jnp.concatenate([t1 * cos - t2 * sin, t2 * cos + t1 * sin], axis=-1)


def _causal_attention(q, k, v):
    B, T, H, Dk = q.shape
    nb = T // Q_BLOCK
    qb = q.reshape(B, nb, Q_BLOCK, H, Dk).transpose(1, 0, 3, 2, 4)
    kpos = jnp.arange(T)

    def one_block(args):
        qi, i = args
        s = jnp.einsum('bhqd,bkhd->bhqk', qi, k, preferred_element_type=jnp.float32) * ATTN_SCALE
        qpos = i * Q_BLOCK + jnp.arange(Q_BLOCK)
        s = jnp.where(kpos[None, :] <= qpos[:, None], s, jnp.finfo(jnp.float32).min)
        p = jax.nn.softmax(s, axis=-1)
        return jnp.einsum('bhqk,bkhd->bqhd', p.astype(v.dtype), v)

    out = lax.map(one_block, (qb, jnp.arange(nb)))
    return out.transpose(1, 0, 2, 3, 4).reshape(B, T, H, v.shape[-1])


def _mla_mixer(h, cos, sin, w_in, q_a_norm, kv_a_norm, w_q_up, w_kv_up, q_norm, k_norm, w_out):
    B, T, _ = h.shape
    p = h @ w_in
    c_q, c_kv, k_pe = jnp.split(p, (Q_RANK, Q_RANK + KV_RANK), axis=-1)
    q = (_rmsnorm(c_q, q_a_norm) @ w_q_up).reshape(B, T, MLA_HEADS, QK_DIM)
    kv = (_rmsnorm(c_kv, kv_a_norm) @ w_kv_up).reshape(B, T, MLA_HEADS, NOPE_DIM + V_DIM)
    k_nope, v = kv[..., :NOPE_DIM], kv[..., NOPE_DIM:]
    k = jnp.concatenate([k_nope, jnp.broadcast_to(k_pe[:, :, None, :], (B, T, MLA_HEADS, ROPE_DIM))], axis=-1)
    q = _rmsnorm(q, q_norm)
    k = _rmsnorm(k, k_norm)
    q = jnp.concatenate([q[..., :NOPE_DIM], _rope(q[..., NOPE_DIM:], cos, sin)], axis=-1)
    k = jnp.concatenate([k[..., :NOPE_DIM], _rope(k[..., NOPE_DIM:], cos, sin)], axis=-1)
    o = _causal_attention(q, k, v)
    return o.reshape(B, T, MLA_HEADS * V_DIM) @ w_out


def setup_inputs(seed: int = 0) -> dict:
    key = jax.random.key(seed)
    ks = iter(jax.random.split(key, 32))
    f32 = jnp.float32

    def nrm(shape, fan_in):
        return jax.random.normal(next(ks), shape, f32) * fan_in ** -0.5

    def gain(shape):
        return 1.0 + 0.02 * jax.random.normal(next(ks), shape, f32)

    def unif(shape, lo, hi):
        return jax.random.uniform(next(ks), shape, f32, lo, hi)

    x = jax.random.normal(next(ks), (BATCH, SEQ, D_MODEL), f32)
    offset = jax.random.randint(next(ks), (BATCH, 1), 0, 1024, jnp.int32)
    positions = (offset + jnp.arange(SEQ, dtype=jnp.int32)[None, :]).astype(jnp.int32)
    return {
        "x": x,
        "positions": positions,
        "norm_gains": gain((DEPTH, 3, D_MODEL)),
        "ffn_w_gate": nrm((DEPTH, 2, D_MODEL, D_FF), D_MODEL),
        "ffn_w_up": nrm((DEPTH, 2, D_MODEL, D_FF), D_MODEL),
        "ffn_w_down": nrm((DEPTH, 2, D_FF, D_MODEL), D_FF),
        "even_w_in": nrm((N_EVEN, D_MODEL, EVEN_IN), D_MODEL),
        "even_conv_w": nrm((N_EVEN, CONV_TAPS, CONV_WIDTH), CONV_TAPS),
        "even_mu_shift": unif((N_EVEN, RWKV_SHIFT_COLS), 0.0, 1.0),
        "rwkv_w0": unif((N_EVEN, RWKV_WIDTH), -6.0, 0.0),
        "rwkv_w_up": nrm((N_EVEN, DECAY_RANK, RWKV_WIDTH), DECAY_RANK),
        "rwkv_a0": 0.1 * jax.random.normal(next(ks), (N_EVEN, RWKV_WIDTH), f32),
        "rwkv_a_up": nrm((N_EVEN, ICLR_RANK, RWKV_WIDTH), ICLR_RANK),
        "rwkv_g_up": nrm((N_EVEN, GATE_RANK, RWKV_WIDTH), GATE_RANK),
        "rwkv_k_k": 0.85 + 0.05 * jax.random.normal(next(ks), (N_EVEN, RWKV_WIDTH), f32),
        "rwkv_k_a": gain((N_EVEN, RWKV_WIDTH)),
        "rwkv_r_k": 0.1 * jax.random.normal(next(ks), (N_EVEN, RWKV_HEADS, RWKV_HEAD), f32),
        "rwkv_ln_w": gain((N_EVEN, RWKV_WIDTH)),
        "rwkv_ln_b": 0.02 * jax.random.normal(next(ks), (N_EVEN, RWKV_WIDTH), f32),
        "even_w_out": nrm((N_EVEN, CONV_WIDTH + RWKV_WIDTH, D_MODEL), CONV_WIDTH + RWKV_WIDTH),
        "odd_w_in": nrm((N_ODD, D_MODEL, ODD_IN), D_MODEL),
        "mla_q_a_norm": gain((N_ODD, Q_RANK)),
        "mla_kv_a_norm": gain((N_ODD, KV_RANK)),
        "mla_w_q_up": nrm((N_ODD, Q_RANK, MLA_HEADS * QK_DIM), Q_RANK),
        "mla_w_kv_up": nrm((N_ODD, KV_RANK, MLA_HEADS * (NOPE_DIM + V_DIM)), KV_RANK),
        "mla_q_norm": gain((N_ODD, QK_DIM)),
        "mla_k_norm": gain((N_ODD, QK_DIM)),
        "odd_w_out": nrm((N_ODD, MLA_HEADS * V_DIM, D_MODEL), MLA_HEADS * V_DIM),
    }


def reference(x, positions, norm_gains, ffn_w_gate, ffn_w_up, ffn_w_down,
              even_w_in, even_conv_w, even_mu_shift, rwkv_w0, rwkv_w_up, rwkv_a0, rwkv_a_up,
              rwkv_g_up, rwkv_k_k, rwkv_k_a, rwkv_r_k, rwkv_ln_w, rwkv_ln_b, even_w_out,
              odd_w_in, mla_q_a_norm, mla_kv_a_norm, mla_w_q_up, mla_w_kv_up,
              mla_q_norm, mla_k_norm, odd_w_out):
    inv_freq = ROPE_THETA ** (-jnp.arange(0, ROPE_DIM, 2, dtype=jnp.float32) / ROPE_DIM)
    ang = positions.astype(jnp.float32)[..., None] * inv_freq
    cos = jnp.cos(ang)[:, :, None, :].astype(x.dtype)
    sin = jnp.sin(ang)[:, :, None, :].astype(x.dtype)

    for layer in range(DEPTH):
        g = norm_gains[layer]
        x = x + 0.5 * _swiglu(_rmsnorm(x, g[0]), ffn_w_gate[layer, 0], ffn_w_up[layer, 0], ffn_w_down[layer, 0])
        h = _rmsnorm(x, g[1])
        if layer % 2 == 0:
            i = layer // 2
            mix = _conv_rwkv_mixer(h, even_w_in[i], even_conv_w[i], even_mu_shift[i], rwkv_w0[i],
                                   rwkv_w_up[i], rwkv_a0[i], rwkv_a_up[i], rwkv_g_up[i], rwkv_k_k[i],
                                   rwkv_k_a[i], rwkv_r_k[i], rwkv_ln_w[i], rwkv_ln_b[i], even_w_out[i])
        else:
            j = layer // 2
            mix = _mla_mixer(h, cos, sin, odd_w_in[j], mla_q_a_norm[j], mla_kv_a_norm[j], mla_w_q_up[j],
                             mla_w_kv_up[j], mla_q_norm[j], mla_k_norm[j], odd_w_out[j])
        x = x + mix.astype(x.dtype)
        x = x + 0.5 * _swiglu(_rmsnorm(x, g[2]), ffn_w_gate[layer, 1], ffn_w_up[layer, 1], ffn_w_down[layer, 1])
    return x
```

```python
import contextlib
import numpy as np
import concourse.bass as bass
import concourse.mybir as mybir
from concourse.bass_utils import run_bass_kernel_spmd

F32 = mybir.dt.float32
BF16 = mybir.dt.bfloat16
I32 = mybir.dt.int32
AF = mybir.ActivationFunctionType
ALU = mybir.AluOpType

D = 1024
DFF = 2816
NM = DFF // 128
NC8 = D // 128
EPS = 1e-6
TT = 512


class Buf:
    __slots__ = ("name", "w", "r", "dsem", "dcnt", "t", "kind")

    def __init__(self, name, t=None, kind="x"):
        self.name = name
        self.kind = kind
        self.w = None
        self.r = {}
        self.dsem = None
        self.dcnt = 0
        self.t = t

    def __getitem__(self, idx):
        return self.t[idx]


ENGS = ("pe", "act", "dve", "pool", "sp")


class _Rec:
    def __init__(self):
        self.call = None

    def __getattr__(self, name):
        def f(*a, **k):
            assert self.call is None
            self.call = (name, a, k)
            return None
        return f


class Prog:
    def __init__(self, nc, stack):
        self.nc = nc
        self.stack = stack
        self.q = {e: [] for e in ENGS}
        self.sems = []
        self.esem = {}
        for e in ENGS:
            self.esem[e] = self.new_sem("e_" + e)
        self.cnt = {e: 0 for e in ENGS}
        self.seen = {e: {} for e in ENGS}
        self.nbuf = 0
        self.psum_rr = 0
        self.psums = []
        self.same_eng_sync = True
        self.dbufs = []
        self.free_dsems = [[], []]
        self.pe_mode = None
        self.pe_drain = False
        self.dsem_issued = {}
        self.pspool = None

    def new_sem(self, name):
        s = self.stack.enter_context(self.nc.semaphore(name))
        self.sems.append(s)
        return len(self.sems) - 1

    def sb(self, name, shape, dtype):
        t = self.stack.enter_context(self.nc.sbuf_tensor(name, list(shape), dtype))
        return Buf(name, t, "sb")

    def ps(self, name, shape, dtype=F32):
        t = self.stack.enter_context(self.nc.psum_tensor(name, list(shape), dtype))
        return Buf(name, t, "ps")

    def dram(self, name, shape, dtype, kind="Internal"):
        t = self.nc.dram_tensor(name, list(shape), dtype, kind=kind)
        return Buf(name, t.ap(), "dram")

    def view(self, b, name):
        return Buf(name, b.t, b.kind)

    def op(self, eng, fn, reads=(), writes=(), dma=None):
        waits = {}
        seen = self.seen[eng]

        def need(tok):
            if tok is None:
                return
            s, v = tok
            if eng == "pe" and s == self.esem["pe"]:
                return
            if (not self.same_eng_sync) and s == self.esem[eng]:
                return
            if s in self.dsem_issued:
                v = max(v, self.dsem_issued[s])
            if seen.get(s, 0) >= v:
                return
            if waits.get(s, 0) < v:
                waits[s] = v

        for b in reads:
            need(b.w)
            if b.kind == "ps":
                for s, v in b.r.items():
                    if s != self.esem[eng]:
                        need((s, v))
        for b in writes:
            need(b.w)
            for s, v in b.r.items():
                need((s, v))
        for s, v in waits.items():
            seen[s] = v
        if dma is not None:
            kind = 1 if eng == "pool" else 0
            if dma.dsem is None:
                dma.dsem = [None, None]
                dma.dcnt = [0, 0]
                self.dbufs.append(dma)
            if dma.dsem[kind] is None:
                if self.free_dsems[kind]:
                    dma.dsem[kind], dma.dcnt[kind] = self.free_dsems[kind].pop()
                else:
                    dma.dsem[kind] = self.new_sem("d%d_%s" % (len(self.sems), dma.name))
            dma.dcnt[kind] += 16
            self.dsem_issued[dma.dsem[kind]] = dma.dcnt[kind]
            tok = (dma.dsem[kind], dma.dcnt[kind])
            inc = 16
        else:
            self.cnt[eng] += 1
            tok = (self.esem[eng], self.cnt[eng])
            inc = 1
        for b in reads:
            if b.r.get(tok[0], 0) < tok[1]:
                b.r[tok[0]] = tok[1]
        for b in writes:
            b.w = tok
            b.r = {}
        rec = _Rec()
        fn(rec)
        assert rec.call is not None
        if eng == "pe":
            st = rec.call[2].get("lhsT", rec.call[2].get("in_"))
            shp = list(st.shape)
            rnd = lambda v: 32 if v <= 32 else (64 if v <= 64 else 128)
            mode = (rnd(shp[0]), rnd(int(np.prod(shp[1:]))))
            if mode != self.pe_mode:
                if self.pe_mode is not None and self.pe_drain:
                    self.q[eng].append(([], ("drain", (), {}), None, 0))
                self.pe_mode = mode
        self.q[eng].append((list(waits.items()), rec.call, tok[0], inc))
        return tok

    def wait_all(self, eng, bufs):
        waits = {}
        for b in bufs:
            toks = [b.w] + list(b.r.items())
            for tok in toks:
                if tok is None:
                    continue
                s, v = tok
                if waits.get(s, 0) < v:
                    waits[s] = v
        self.q[eng].append((list(waits.items()), None, None, 0))

    def emit(self):
        nc = self.nc
        with nc.allow_non_contiguous_dma(reason="small param loads"), nc.Block() as block:
            def run(ename):
                def body(eng):
                    for waits, fn, s, inc in self.q[ename]:
                        for ws, wv in waits:
                            eng.wait_ge(self.sems[ws], wv)
                        if fn is not None:
                            name, a, k = fn
                            ins = getattr(eng, name)(*a, **k)
                            if s is not None:
                                ins.then_inc(self.sems[s], inc)
                return body

            block.tensor(run("pe"))
            block.scalar(run("act"))
            block.vector(run("dve"))
            block.gpsimd(run("pool"))
            block.sync(run("sp"))

    def dma(self, eng, out_b, out_ap, in_b, in_ap, sem_b=None):
        if sem_b is None:
            sem_b = out_b if out_b.kind == "sb" else in_b
            assert sem_b.kind == "sb"
        return self.op(eng, lambda e: e.dma_start(out=out_ap, in_=in_ap),
                       reads=[in_b], writes=[out_b], dma=sem_b)

    def next_psum(self):
        pool = self.pspool if self.pspool is not None else self.psums
        b = pool[self.psum_rr % len(pool)]
        self.psum_rr += 1
        return b

    def barrier(self):
        snap = {}
        for e in ENGS:
            if self.cnt[e] > 0:
                snap[self.esem[e]] = self.cnt[e]
        for b in self.dbufs:
            for kind in (0, 1):
                if b.dsem[kind] is not None:
                    snap[b.dsem[kind]] = b.dcnt[kind]
        for e in ENGS:
            waits = [(s, v) for s, v in snap.items() if self.seen[e].get(s, 0) < v]
            for s, v in waits:
                self.seen[e][s] = v
            self.q[e].append((waits, None, None, 0))


class Arena:
    def __init__(self, P, words):
        self.P = P
        self.t = P.stack.enter_context(P.nc.sbuf_tensor("arena", [128, words], F32))
        self.off = 0
        self.words = words
        self.peak = 0
        self.live = []

    def mark(self):
        return self.off

    def reset(self, m):
        keep = []
        for off, b in self.live:
            if off >= m:
                if b.dsem is not None:
                    for kind in (0, 1):
                        if b.dsem[kind] is not None:
                            self.P.free_dsems[kind].append((b.dsem[kind], b.dcnt[kind]))
                    self.P.dbufs.remove(b)
                    b.dsem = None
            else:
                keep.append((off, b))
        self.live = keep
        self.off = m

    def alloc(self, name, free, dtype=F32, parts=128):
        free = list(free)
        n = int(np.prod(free))
        four = dtype in (F32, I32)
        w = n if four else (n + 1) // 2
        wal = (w + 7) // 8 * 8
        assert self.off + wal <= self.words, "arena overflow %s: %d + %d > %d" % (name, self.off, wal, self.words)
        a = self.t[0:parts, self.off:self.off + w]
        if dtype != F32:
            a = a.bitcast(dtype)
        if len(free) > 1:
            names = ["a%d" % i for i in range(len(free))]
            a = a.rearrange("p (%s) -> p %s" % (" ".join(names), " ".join(names)),
                            **{nm: v for nm, v in zip(names, free)})
        b = Buf(name, a, "sb")
        self.live.append((self.off, b))
        self.off += wal
        self.peak = max(self.peak, self.off)
        return b


QR = 384
KVR = 256
ATTN_SCALE = 192 ** -0.5
ARENA_WORDS = 48000


class Cfg:
    def __init__(self, T=4096, layers=4, stop=None, skip_ffn=False, seq=None, upto=None):
        self.seq = seq
        self.upto = upto
        self.a_every = 2
        self.T = T
        self.layers = layers
        self.stop = stop
        self.skip_ffn = skip_ffn


def build(cfg):
    T = cfg.T
    NT = T // TT
    NB = T // 128
    nc = bass.Bass("TRN2", target_bir_lowering=False)
    stack = contextlib.ExitStack()
    with stack:
        P = Prog(nc, stack)
        A = Arena(P, ARENA_WORDS)

        def dram_in(name, shape, dt=F32):
            return P.dram(name, shape, dt, kind="ExternalInput")

        x_in = dram_in("x", [T, D])
        pos_in = dram_in("positions", [1, T], I32)
        gains_in = dram_in("norm_gains", [4, 3, D])
        wg_in = dram_in("ffn_w_gate", [4, 2, D, DFF])
        wu_in = dram_in("ffn_w_up", [4, 2, D, DFF])
        wd_in = dram_in("ffn_w_down", [4, 2, DFF, D])
        owin_in = dram_in("odd_w_in_p", [2, D, 768])
        wq_in = dram_in("mla_w_q_up_p", [2, QR, 2048])
        wkv_in = dram_in("mla_w_kv_up", [2, KVR, 2048])
        owout_in = dram_in("odd_w_out", [2, D, D])
        qa_in = dram_in("mla_q_a_norm", [2, QR])
        kva_in = dram_in("mla_kv_a_norm", [2, KVR])
        qkc_in = dram_in("qk_cols", [2, 128, 6])
        qn_in = dram_in("mla_q_norm", [2, 192])
        kn_in = dram_in("mla_k_norm", [2, 192])
        ropec_in = dram_in("rope_c", [64, 2])
        ewin_in = dram_in("even_w_in", [2, D, 3232])
        ewout_in = dram_in("even_w_out", [2, D, D])
        cw_in = dram_in("even_conv_w", [2, 3, 512])
        emu_in = dram_in("even_mu_shift", [2, 1696])
        w0_in = dram_in("rwkv_w0", [2, 512])
        a0_in = dram_in("rwkv_a0", [2, 512])
        kk_in = dram_in("rwkv_k_k", [2, 512])
        ka_in = dram_in("rwkv_k_a", [2, 512])
        rk_in = dram_in("rwkv_r_k", [2, 512])
        lnw_in = dram_in("rwkv_ln_w", [2, 512])
        lnb_in = dram_in("rwkv_ln_b", [2, 512])
        wup_in = dram_in("rwkv_w_up", [2, 32, 512])
        aup_in = dram_in("rwkv_a_up", [2, 32, 512])
        gup_in = dram_in("rwkv_g_up", [2, 96, 512])
        y_out = P.dram("y", [T, D], F32, kind="ExternalOutput")

        XT = P.dram("XT", [D, T], F32)
        XTv = XT.t.rearrange("(c p) t -> p c t", p=128)
        XTb = [P.view(XT, "XT%d" % i) for i in range(NT)]
        wg_bfs = [P.dram("wg_bf%d" % i, [NM, 128, NC8, 128], BF16) for i in range(2)]
        wu_bfs = [P.dram("wu_bf%d" % i, [NM, 128, NC8, 128], BF16) for i in range(2)]
        wd_bfs = [P.dram("wd_bf%d" % i, [NC8, 128, NM, 128], BF16) for i in range(2)]
        castsems = [Buf("castsem%d" % i) for i in range(2)]
        we_bfs = [P.dram("we_bf%d" % i, [24, 128, NC8, 128], BF16) for i in range(2)]
        wl_bfs = [P.dram("wl_bf%d" % i, [128, NC8, 160], BF16) for i in range(2)]
        woc_bfs = [P.dram("woc_bf%d" % i, [NC8, 128, 4, 128], BF16) for i in range(2)]
        wor_bfs = [P.dram("wor_bf%d" % i, [NC8, 64, 8, 128], BF16) for i in range(2)]
        ecastsem = [Buf("ecastsem%d" % i) for i in range(2)]
        OTd = P.dram("OTd", [8, 128, T], BF16)
        OTb = [P.view(OTd, "OT%d" % i) for i in range(NT)]

        for i in range(8):
            P.psums.append(P.ps("ps%d" % i, [128, 512], F32))

        ident = A.alloc("ident", [128], F32)
        ones_bf = A.alloc("ones_bf", [128], BF16)
        ident_bf = A.alloc("ident_bf", [128], BF16)
        ones_f = A.alloc("ones_f", [128], F32)
        tri_bf = A.alloc("tri_bf", [128], BF16)
        gains_sb = A.alloc("gains_sb", [12, NC8], F32)
        epsb = A.alloc("epsb", [1], F32)
        ropec = A.alloc("ropec", [2], F32, parts=64)
        ropeD = P.dram("ropeD", [2, 64, T], F32)

        P.op("pool", lambda e: e.memset(ones_bf[:], 1.0), writes=[ones_bf])
        P.op("pool", lambda e: e.memset(ones_f[:], 1.0), writes=[ones_f])
        P.op("pool", lambda e: e.memset(epsb[:], EPS), writes=[epsb])
        P.op("pool", lambda e: e.memset(ident[:], 0.0), writes=[ident])
        P.op("pool", lambda e: e.affine_select(out=ident[:], in_=ident[:], pattern=[[-1, 128]],
                                               compare_op=ALU.not_equal, fill=1.0, base=0,
                                               channel_multiplier=1),
             reads=[ident], writes=[ident])
        P.op("pool", lambda e: e.tensor_copy(out=ident_bf[:], in_=ident[:]), reads=[ident], writes=[ident_bf])
        P.op("pool", lambda e: e.memset(tri_bf[:], 1.0), writes=[tri_bf])
        P.op("pool", lambda e: e.affine_select(out=tri_bf[:], in_=tri_bf[:], pattern=[[1, 128]],
                                               compare_op=ALU.is_ge, fill=0.0, base=0,
                                               channel_multiplier=-1),
             reads=[tri_bf], writes=[tri_bf])
        P.dma("sp", gains_sb, gains_sb[:], gains_in, gains_in.t.rearrange("l s (c p) -> p (l s) c", p=128))
        P.dma("sp", ropec, ropec[:], ropec_in, ropec_in.t)

        def rope_tables():
            m = A.mark()
            cos2 = A.alloc("cos2t", [T], F32, parts=64)
            ssin2 = A.alloc("ssin2t", [T], F32, parts=64)
            posi = A.alloc("posi", [T], I32, parts=64)
            ang = A.alloc("ang", [T], F32, parts=64)
            kf = A.alloc("kf", [T], F32, parts=64)
            ki = A.alloc("ki", [T], I32, parts=64)
            rr = A.alloc("rr", [T], F32, parts=64)
            msk = A.alloc("msk", [T], F32, parts=64)
            P.dma("sp", posi, posi[:], pos_in, pos_in.t.partition_broadcast(64))
            P.op("dve", lambda e: e.tensor_copy(out=ang[:], in_=posi[:]), reads=[posi], writes=[ang])
            P.op("dve", lambda e: e.tensor_scalar(out=ang[:], in0=ang[:], scalar1=ropec[:, 0:1], scalar2=None,
                                                  op0=ALU.mult), reads=[ang, ropec], writes=[ang])
            TWO_PI = 2.0 * np.pi
            c1 = float(np.float32(6.28125))
            c2 = float(np.float32(TWO_PI - 6.28125))
            c3 = float(TWO_PI - c1 - c2)
            P.op("dve", lambda e: e.tensor_scalar(out=kf[:], in0=ang[:], scalar1=float(1.0 / TWO_PI), scalar2=None,
                                                  op0=ALU.mult), reads=[ang], writes=[kf])
            P.op("dve", lambda e: e.tensor_copy(out=ki[:], in_=kf[:]), reads=[kf], writes=[ki])
            P.op("dve", lambda e: e.tensor_copy(out=kf[:], in_=ki[:]), reads=[ki], writes=[kf])
            for cc in (c1, c2, c3):
                P.op("dve", lambda e, cc=cc: e.scalar_tensor_tensor(out=ang[:], in0=kf[:], scalar=-cc, in1=ang[:],
                                                                   op0=ALU.mult, op1=ALU.add),
                     reads=[kf, ang], writes=[ang])

            def wrap(dst, src, shift):
                P.op("dve", lambda e: e.tensor_scalar(out=dst[:], in0=src[:], scalar1=float(shift), scalar2=None,
                                                      op0=ALU.add), reads=[src], writes=[dst])
                P.op("dve", lambda e: e.tensor_single_scalar(out=msk[:], in_=dst[:], scalar=float(np.pi), op=ALU.is_gt),
                     reads=[dst], writes=[msk])
                P.op("dve", lambda e: e.scalar_tensor_tensor(out=dst[:], in0=msk[:], scalar=-TWO_PI, in1=dst[:],
                                                             op0=ALU.mult, op1=ALU.add), reads=[msk, dst], writes=[dst])
                P.op("dve", lambda e: e.tensor_single_scalar(out=msk[:], in_=dst[:], scalar=float(-np.pi), op=ALU.is_lt),
                     reads=[dst], writes=[msk])
                P.op("dve", lambda e: e.scalar_tensor_tensor(out=dst[:], in0=msk[:], scalar=TWO_PI, in1=dst[:],
                                                             op0=ALU.mult, op1=ALU.add), reads=[msk, dst], writes=[dst])
                P.op("dve", lambda e: e.tensor_scalar(out=dst[:], in0=dst[:], scalar1=float(np.pi), scalar2=float(-np.pi),
                                                      op0=ALU.min, op1=ALU.max), reads=[dst], writes=[dst])

            wrap(rr, ang, 0.0)
            P.op("act", lambda e: e.activation(out=ssin2[:], in_=rr[:], func=AF.Sin), reads=[rr], writes=[ssin2])
            P.op("dve", lambda e: e.tensor_scalar(out=ssin2[:], in0=ssin2[:], scalar1=ropec[:, 1:2], scalar2=None,
                                                  op0=ALU.mult), reads=[ssin2, ropec], writes=[ssin2])
            wrap(kf, rr, np.pi / 2)
            P.op("act", lambda e: e.activation(out=cos2[:], in_=kf[:], func=AF.Sin), reads=[kf], writes=[cos2])
            P.dma("sp", ropeD, ropeD.t[0], cos2, cos2[:])
            P.dma("sp", ropeD, ropeD.t[1], ssin2, ssin2[:])
            P.barrier()
            A.reset(m)


        def transpose_in():
            m = A.mark()
            xin_tiles = [A.alloc("xin%d" % i, [D], F32) for i in range(2)]
            xtr_tiles = [A.alloc("xtr%d" % i, [NC8, 128], F32) for i in range(2)]
            for b in range(NB):
                xi = xin_tiles[b % 2]
                xo = xtr_tiles[b % 2]
                P.dma("sp", xi, xi[:], x_in, x_in.t[b * 128:(b + 1) * 128, :])
                for half in range(2):
                    ps = P.next_psum()
                    for j in range(4):
                        c = half * 4 + j
                        P.op("pe", lambda e, ps=ps, xi=xi, c=c, j=j: e.transpose(
                            out=ps[:, j * 128:(j + 1) * 128], in_=xi[:, c * 128:(c + 1) * 128], identity=ident[:]),
                            reads=[xi, ident], writes=[ps])
                    if half:
                        P.op("act", lambda e, ps=ps, xo=xo, half=half: e.copy(
                            out=xo[:, half * 4:(half + 1) * 4, :], in_=ps[:].rearrange("p (j t) -> p j t", j=4)),
                            reads=[ps], writes=[xo])
                    else:
                        P.op("dve", lambda e, ps=ps, xo=xo, half=half: e.tensor_copy(
                            out=xo[:, half * 4:(half + 1) * 4, :], in_=ps[:].rearrange("p (j t) -> p j t", j=4)),
                            reads=[ps], writes=[xo])
                P.dma("pool", XTb[b // 4], XTv[:, :, b * 128:(b + 1) * 128], xo, xo[:])
            P.barrier()
            A.reset(m)


        def rmsnorm_tile(xt, h, gidx, sq_tile, rstd_t, w=TT):
            P.op("act", lambda e: e.activation(out=sq_tile[:], in_=xt[:], func=AF.Square),
                 reads=[xt], writes=[sq_tile])
            ps = P.next_psum()
            for c in range(NC8):
                P.op("pe", lambda e, c=c: e.matmul(ps[:, :w], lhsT=ones_bf[:], rhs=sq_tile[:, c, :],
                                                  start=(c == 0), stop=(c == NC8 - 1)),
                     reads=[ones_bf, sq_tile], writes=[ps])
            P.op("act", lambda e: e.activation(out=rstd_t[:], in_=ps[:, :w], func=AF.Sqrt,
                                               bias=epsb[:], scale=1.0 / D),
                 reads=[ps, epsb], writes=[rstd_t])
            P.op("dve", lambda e: e.reciprocal(out=rstd_t[:], in_=rstd_t[:]), reads=[rstd_t], writes=[rstd_t])
            for c in range(NC8):
                P.op("dve", lambda e, c=c: e.scalar_tensor_tensor(
                    out=h[:, c, :], in0=xt[:, c, :], scalar=gains_sb[:, gidx, c:c + 1], in1=rstd_t[:],
                    op0=ALU.mult, op1=ALU.mult), reads=[xt, gains_sb, rstd_t], writes=[h])

        def rstd_from_ss(dst, ps, n, parts=128):
            P.op("act", lambda e: e.activation(out=dst[:parts], in_=ps[:parts, :TT], func=AF.Ln,
                                               bias=epsb[:parts], scale=1.0 / n),
                 reads=[ps, epsb], writes=[dst])
            P.op("act", lambda e: e.activation(out=dst[:parts], in_=dst[:parts], func=AF.Exp, scale=-0.5),
                 reads=[dst], writes=[dst])

        def cast_weights(layer, which):
            par = (layer * 2 + which) % 2
            wg_bf, wu_bf, wd_bf, castsem = wg_bfs[par], wu_bfs[par], wd_bfs[par], castsems[par]
            for m in range(NM):
                P.op("pool", lambda e, m=m: e.dma_start(
                    out=wg_bf.t[m], in_=wg_in.t[layer, which].rearrange("(c p) f -> p c f", p=128)[:, :, m * 128:(m + 1) * 128]),
                    reads=[wg_in], writes=[wg_bf], dma=castsem)
                P.op("pool", lambda e, m=m: e.dma_start(
                    out=wu_bf.t[m], in_=wu_in.t[layer, which].rearrange("(c p) f -> p c f", p=128)[:, :, m * 128:(m + 1) * 128]),
                    reads=[wu_in], writes=[wu_bf], dma=castsem)
            for o in range(NC8):
                P.op("pool", lambda e, o=o: e.dma_start(
                    out=wd_bf.t[o], in_=wd_in.t[layer, which].rearrange("(m p) f -> p m f", p=128)[:, :, o * 128:(o + 1) * 128]),
                    reads=[wd_in], writes=[wd_bf], dma=castsem)

        TF = 1024

        def ffn_phase(layer, which):
            m0 = A.mark()
            P.pspool = None
            NTF = T // TF
            NH2 = TF // 512
            xt_tiles = [A.alloc("fxt%d" % i, [NC8, TF], F32) for i in range(2)]
            sqr = [A.alloc("fsq%d" % i, [TF], BF16) for i in range(2)]
            h_tiles = [A.alloc("fh0", [NC8, TF], BF16)] * 2
            hid = A.alloc("hid", [NM, TF], BF16)
            rstd_t = A.alloc("frstd", [TF], F32)
            sg_tiles = [A.alloc("sg%d" % i, [TF], F32) for i in range(2)]
            NWB = 4
            wgu_tiles = [A.alloc("wgu%d" % i, [2, NC8, 128], BF16) for i in range(NWB)]
            wdn_tiles = [A.alloc("wdn%d" % i, [NM, 128], BF16) for i in range(2)]
            par = (layer * 2 + which) % 2
            wg_bf, wu_bf, wd_bf = wg_bfs[par], wu_bfs[par], wd_bfs[par]
            gidx = layer * 3 + (0 if which == 0 else 2)
            wcount = 0
            dcount = 0
            def f_load(tf):
                xt = xt_tiles[tf % 2]
                for hf in range(NH2):
                    tt = tf * NH2 + hf
                    P.dma("sp", xt, xt[:, :, hf * 512:(hf + 1) * 512], XTb[tt], XTv[:, :, tt * 512:(tt + 1) * 512])

            def f_norm(tf):
                xt = xt_tiles[tf % 2]
                h = h_tiles[tf % 2]
                pss = [P.next_psum() for _ in range(NH2)]
                for c in range(NC8):
                    sq = sqr[c % 2]
                    P.op("act", lambda e, c=c, sq=sq: e.activation(out=sq[:], in_=xt[:, c, :], func=AF.Square),
                         reads=[xt], writes=[sq])
                    for hf in range(NH2):
                        P.op("pe", lambda e, c=c, hf=hf, sq=sq: e.matmul(pss[hf][:, :512], lhsT=ones_bf[:], rhs=sq[:, hf * 512:(hf + 1) * 512],
                                                                        start=(c == 0), stop=(c == NC8 - 1)),
                             reads=[ones_bf, sq], writes=[pss[hf]])
                for hf in range(NH2):
                    P.op("act", lambda e, hf=hf: e.activation(out=rstd_t[:, hf * 512:(hf + 1) * 512], in_=pss[hf][:, :512], func=AF.Sqrt,
                                                              bias=epsb[:], scale=1.0 / D), reads=[pss[hf], epsb], writes=[rstd_t])
                P.op("dve", lambda e: e.reciprocal(out=rstd_t[:], in_=rstd_t[:]), reads=[rstd_t], writes=[rstd_t])
                for c in range(NC8):
                    P.op("dve", lambda e, c=c: e.scalar_tensor_tensor(
                        out=h[:, c, :], in0=xt[:, c, :], scalar=gains_sb[:, gidx, c:c + 1], in1=rstd_t[:],
                        op0=ALU.mult, op1=ALU.mult), reads=[xt, gains_sb, rstd_t], writes=[h])

            f_load(0)
            f_norm(0)
            for tf in range(NTF):
                xt = xt_tiles[tf % 2]
                h = h_tiles[tf % 2]
                if tf + 1 < NTF:
                    f_load(tf + 1)
                for m in range(NM):
                    w = wgu_tiles[wcount % NWB]
                    wcount += 1
                    P.dma("sp", w, w[:, 0], wg_bf, wg_bf.t[m])
                    P.dma("sp", w, w[:, 1], wu_bf, wu_bf.t[m])
                    psg = [P.next_psum() for _ in range(NH2)]
                    psu = [P.next_psum() for _ in range(NH2)]
                    for kind, pst in ((0, psg), (1, psu)):
                        for c in range(NC8):
                            for hf in range(NH2):
                                P.op("pe", lambda e, c=c, w=w, hf=hf, kind=kind, pst=pst, h=h: e.matmul(
                                    pst[hf][:, :512], lhsT=w[:, kind, c, :], rhs=h[:, c, hf * 512:(hf + 1) * 512],
                                    start=(c == 0), stop=(c == NC8 - 1)), reads=[w, h], writes=[pst[hf]])
                    sg = sg_tiles[m % 2]
                    for hf in range(NH2):
                        P.op("act", lambda e, sg=sg, hf=hf, psg=psg: e.activation(out=sg[:, hf * 512:(hf + 1) * 512], in_=psg[hf][:, :512],
                                                                                 func=AF.Silu), reads=[psg[hf]], writes=[sg])
                    for hf in range(NH2):
                        P.op("dve", lambda e, sg=sg, hf=hf, psu=psu, m=m: e.tensor_tensor(
                            out=hid[:, m, hf * 512:(hf + 1) * 512], in0=psu[hf][:, :512], in1=sg[:, hf * 512:(hf + 1) * 512], op=ALU.mult),
                            reads=[psu[hf], sg], writes=[hid])
                if tf + 1 < NTF:
                    f_norm(tf + 1)
                for o in range(NC8):
                    wd = wdn_tiles[dcount % 2]
                    dcount += 1
                    P.dma("sp", wd, wd[:], wd_bf, wd_bf.t[o])
                    pso = [P.next_psum() for _ in range(NH2)]
                    for m in range(NM):
                        for hf in range(NH2):
                            P.op("pe", lambda e, m=m, wd=wd, hf=hf, pso=pso: e.matmul(
                                pso[hf][:, :512], lhsT=wd[:, m, :], rhs=hid[:, m, hf * 512:(hf + 1) * 512],
                                start=(m == 0), stop=(m == NM - 1)), reads=[wd, hid], writes=[pso[hf]])
                    for hf in range(NH2):
                        P.op("dve", lambda e, o=o, hf=hf, pso=pso, xt=xt: e.scalar_tensor_tensor(
                            out=xt[:, o, hf * 512:(hf + 1) * 512], in0=pso[hf][:, :512], scalar=0.5, in1=xt[:, o, hf * 512:(hf + 1) * 512],
                            op0=ALU.mult, op1=ALU.add), reads=[pso[hf], xt], writes=[xt])
                for hf in range(NH2):
                    tt = tf * NH2 + hf
                    P.dma("pool", XTb[tt], XTv[:, :, tt * 512:(tt + 1) * 512], xt, xt[:, :, hf * 512:(hf + 1) * 512])
            P.barrier()
            A.reset(m0)

        def mla_layer(layer):
            j = layer // 2
            gidx = layer * 3 + 1
            m0 = A.mark()
            P.pspool = P.psums[0:4]
            cos2 = A.alloc("cos2", [T], F32, parts=64)
            ssin2 = A.alloc("ssin2", [T], F32, parts=64)
            P.dma("sp", cos2, cos2[:], ropeD, ropeD.t[0])
            P.dma("sp", ssin2, ssin2[:], ropeD, ropeD.t[1])
            cqn = A.alloc("cqn", [3, T], BF16)
            ckvn = A.alloc("ckvn", [2, T], BF16)
            kper = A.alloc("kper", [T], BF16, parts=64)
            kpsq = A.alloc("kpsq", [T], BF16, parts=64)
            wq = A.alloc("wq", [3, 2048], BF16)
            wkv = A.alloc("wkv", [2, 2048], BF16)
            cols = A.alloc("mcols", [16], F32)
            grow = A.alloc("grow", [2, 192], F32, parts=1)
            gmax = A.alloc("gmax", [4], F32, parts=1)
            P.dma("pool", wq, wq[:], wq_in, wq_in.t[j].rearrange("(c p) f -> p c f", p=128))
            P.dma("pool", wkv, wkv[:], wkv_in, wkv_in.t[j].rearrange("(c p) f -> p c f", p=128))
            P.dma("sp", cols, cols[:, 0:3], qa_in, qa_in.t[j].rearrange("(c p) -> p c", p=128))
            P.dma("sp", cols, cols[:, 3:5], kva_in, kva_in.t[j].rearrange("(c p) -> p c", p=128))
            P.dma("sp", cols, cols[:, 5:11], qkc_in, qkc_in.t[j])
            P.dma("sp", grow, grow[:, 0, :], qn_in, qn_in.t[j:j + 1, :])
            P.dma("sp", grow, grow[:, 1, :], kn_in, kn_in.t[j:j + 1, :])
            P.op("dve", lambda e: e.tensor_reduce(out=gmax[:, 0:2], in_=grow[:], axis=mybir.AxisListType.X,
                                                  op=ALU.max, apply_absolute_value=True),
                 reads=[grow], writes=[gmax])
            P.op("dve", lambda e: e.scalar_tensor_tensor(out=gmax[:, 2:3], in0=gmax[:, 0:1], scalar=-ATTN_SCALE * 192.0,
                                                         in1=gmax[:, 1:2], op0=ALU.mult, op1=ALU.mult),
                 reads=[gmax], writes=[gmax])
            P.op("dve", lambda e: e.tensor_copy(out=gmax[:, 3:4], in_=gmax[:, 2:3]), reads=[gmax], writes=[gmax])
            psb = P.next_psum()
            P.op("pe", lambda e: e.matmul(psb[:, 0:2], lhsT=ones_f[0:1, :], rhs=gmax[:, 2:4], start=True, stop=True),
                 reads=[ones_f, gmax], writes=[psb])
            P.op("dve", lambda e: e.tensor_copy(out=cols[:, 12:13], in_=psb[:, 0:1]), reads=[psb], writes=[cols])

            if cfg.upto == "m0":
                P.barrier()
                A.reset(m0)
                return
            m1 = A.mark()
            win = A.alloc("win", [NC8, 768], BF16)
            P.dma("pool", win, win[:], owin_in, owin_in.t[j].rearrange("(c p) f -> p c f", p=128))
            xt_tiles = [A.alloc("xt%d" % i, [NC8, TT], F32) for i in range(1)] * 2
            sq_tile = A.alloc("sq", [NC8, TT], BF16)
            h_tiles = [A.alloc("h%d" % i, [NC8, TT], BF16) for i in range(1)] * 2
            rstd_t = A.alloc("rstd", [TT], F32)
            c32 = A.alloc("c32", [5, TT], F32)
            csq = A.alloc("csq", [5, TT], BF16)
            rl = [A.alloc("rl%d" % i, [TT], F32) for i in range(2)]
            tA = A.alloc("tA", [TT], F32, parts=64)
            tB = A.alloc("tB", [TT], F32, parts=64)
            for tt in range(NT):
                ts = slice(tt * TT, (tt + 1) * TT)
                xt = xt_tiles[tt % 2]
                h = h_tiles[tt % 2]
                P.dma("sp", xt, xt[:], XTb[tt], XTv[:, :, ts])
                rmsnorm_tile(xt, h, gidx, sq_tile, rstd_t)
                for m in range(5):
                    if cfg.upto == "m1a0":
                        continue
                    ps = P.next_psum()
                    for c in range(NC8):
                        P.op("pe", lambda e, c=c, m=m, ps=ps, h=h: e.matmul(
                            ps[:, :TT], lhsT=win[:, c, m * 128:(m + 1) * 128], rhs=h[:, c, :],
                            start=(c == 0), stop=(c == NC8 - 1)), reads=[win, h], writes=[ps])
                    P.op("act", lambda e, m=m, ps=ps: e.activation(out=csq[:, m, :], in_=ps[:, :TT], func=AF.Square),
                         reads=[ps], writes=[csq])
                    if cfg.upto != "m1a1":
                        P.op("dve", lambda e, m=m, ps=ps: e.tensor_copy(out=c32[:, m, :], in_=ps[:, :TT]),
                             reads=[ps], writes=[c32])
                if cfg.upto in ("m1a", "m1a0", "m1a1"):
                    continue
                for (lo, hi, n, dst, cbase, ri) in ((0, 3, QR, cqn, 0, 0), (3, 5, KVR, ckvn, 3, 1)):
                    ps = P.next_psum()
                    for m in range(lo, hi):
                        P.op("pe", lambda e, m=m, ps=ps, lo=lo, hi=hi: e.matmul(
                            ps[:, :TT], lhsT=ones_bf[:], rhs=csq[:, m, :], start=(m == lo), stop=(m == hi - 1)),
                            reads=[ones_bf, csq], writes=[ps])
                    rstd_from_ss(rl[ri], ps, n)
                    for m in range(lo, hi):
                        P.op("dve", lambda e, m=m, lo=lo, dst=dst, cbase=cbase, ri=ri: e.scalar_tensor_tensor(
                            out=dst[:, m - lo, ts], in0=c32[:, m, :], scalar=cols[:, cbase + m - lo:cbase + m - lo + 1],
                            in1=rl[ri][:], op0=ALU.mult, op1=ALU.mult), reads=[c32, cols, rl[ri]], writes=[dst])
                if cfg.upto == "m1b":
                    continue
                pk = P.next_psum()
                pw = P.next_psum()
                for c in range(NC8):
                    P.op("pe", lambda e, c=c, pk=pk, h=h: e.matmul(
                        pk[:64, :TT], lhsT=win[:, c, 640:704], rhs=h[:, c, :], start=(c == 0), stop=(c == NC8 - 1)),
                        reads=[win, h], writes=[pk])
                for c in range(NC8):
                    P.op("pe", lambda e, c=c, pw=pw, h=h: e.matmul(
                        pw[:64, :TT], lhsT=win[:, c, 704:768], rhs=h[:, c, :], start=(c == 0), stop=(c == NC8 - 1)),
                        reads=[win, h], writes=[pw])
                P.op("act", lambda e, pk=pk: e.activation(out=kpsq[:, ts], in_=pk[:64, :TT], func=AF.Square),
                     reads=[pk], writes=[kpsq])
                P.op("dve", lambda e, pk=pk: e.scalar_tensor_tensor(
                    out=tA[:], in0=pk[:64, :TT], scalar=cols[:64, 9:10], in1=cos2[:, ts], op0=ALU.mult, op1=ALU.mult),
                    reads=[pk, cols, cos2], writes=[tA])
                P.op("dve", lambda e, pw=pw: e.scalar_tensor_tensor(
                    out=tB[:], in0=pw[:64, :TT], scalar=cols[:64, 10:11], in1=ssin2[:, ts], op0=ALU.mult, op1=ALU.mult),
                    reads=[pw, cols, ssin2], writes=[tB])
                P.op("dve", lambda e: e.tensor_tensor(out=kper[:, ts], in0=tA[:], in1=tB[:], op=ALU.add),
                     reads=[tA, tB], writes=[kper])
            P.barrier()
            A.reset(m1)
            if cfg.upto in ("m1", "m1a", "m1b", "m1a0", "m1a1"):
                A.reset(m0)
                return

            QTn = A.alloc("QTn", [T], BF16)
            QTr = A.alloc("QTr", [T], BF16, parts=64)
            KTn = A.alloc("KTn", [T], BF16)
            KTr = A.alloc("KTr", [T], BF16, parts=64)
            Vh = A.alloc("Vh", [NB, 128], BF16)
            sqn = A.alloc("sqn", [TT], BF16)
            sqr = A.alloc("sqr", [TT], BF16, parts=64)
            rq = A.alloc("rq", [TT], F32)
            rk = A.alloc("rk", [TT], F32)
            t1 = A.alloc("t1", [TT], F32, parts=64)
            t2 = A.alloc("t2", [TT], F32, parts=64)
            pts = [A.alloc("pt%d" % i, [TT], BF16) for i in range(4)]
            rden = A.alloc("rden", [TT], F32)
            obf = [A.alloc("obf%d" % i, [TT], BF16) for i in range(2)]
            oacc = [P.psums[4], P.psums[5]]
            dacc = [P.psums[6], P.psums[7]]
            ptc = [0]
            for hd in range(8):
                qb = hd * 256
                kb = hd * 256
                for tt in range(NT):
                    ts = slice(tt * TT, (tt + 1) * TT)
                    pqn = P.next_psum()
                    for c in range(3):
                        P.op("pe", lambda e, c=c, pqn=pqn, ts=ts, qb=qb: e.matmul(
                            pqn[:, :TT], lhsT=wq[:, c, qb:qb + 128], rhs=cqn[:, c, ts], start=(c == 0), stop=(c == 2)),
                            reads=[wq, cqn], writes=[pqn])
                    pqr = P.next_psum()
                    for c in range(3):
                        P.op("pe", lambda e, c=c, pqr=pqr, ts=ts, qb=qb: e.matmul(
                            pqr[:64, :TT], lhsT=wq[:, c, qb + 128:qb + 192], rhs=cqn[:, c, ts], start=(c == 0), stop=(c == 2)),
                            reads=[wq, cqn], writes=[pqr])
                    pqs = P.next_psum()
                    for c in range(3):
                        P.op("pe", lambda e, c=c, pqs=pqs, ts=ts, qb=qb: e.matmul(
                            pqs[:64, :TT], lhsT=wq[:, c, qb + 192:qb + 256], rhs=cqn[:, c, ts], start=(c == 0), stop=(c == 2)),
                            reads=[wq, cqn], writes=[pqs])
                    P.op("act", lambda e, pqn=pqn: e.activation(out=sqn[:], in_=pqn[:, :TT], func=AF.Square),
                         reads=[pqn], writes=[sqn])
                    P.op("act", lambda e, pqr=pqr: e.activation(out=sqr[:], in_=pqr[:64, :TT], func=AF.Square),
                         reads=[pqr], writes=[sqr])
                    pss = P.next_psum()
                    P.op("pe", lambda e, pss=pss: e.matmul(pss[:, :TT], lhsT=ones_bf[:], rhs=sqn[:], start=True, stop=False),
                         reads=[ones_bf, sqn], writes=[pss])
                    P.op("pe", lambda e, pss=pss: e.matmul(pss[:, :TT], lhsT=ones_bf[:64, :], rhs=sqr[:], start=False, stop=True),
                         reads=[ones_bf, sqr], writes=[pss])
                    rstd_from_ss(rq, pss, 192)
                    P.op("dve", lambda e, pqn=pqn, ts=ts: e.scalar_tensor_tensor(
                        out=QTn[:, ts], in0=pqn[:, :TT], scalar=cols[:, 5:6], in1=rq[:], op0=ALU.mult, op1=ALU.mult),
                        reads=[pqn, cols, rq], writes=[QTn])
                    P.op("dve", lambda e, pqr=pqr, ts=ts: e.scalar_tensor_tensor(
                        out=t1[:], in0=pqr[:64, :TT], scalar=cols[:64, 6:7], in1=cos2[:, ts], op0=ALU.mult, op1=ALU.mult),
                        reads=[pqr, cols, cos2], writes=[t1])
                    P.op("dve", lambda e, pqs=pqs, ts=ts: e.scalar_tensor_tensor(
                        out=t2[:], in0=pqs[:64, :TT], scalar=cols[:64, 7:8], in1=ssin2[:, ts], op0=ALU.mult, op1=ALU.mult),
                        reads=[pqs, cols, ssin2], writes=[t2])
                    P.op("dve", lambda e: e.tensor_tensor(out=t1[:], in0=t1[:], in1=t2[:], op=ALU.add),
                         reads=[t1, t2], writes=[t1])
                    P.op("dve", lambda e, ts=ts: e.tensor_tensor(out=QTr[:, ts], in0=t1[:], in1=rq[:64], op=ALU.mult),
                         reads=[t1, rq], writes=[QTr])
                    pkn = P.next_psum()
                    for c in range(2):
                        P.op("pe", lambda e, c=c, pkn=pkn, ts=ts, kb=kb: e.matmul(
                            pkn[:, :TT], lhsT=wkv[:, c, kb:kb + 128], rhs=ckvn[:, c, ts], start=(c == 0), stop=(c == 1)),
                            reads=[wkv, ckvn], writes=[pkn])
                    P.op("act", lambda e, pkn=pkn: e.activation(out=sqn[:], in_=pkn[:, :TT], func=AF.Square),
                         reads=[pkn], writes=[sqn])
                    pss2 = P.next_psum()
                    P.op("pe", lambda e, pss2=pss2: e.matmul(pss2[:, :TT], lhsT=ones_bf[:], rhs=sqn[:], start=True, stop=False),
                         reads=[ones_bf, sqn], writes=[pss2])
                    P.op("pe", lambda e, pss2=pss2, ts=ts: e.matmul(pss2[:, :TT], lhsT=ones_bf[:64, :], rhs=kpsq[:, ts],
                                                                   start=False, stop=True),
                         reads=[ones_bf, kpsq], writes=[pss2])
                    rstd_from_ss(rk, pss2, 192)
                    P.op("dve", lambda e, pkn=pkn, ts=ts: e.scalar_tensor_tensor(
                        out=KTn[:, ts], in0=pkn[:, :TT], scalar=cols[:, 8:9], in1=rk[:], op0=ALU.mult, op1=ALU.mult),
                        reads=[pkn, cols, rk], writes=[KTn])
                    P.op("dve", lambda e, ts=ts: e.tensor_tensor(out=KTr[:, ts], in0=kper[:, ts], in1=rk[:64], op=ALU.mult),
                         reads=[kper, rk], writes=[KTr])
                    pv = P.next_psum()
                    for blk in range(4):
                        tb = slice(tt * TT + blk * 128, tt * TT + (blk + 1) * 128)
                        for c in range(2):
                            P.op("pe", lambda e, c=c, pv=pv, tb=tb, blk=blk, kb=kb: e.matmul(
                                pv[:, blk * 128:(blk + 1) * 128], lhsT=ckvn[:, c, tb], rhs=wkv[:, c, kb + 128:kb + 256],
                                start=(c == 0), stop=(c == 1)), reads=[ckvn, wkv], writes=[pv])
                    P.op("act", lambda e, pv=pv, tt=tt: e.copy(
                        out=Vh[:, tt * 4:(tt + 1) * 4, :], in_=pv[:].rearrange("p (b d) -> p b d", b=4)),
                        reads=[pv], writes=[Vh])

                if cfg.upto == "m2p":
                    continue
                units = [(i, jb) for i in range(NT) for jb in range(4 * i + 4)]

                def emit_S(u):
                    i, jb = u
                    q0 = max(i * TT, jb * 128)
                    n = (i + 1) * TT - q0
                    ps = P.next_psum()
                    ks = slice(jb * 128, (jb + 1) * 128)
                    qs = slice(q0, q0 + n)
                    P.op("pe", lambda e: e.matmul(ps[:, :n], lhsT=KTn[:, ks], rhs=QTn[:, qs], start=True, stop=False),
                         reads=[KTn, QTn], writes=[ps])
                    P.op("pe", lambda e: e.matmul(ps[:, :n], lhsT=KTr[:, ks], rhs=QTr[:, qs], start=False, stop=True),
                         reads=[KTr, QTr], writes=[ps])
                    pt = pts[ptc[0] % 4]
                    ptc[0] += 1
                    P.op("act", lambda e: e.activation(out=pt[:, :n], in_=ps[:, :n], func=AF.Exp,
                                                       bias=cols[:, 12:13], scale=ATTN_SCALE),
                         reads=[ps, cols], writes=[pt])
                    if jb >= 4 * i:
                        P.op("pool", lambda e: e.tensor_tensor(out=pt[:, 0:128], in0=pt[:, 0:128], in1=tri_bf[:], op=ALU.mult),
                             reads=[pt, tri_bf], writes=[pt])
                    return (i, jb, pt, q0 - i * TT, n)

                def emit_PV(s):
                    i, jb, pt, c0, n = s
                    po = oacc[i % 2]
                    pd = dacc[i % 2]
                    last = (jb == 4 * i + 3)
                    P.op("pe", lambda e: e.matmul(po[:, c0:c0 + n], lhsT=Vh[:, jb, :], rhs=pt[:, :n],
                                                  start=(jb == 0), stop=last), reads=[Vh, pt], writes=[po])
                    P.op("pe", lambda e: e.matmul(pd[:, c0:c0 + n], lhsT=ones_bf[:], rhs=pt[:, :n],
                                                  start=(jb == 0), stop=last), reads=[ones_bf, pt], writes=[pd])
                    if last:
                        P.op("act", lambda e: e.activation(out=rden[:], in_=pd[:, :TT], func=AF.Ln),
                             reads=[pd], writes=[rden])
                        P.op("act", lambda e: e.activation(out=rden[:], in_=rden[:], func=AF.Exp, scale=-1.0),
                             reads=[rden], writes=[rden])
                        ob = obf[i % 2]
                        P.op("dve", lambda e: e.tensor_tensor(out=ob[:], in0=po[:, :TT], in1=rden[:], op=ALU.mult),
                             reads=[po, rden], writes=[ob])
                        P.dma("pool", OTb[i], OTd.t[hd, :, i * TT:(i + 1) * TT], ob, ob[:])

                pend = []
                for u in units:
                    pend.append(emit_S(u))
                    if len(pend) > 2:
                        emit_PV(pend.pop(0))
                while pend:
                    emit_PV(pend.pop(0))
            P.barrier()
            A.reset(m1)

            if cfg.upto in ("m2p", "m2"):
                A.reset(m0)
                return
            P.pspool = None
            wo = A.alloc("wo", [8, D], BF16)
            P.dma("pool", wo, wo[:], owout_in, owout_in.t[j].rearrange("(h p) f -> p h f", p=128))
            xt_tiles = [A.alloc("xt%d" % i, [NC8, TT], F32) for i in range(2)]
            ot_tiles = [A.alloc("ot%d" % i, [8, TT], BF16) for i in range(2)]
            for tt in range(NT):
                ts = slice(tt * TT, (tt + 1) * TT)
                xt = xt_tiles[tt % 2]
                ot = ot_tiles[tt % 2]
                P.dma("sp", xt, xt[:], XTb[tt], XTv[:, :, ts])
                P.dma("sp", ot, ot[:], OTb[tt], OTd.t.rearrange("h p t -> p h t")[:, :, ts])
                for o in range(NC8):
                    ps = P.next_psum()
                    for hh in range(8):
                        P.op("pe", lambda e, hh=hh, o=o, ps=ps, ot=ot: e.matmul(
                            ps[:, :TT], lhsT=wo[:, hh, o * 128:(o + 1) * 128], rhs=ot[:, hh, :],
                            start=(hh == 0), stop=(hh == 7)), reads=[wo, ot], writes=[ps])
                    P.op("dve", lambda e, o=o, ps=ps, xt=xt: e.tensor_tensor(
                        out=xt[:, o, :], in0=ps[:, :TT], in1=xt[:, o, :], op=ALU.add), reads=[ps, xt], writes=[xt])
                P.dma("pool", XTb[tt], XTv[:, :, ts], xt, xt[:])
            P.barrier()
            A.reset(m0)

        TE = 128
        SDT = BF16
        NCHT = TE // 64
        HG = 4
        LWS = -float(np.exp(-0.5))

        def cast_even(i):
            for jj in range(24):
                P.op("pool", lambda e, jj=jj: e.dma_start(
                    out=we_bfs[i].t[jj], in_=ewin_in.t[i].rearrange("(c p) f -> p c f", p=128)[:, :, jj * 128:(jj + 1) * 128]),
                    reads=[ewin_in], writes=[we_bfs[i]], dma=ecastsem[i])
            P.op("pool", lambda e: e.dma_start(
                out=wl_bfs[i].t, in_=ewin_in.t[i].rearrange("(c p) f -> p c f", p=128)[:, :, 3072:3232]),
                reads=[ewin_in], writes=[wl_bfs[i]], dma=ecastsem[i])
            for o in range(NC8):
                P.op("pool", lambda e, o=o: e.dma_start(
                    out=woc_bfs[i].t[o], in_=ewout_in.t[i, 0:512, :].rearrange("(c p) f -> p c f", p=128)[:, :, o * 128:(o + 1) * 128]),
                    reads=[ewout_in], writes=[woc_bfs[i]], dma=ecastsem[i])
                P.op("pool", lambda e, o=o: e.dma_start(
                    out=wor_bfs[i].t[o], in_=ewout_in.t[i, 512:1024, :].rearrange("(h v) f -> v h f", v=64)[:, :, o * 128:(o + 1) * 128]),
                    reads=[ewout_in], writes=[wor_bfs[i]], dma=ecastsem[i])

        def even_layer(layer):
            i = layer // 2
            gidx = layer * 3 + 1
            NTE = T // TE
            m0 = A.mark()
            P.pspool = None
            we_bf, wl_bf, woc_bf, wor_bf = we_bfs[i], wl_bfs[i], woc_bfs[i], wor_bfs[i]
            m_xm = A.alloc("m_xm", [HG, 128], F32, parts=64)
            m_xt = A.alloc("m_xt", [HG, 64], F32, parts=64)
            identg = A.alloc("identg", [HG, 64], F32, parts=64)
            mtmp = A.alloc("mtmp", [3, 64], F32, parts=64)
            blockones = A.alloc("blockones", [128], BF16)
            rmask = A.alloc("rmask", [TE], F32)
            ecols = A.alloc("ecols", [48], F32)
            lcols = A.alloc("lcols", [4], F32)
            hcols = A.alloc("hcols", [16], F32, parts=64)
            ccon = A.alloc("ccon", [4], F32)
            w_up = A.alloc("w_up", [512], BF16, parts=32)
            a_up = A.alloc("a_up", [512], BF16, parts=32)
            g_up = A.alloc("g_up", [512], BF16, parts=96)
            wlora = A.alloc("wlora", [NC8, 160], BF16)
            Ss = [A.alloc("S%d" % k, [8, 64], F32, parts=64) for k in range(2)]
            for k in range(3):
                P.op("pool", lambda e, k=k: e.memset(mtmp[:, k, :], 1.0), writes=[mtmp])
            P.op("pool", lambda e: e.affine_select(out=mtmp[:, 0, :], in_=mtmp[:, 0, :], pattern=[[1, 64]],
                                                   compare_op=ALU.is_gt, fill=0.0, base=0, channel_multiplier=-1),
                 reads=[mtmp], writes=[mtmp])
            P.op("pool", lambda e: e.affine_select(out=mtmp[:, 1, :], in_=mtmp[:, 1, :], pattern=[[1, 64]],
                                                   compare_op=ALU.is_ge, fill=0.0, base=0, channel_multiplier=-1),
                 reads=[mtmp], writes=[mtmp])
            P.op("pool", lambda e: e.affine_select(out=mtmp[:, 2, :], in_=mtmp[:, 2, :], pattern=[[-1, 64]],
                                                   compare_op=ALU.is_gt, fill=0.0, base=0, channel_multiplier=1),
                 reads=[mtmp], writes=[mtmp])
            for hh in range(HG):
                P.op("pool", lambda e, hh=hh: e.tensor_copy(out=m_xm[:, hh, 0:64], in_=mtmp[:, 0, :]), reads=[mtmp], writes=[m_xm])
                P.op("pool", lambda e, hh=hh: e.tensor_copy(out=m_xm[:, hh, 64:128], in_=mtmp[:, 1, :]), reads=[mtmp], writes=[m_xm])
                P.op("pool", lambda e, hh=hh: e.tensor_copy(out=m_xt[:, hh, :], in_=mtmp[:, 2, :]), reads=[mtmp], writes=[m_xt])
                P.op("pool", lambda e, hh=hh: e.tensor_copy(out=identg[:, hh, :], in_=ident[0:64, 0:64]), reads=[ident], writes=[identg])
            P.op("pool", lambda e: e.memset(blockones[:], 1.0), writes=[blockones])
            P.op("pool", lambda e: e.memset(blockones[0:64, 64:128], 0.0), reads=[blockones], writes=[blockones])
            P.op("pool", lambda e: e.memset(blockones[64:128, 0:64], 0.0), reads=[blockones], writes=[blockones])
            P.op("pool", lambda e: e.memset(rmask[:], 1.0), writes=[rmask])
            P.op("pool", lambda e: e.memset(rmask[:].rearrange("p (c t) -> p c t", t=64)[:, :, 0:1], 0.0),
                 reads=[rmask], writes=[rmask])
            P.op("pool", lambda e: e.memset(ccon[:, 0:1], 1e-24), writes=[ccon])
            P.op("pool", lambda e: e.memset(ccon[:, 1:2], 64e-5), reads=[ccon], writes=[ccon])
            P.op("pool", lambda e: e.memset(Ss[0][:], 0.0), writes=[Ss[0]])
            P.dma("sp", ecols, ecols[:, 0:12], emu_in, emu_in.t[i, 0:1536].rearrange("(j p) -> p j", p=128))
            for k, src in enumerate((w0_in, a0_in, kk_in, ka_in, rk_in)):
                P.dma("sp", ecols, ecols[:, 12 + 4 * k:16 + 4 * k], src, src.t[i].rearrange("(j p) -> p j", p=128))
            P.dma("sp", ecols, ecols[:, 36:48].rearrange("p (j c) -> p j c", j=3), cw_in,
                  cw_in.t[i].rearrange("j (c p) -> p j c", p=128))
            P.op("dve", lambda e: e.tensor_scalar(out=ecols[:, 32:36], in0=ecols[:, 24:28], scalar1=-1.0, scalar2=1.0,
                                                  op0=ALU.mult, op1=ALU.add), reads=[ecols], writes=[ecols])
            P.dma("sp", lcols, lcols[0:32, 0:1], emu_in, emu_in.t[i, 1536:1568].rearrange("(p o) -> p o", o=1))
            P.dma("sp", lcols, lcols[0:32, 1:2], emu_in, emu_in.t[i, 1568:1600].rearrange("(p o) -> p o", o=1))
            P.dma("sp", lcols, lcols[0:96, 2:3], emu_in, emu_in.t[i, 1600:1696].rearrange("(p o) -> p o", o=1))
            P.dma("sp", hcols, hcols[:, 0:8], lnw_in, lnw_in.t[i].rearrange("(h v) -> v h", v=64))
            P.dma("sp", hcols, hcols[:, 8:16], lnb_in, lnb_in.t[i].rearrange("(h v) -> v h", v=64))
            P.dma("pool", w_up, w_up[:], wup_in, wup_in.t[i])
            P.dma("pool", a_up, a_up[:], aup_in, aup_in.t[i])
            P.dma("pool", g_up, g_up[:], gup_in, gup_in.t[i])
            P.dma("sp", wlora, wlora[:], wl_bf, wl_bf.t)

            xt_tiles = [A.alloc("ext%d" % k, [NC8, TE], F32) for k in range(2)]
            sq_tile = A.alloc("esq", [NC8, TE], BF16)
            h_tiles = [A.alloc("eh%d" % k, [NC8, TE], BF16) for k in range(2)]
            rstd_t = A.alloc("erstd", [TE], F32)
            NWB = 4
            we_tiles = [A.alloc("we%d" % k, [NC8, 128], BF16) for k in range(NWB)]
            gc = A.alloc("gc", [4, TE], F32)
            gb = A.alloc("gb", [4, TE], F32)
            ub = A.alloc("ub", [4, TE + 2], F32)
            cacc = [A.alloc("cacc%d" % k, [TE], F32) for k in range(2)]
            yconv = A.alloc("yconv", [4, TE], BF16)
            PR = [A.alloc("PR%d" % k, [TE + 1], F32) for k in range(12)]
            PL = [A.alloc("PL%d" % k, [TE + 1], F32) for k in range(3)]
            PRh = A.alloc("PRh", [16], F32)
            dtmp = [A.alloc("dtmp%d" % k, [TE], F32) for k in range(3)]
            tdw = A.alloc("tdw", [TE], BF16, parts=32)
            dab = A.alloc("dab", [TE], BF16, parts=32)
            sdg = A.alloc("sdg", [TE], BF16, parts=96)
            lw = A.alloc("lw", [TE], F32)
            aa = A.alloc("aa", [TE], F32)
            kkb = A.alloc("kk", [TE], F32)
            kk2 = A.alloc("kk2", [TE], BF16)
            rsb = A.alloc("rs", [TE], F32)
            kkn = A.alloc("kkn", [TE], F32)
            tmpk = A.alloc("tmpk", [TE], F32)
            bv = A.alloc("bv", [TE], F32)
            rkr = A.alloc("rkr", [TE], BF16)
            cc = A.alloc("cc", [TE], F32)
            cp = A.alloc("cp", [TE], F32)
            cd = A.alloc("cd", [TE], F32)
            ec = A.alloc("ec", [TE], F32)
            eci = A.alloc("eci", [TE], F32)
            ecp = A.alloc("ecp", [TE], F32)
            eCc = A.alloc("eCc", [TE], F32)
            AR = A.alloc("AR", [4, NCHT, 2, 64], SDT)
            Bt = A.alloc("Bt", [4, TE], SDT)
            Kt = A.alloc("Kt", [4, TE], SDT)
            BhT = A.alloc("BhT", [4, TE], SDT)
            KhT = A.alloc("KhT", [4, TE], SDT)
            bonus = A.alloc("bonus", [4, TE], F32)
            wCfm = A.alloc("wCfm", [4, NCHT], F32)
            wCT = A.alloc("wCT", [8, NCHT], F32, parts=64)
            Yb = A.alloc("Yb", [8, TE], F32, parts=64)
            RtL = A.alloc("RtL", [8, TE], F32, parts=64)
            ysq = A.alloc("ysq", [8, TE], F32, parts=64)
            mean = A.alloc("mean", [8, TE], F32, parts=64)
            var = A.alloc("var", [8, TE], F32, parts=64)
            ycb = A.alloc("ycb", [8, TE], F32, parts=64)
            yfin = A.alloc("yfin", [8, TE], BF16, parts=64)
            woc_t = [A.alloc("woc%d" % k, [4, 128], BF16) for k in range(2)]
            wor_t = [A.alloc("wor%d" % k, [8, 128], BF16, parts=64) for k in range(2)]
            G_ = []
            for g in range(2 * NCHT):
                d = {}
                for nm, shp in (("W1A", [HG, 2, 64]), ("Bh", [HG, 64]), ("Kh", [HG, 64]), ("Vt", [HG, 64]),
                                ("XM", [HG, 2, 64]), ("LM", [HG, 2, 64]), ("XT", [HG, 64]),
                                ("P0", [HG, 64]), ("P1", [HG, 64]), ("PT0", [HG, 64]), ("PT1", [HG, 64]),
                                ("Ac0", [HG, 64]), ("Ac1", [HG, 64]), ("UZ", [HG, 2, 64]), ("GT", [HG, 64]), ("QT", [HG, 64])):
                    d[nm] = A.alloc("%s_%d" % (nm, g), shp, F32 if nm in ("GT", "QT") else SDT, parts=64)
                G_.append(d)

            hset0 = (AR, Bt, Kt, BhT, KhT, bonus, wCT, RtL, sdg, yconv, PR)
            hset1 = (A.alloc("AR1", [4, NCHT, 2, 64], SDT), A.alloc("Bt1", [4, TE], SDT), A.alloc("Kt1", [4, TE], SDT),
                     A.alloc("BhT1", [4, TE], SDT), A.alloc("KhT1", [4, TE], SDT), A.alloc("bonus1", [4, TE], F32),
                     A.alloc("wCT1", [8, NCHT], F32, parts=64), A.alloc("RtL1", [8, TE], F32, parts=64),
                     A.alloc("sdg1", [TE], BF16, parts=96), A.alloc("yconv1", [4, TE], BF16),
                     [A.alloc("PRb%d" % k, [TE + 1], F32) for k in range(12)])
            P.op("pool", lambda e: e.memset(ub[:, :, 0:2], 0.0), writes=[ub])
            P.op("pool", lambda e: e.memset(PRh[:], 0.0), writes=[PRh])

            wcnt = [0]
            ocnt = [0]
            scur = [0]

            def proj(j_lo, ncols, h, consume):
                w = we_tiles[wcnt[0] % NWB]
                wcnt[0] += 1
                P.dma("sp", w, w[:], we_bf, we_bf.t[j_lo])
                ps = P.next_psum()
                for c in range(NC8):
                    P.op("pe", lambda e, c=c: e.matmul(ps[:ncols, :TE], lhsT=w[:, c, 0:ncols], rhs=h[:, c, :],
                                                      start=(c == 0), stop=(c == NC8 - 1)), reads=[w, h], writes=[ps])
                consume(ps)

            def chunk_pipeline(ch, g, te):
                B = G_[ch * 2 + g]
                cs = slice(ch * 64, (ch + 1) * 64)
                heads = list(range(g * HG, (g + 1) * HG))

                def hp(hd):
                    return hd // 2, (hd % 2) * 64

                for (nm, srcf, dst) in (("A", lambda pc: AR[:, pc, ch, 0, :], B["W1A"]),
                                        ("Bh", lambda pc: BhT[:, pc, cs], B["Bh"]),
                                        ("Kh", lambda pc: KhT[:, pc, cs], B["Kh"]),
                                        ("V", lambda pc: PR[8 + pc][:, 1 + ch * 64:1 + (ch + 1) * 64], B["Vt"])):
                    ps = P.next_psum()
                    for k in range(2):
                        pc = g * 2 + k
                        srcb = AR if nm == "A" else (BhT if nm == "Bh" else (KhT if nm == "Kh" else PR[8 + pc]))
                        idm = ident if (nm == "V" or SDT == F32) else ident_bf
                        P.op("pe", lambda e, k=k, pc=pc, idm=idm: e.matmul(ps[:64, k * 128:(k + 1) * 128], lhsT=srcf(pc),
                                                                           rhs=idm[:], start=True, stop=True),
                             reads=[srcb, idm], writes=[ps])
                    if nm == "A":
                        P.op("act", lambda e: e.copy(out=dst[:, :, 1, :], in_=ps[:64, 0:256].rearrange("p (h k) -> p h k", k=64)),
                             reads=[ps], writes=[dst])
                    else:
                        P.op("dve" if nm != "V" else "act",
                             (lambda e: e.tensor_copy(out=dst[:], in_=ps[:64, 0:256].rearrange("p (h k) -> p h k", k=64)))
                             if nm != "V" else
                             (lambda e: e.copy(out=dst[:], in_=ps[:64, 0:256].rearrange("p (h k) -> p h k", k=64))),
                             reads=[ps], writes=[dst])
                yield
                psx = [P.next_psum(), P.next_psum()]
                psl = [P.next_psum(), P.next_psum()]
                pst = [P.next_psum(), P.next_psum()]
                for k, hd in enumerate(heads):
                    pc, pb = hp(hd)
                    par = hd % 2
                    a = k // 2
                    P.op("pe", lambda e, a=a, par=par, pc=pc, pb=pb: e.matmul(
                        psx[par][:64, a * 128:(a + 1) * 128], lhsT=Bt[pb:pb + 64, pc, cs],
                        rhs=AR[pb:pb + 64, pc, ch].rearrange("p a t -> p (a t)"), start=True, stop=True),
                        reads=[Bt, AR], writes=[psx[par]])
                    P.op("pe", lambda e, a=a, par=par, pc=pc, pb=pb: e.matmul(
                        psl[par][:64, a * 128:(a + 1) * 128], lhsT=Kt[pb:pb + 64, pc, cs],
                        rhs=AR[pb:pb + 64, pc, ch].rearrange("p a t -> p (a t)"), start=True, stop=True),
                        reads=[Kt, AR], writes=[psl[par]])
                    P.op("pe", lambda e, a=a, par=par, pc=pc, pb=pb: e.matmul(
                        pst[par][:64, a * 64:(a + 1) * 64], lhsT=AR[pb:pb + 64, pc, ch, 0, :],
                        rhs=Bt[pb:pb + 64, pc, cs], start=True, stop=True),
                        reads=[Bt, AR], writes=[pst[par]])
                for par in range(2):
                    P.op("dve", lambda e, par=par: e.tensor_tensor(
                        out=B["XM"][:].rearrange("p (a q) x t -> p a q (x t)", q=2)[:, :, par, :],
                        in0=psx[par][:64, 0:256].rearrange("p (a x) -> p a x", x=128),
                        in1=m_xm[:, 0:2, :], op=ALU.mult), reads=[psx[par], m_xm], writes=[B["XM"]])
                    P.op("dve", lambda e, par=par: e.tensor_tensor(
                        out=B["LM"][:].rearrange("p (a q) x t -> p a q (x t)", q=2)[:, :, par, :],
                        in0=psl[par][:64, 0:256].rearrange("p (a x) -> p a x", x=128),
                        in1=m_xm[:, 0:2, :], op=ALU.mult), reads=[psl[par], m_xm], writes=[B["LM"]])
                    P.op("dve", lambda e, par=par: e.tensor_tensor(
                        out=B["XT"][:].rearrange("p (a q) t -> p a q t", q=2)[:, :, par, :],
                        in0=pst[par][:64, 0:128].rearrange("p (a x) -> p a x", x=64),
                        in1=m_xt[:, 0:2, :], op=ALU.mult), reads=[pst[par], m_xt], writes=[B["XT"]])
                P.op("pool", lambda e: e.tensor_tensor(out=B["Ac0"][:], in0=B["XM"][:, :, 0, :], in1=identg[:], op=ALU.add),
                     reads=[B["XM"], identg], writes=[B["Ac0"]])
                yield
                Pc, PTc, Ac = (B["XM"], lambda k: B["XM"][:, k, 0, :]), (B["XT"], lambda k: B["XT"][:, k, :]), B["Ac0"]
                for lvl in range(5):
                    lastl = (lvl == 4)
                    Pn = B["P%d" % (lvl % 2)]
                    PTn = B["PT%d" % (lvl % 2)]
                    Acn = B["Ac%d" % ((lvl + 1) % 2)]
                    psB = P.next_psum()
                    psA = None if lastl else P.next_psum()
                    for k in range(HG):
                        P.op("pe", lambda e, k=k: e.matmul(psB[:64, k * 64:(k + 1) * 64], lhsT=Pc[1](k), rhs=PTc[1](k),
                                                          start=True, stop=True), reads=[Pc[0], PTc[0]], writes=[psB])
                    if not lastl:
                        for k in range(HG):
                            P.op("pe", lambda e, k=k: e.matmul(psA[:64, k * 64:(k + 1) * 64], lhsT=PTc[1](k), rhs=Pc[1](k),
                                                              start=True, stop=True), reads=[Pc[0], PTc[0]], writes=[psA])
                    P.op("dve", lambda e: e.tensor_copy(out=PTn[:], in_=psB[:64, 0:256].rearrange("p (h x) -> p h x", x=64)),
                         reads=[psB], writes=[PTn])
                    if not lastl:
                        P.op("act", lambda e: e.copy(out=Pn[:], in_=psA[:64, 0:256].rearrange("p (h x) -> p h x", x=64)),
                             reads=[psA], writes=[Pn])
                    yield
                    psC = P.next_psum()
                    for k in range(HG):
                        P.op("pe", lambda e, k=k: e.matmul(psC[:64, k * 64:(k + 1) * 64], lhsT=PTn[:, k, :], rhs=Ac[:, k, :],
                                                          start=True, stop=True), reads=[PTn, Ac], writes=[psC])
                    P.op("dve", lambda e: e.tensor_tensor(out=Acn[:], in0=psC[:64, 0:256].rearrange("p (h x) -> p h x", x=64),
                                                          in1=Ac[:], op=ALU.add), reads=[psC, Ac], writes=[Acn])
                    Pc = (Pn, lambda k, Pn=Pn: Pn[:, k, :])
                    PTc = (PTn, lambda k, PTn=PTn: PTn[:, k, :])
                    Ac = Acn
                    yield
                NTb = Ac
                psW = P.next_psum()
                for k in range(HG):
                    P.op("pe", lambda e, k=k: e.matmul(psW[:64, k * 64:(k + 1) * 64], lhsT=B["LM"][:, k, 0, :], rhs=B["Vt"][:, k, :],
                                                      start=True, stop=True), reads=[B["LM"], B["Vt"]], writes=[psW])
                P.op("act", lambda e: e.copy(out=B["W1A"][:, :, 0, :], in_=psW[:64, 0:256].rearrange("p (h x) -> p h x", x=64)),
                     reads=[psW], writes=[B["W1A"]])
                yield
                psU = P.next_psum()
                for k in range(HG):
                    P.op("pe", lambda e, k=k: e.matmul(psU[:64, k * 128:(k + 1) * 128], lhsT=NTb[:, k, :],
                                                      rhs=B["W1A"][:, k].rearrange("p a t -> p (a t)"),
                                                      start=True, stop=True), reads=[NTb, B["W1A"]], writes=[psU])
                P.op("dve", lambda e: e.tensor_copy(out=B["UZ"][:].rearrange("p h a t -> p h (a t)"),
                                                    in_=psU[:64, :].rearrange("p (h x) -> p h x", x=128)),
                     reads=[psU], writes=[B["UZ"]])
                yield
                psG = P.next_psum()
                psQ = P.next_psum()
                for k, hd in enumerate(heads):
                    pc, pb = hp(hd)
                    P.op("pe", lambda e, k=k: e.matmul(psG[:64, k * 64:(k + 1) * 64], lhsT=B["UZ"][:, k, 1, :], rhs=B["Bh"][:, k, :],
                                                      start=True, stop=True), reads=[B["UZ"], B["Bh"]], writes=[psG])
                    P.op("pe", lambda e, k=k: e.matmul(psQ[:64, k * 64:(k + 1) * 64], lhsT=B["UZ"][:, k, 1, :], rhs=B["XM"][:, k, 1, :],
                                                      start=True, stop=True), reads=[B["UZ"], B["XM"]], writes=[psQ])
                P.op("act", lambda e: e.copy(out=B["GT"][:], in_=psG[:64, 0:256].rearrange("p (h x) -> p h x", x=64)),
                     reads=[psG], writes=[B["GT"]])
                P.op("dve", lambda e: e.tensor_tensor(out=B["QT"][:], in0=psQ[:64, 0:256].rearrange("p (h x) -> p h x", x=64),
                                                      in1=RtL[:, g * HG:(g + 1) * HG, cs], op=ALU.add),
                     reads=[psQ, RtL], writes=[B["QT"]])
                yield
                So = Ss[(scur[0] + ch) % 2]
                Sn = Ss[(scur[0] + ch + 1) % 2]
                psY = P.next_psum()
                psS = P.next_psum()
                for k, hd in enumerate(heads):
                    P.op("pe", lambda e, k=k: e.matmul(psY[:64, k * 64:(k + 1) * 64], lhsT=B["UZ"][:, k, 0, :], rhs=B["XM"][:, k, 1, :],
                                                      start=True, stop=False), reads=[B["UZ"], B["XM"]], writes=[psY])
                    P.op("pe", lambda e, k=k: e.matmul(psY[:64, k * 64:(k + 1) * 64], lhsT=B["Vt"][:, k, :], rhs=B["LM"][:, k, 1, :],
                                                      start=False, stop=False), reads=[B["Vt"], B["LM"]], writes=[psY])
                    P.op("pe", lambda e, k=k, hd=hd: e.matmul(psY[:64, k * 64:(k + 1) * 64], lhsT=So[:, hd, :], rhs=B["QT"][:, k, :],
                                                             start=False, stop=True), reads=[So, B["QT"]], writes=[psY])
                for k, hd in enumerate(heads):
                    P.op("pe", lambda e, k=k: e.matmul(psS[:64, k * 64:(k + 1) * 64], lhsT=B["Bh"][:, k, :], rhs=B["UZ"][:, k, 0, :],
                                                      start=True, stop=False), reads=[B["UZ"], B["Bh"]], writes=[psS])
                    P.op("pe", lambda e, k=k: e.matmul(psS[:64, k * 64:(k + 1) * 64], lhsT=B["Kh"][:, k, :], rhs=B["Vt"][:, k, :],
                                                      start=False, stop=False), reads=[B["Kh"], B["Vt"]], writes=[psS])
                    P.op("pe", lambda e, k=k, hd=hd: e.matmul(psS[:64, k * 64:(k + 1) * 64], lhsT=B["GT"][:, k, :], rhs=So[:, hd, :],
                                                             start=False, stop=True), reads=[So, B["GT"]], writes=[psS])
                P.op("act", lambda e: e.copy(out=Yb[:, g * HG:(g + 1) * HG, cs], in_=psY[:64, 0:256].rearrange("p (h x) -> p h x", x=64)),
                     reads=[psY], writes=[Yb])
                for k, hd in enumerate(heads):
                    P.op("dve", lambda e, k=k, hd=hd: e.scalar_tensor_tensor(
                        out=Sn[:, hd, :], in0=So[:, hd, :], scalar=wCT[:, hd, ch:ch + 1], in1=psS[:64, k * 64:(k + 1) * 64],
                        op0=ALU.mult, op1=ALU.add), reads=[So, wCT, psS], writes=[Sn])
                yield

            def phase_a(te):
                ts = slice(te * TE, (te + 1) * TE)
                xt = xt_tiles[te % 2]
                h = h_tiles[te % 2]
                tt = te // (TT // TE)
                P.dma("sp", xt, xt[:], XTb[tt], XTv[:, :, ts])
                rmsnorm_tile(xt, h, gidx, sq_tile, rstd_t, w=TE)
                for pc in range(4):
                    proj(4 + pc, 128, h, lambda ps, pc=pc: P.op(
                        "act", lambda e: e.copy(out=gc[:, pc, :], in_=ps[:, :TE]), reads=[ps], writes=[gc]))
                yield
                for pc in range(4):
                    proj(8 + pc, 128, h, lambda ps, pc=pc: P.op(
                        "dve", lambda e: e.tensor_tensor(out=ub[:, pc, 2:TE + 2], in0=ps[:, :TE], in1=gc[:, pc, :], op=ALU.mult),
                        reads=[ps, gc], writes=[ub]))
                yield
                for pc in range(4):
                    proj(pc, 128, h, lambda ps, pc=pc: P.op(
                        "act", lambda e: e.copy(out=gb[:, pc, :], in_=ps[:, :TE]), reads=[ps], writes=[gb]))
                yield
                for pc in range(4):
                    ca = cacc[pc % 2]
                    P.op("dve", lambda e, pc=pc, ca=ca: e.tensor_scalar(out=ca[:], in0=ub[:, pc, 2:TE + 2],
                                                                      scalar1=ecols[:, 36 + 8 + pc:36 + 8 + pc + 1], scalar2=None,
                                                                      op0=ALU.mult), reads=[ub, ecols], writes=[ca])
                    P.op("dve", lambda e, pc=pc, ca=ca: e.scalar_tensor_tensor(
                        out=ca[:], in0=ub[:, pc, 1:TE + 1], scalar=ecols[:, 36 + 4 + pc:36 + 4 + pc + 1], in1=ca[:],
                        op0=ALU.mult, op1=ALU.add), reads=[ub, ecols, ca], writes=[ca])
                    P.op("dve", lambda e, pc=pc, ca=ca: e.scalar_tensor_tensor(
                        out=ca[:], in0=ub[:, pc, 0:TE], scalar=ecols[:, 36 + pc:36 + pc + 1], in1=ca[:],
                        op0=ALU.mult, op1=ALU.add), reads=[ub, ecols, ca], writes=[ca])
                    P.op("dve", lambda e, pc=pc, ca=ca: e.tensor_tensor(out=yconv[:, pc, :], in0=ca[:], in1=gb[:, pc, :], op=ALU.mult),
                         reads=[ca, gb], writes=[yconv])
                P.op("pool", lambda e: e.tensor_copy(out=ub[:, :, 0:2], in_=ub[:, :, TE:TE + 2]), reads=[ub], writes=[ub])
                yield
                for j in range(12):
                    def cons(ps, j=j):
                        pr = PR[j]
                        P.op("act", lambda e: e.copy(out=pr[:, 0:1], in_=PRh[:, j:j + 1]), reads=[PRh], writes=[pr])
                        P.op("act", lambda e: e.copy(out=pr[:, 1:TE + 1], in_=ps[:, :TE]), reads=[ps], writes=[pr])
                        d = dtmp[j % 3]
                        P.op("pool", lambda e: e.tensor_tensor(out=d[:], in0=pr[:, 0:TE], in1=pr[:, 1:TE + 1], op=ALU.subtract),
                             reads=[pr], writes=[d])
                        P.op("pool", lambda e: e.tensor_copy(out=PRh[:, j:j + 1], in_=pr[:, TE:TE + 1]), reads=[pr], writes=[PRh])
                        P.op("dve", lambda e: e.scalar_tensor_tensor(out=pr[:, 1:TE + 1], in0=d[:], scalar=ecols[:, j:j + 1],
                                                                     in1=pr[:, 1:TE + 1], op0=ALU.mult, op1=ALU.add),
                             reads=[d, ecols, pr], writes=[pr])
                    proj(12 + j, 128, h, cons)
                    if j % 3 == 2:
                        yield
                for li, (lo, n) in enumerate(((0, 32), (32, 32), (64, 96))):
                    ps = P.next_psum()
                    for c in range(NC8):
                        P.op("pe", lambda e, c=c, lo=lo, n=n: e.matmul(ps[:n, :TE], lhsT=wlora[:, c, lo:lo + n], rhs=h[:, c, :],
                                                                      start=(c == 0), stop=(c == NC8 - 1)),
                             reads=[wlora, h], writes=[ps])
                    pl = PL[li]
                    P.op("act", lambda e, li=li, n=n, pl=pl: e.copy(out=pl[:n, 0:1], in_=PRh[:n, 12 + li:13 + li]),
                         reads=[PRh], writes=[pl])
                    P.op("act", lambda e, n=n, pl=pl, ps=ps: e.copy(out=pl[:n, 1:TE + 1], in_=ps[:n, :TE]), reads=[ps], writes=[pl])
                    d = dtmp[li % 3]
                    P.op("pool", lambda e, n=n, pl=pl, d=d: e.tensor_tensor(out=d[:n], in0=pl[:n, 0:TE], in1=pl[:n, 1:TE + 1],
                                                                           op=ALU.subtract), reads=[pl], writes=[d])
                    P.op("pool", lambda e, n=n, pl=pl, li=li: e.tensor_copy(out=PRh[:n, 12 + li:13 + li], in_=pl[:n, TE:TE + 1]),
                         reads=[pl], writes=[PRh])
                    P.op("dve", lambda e, n=n, pl=pl, d=d, li=li: e.scalar_tensor_tensor(
                        out=pl[:n, 1:TE + 1], in0=d[:n], scalar=lcols[:n, li:li + 1], in1=pl[:n, 1:TE + 1],
                        op0=ALU.mult, op1=ALU.add), reads=[d, lcols, pl], writes=[pl])
                P.op("act", lambda e: e.activation(out=tdw[:], in_=PL[0][:32, 1:TE + 1], func=AF.Tanh), reads=[PL[0]], writes=[tdw])
                P.op("act", lambda e: e.copy(out=dab[:], in_=PL[1][:32, 1:TE + 1]), reads=[PL[1]], writes=[dab])
                P.op("act", lambda e: e.activation(out=sdg[:], in_=PL[2][:96, 1:TE + 1], func=AF.Sigmoid), reads=[PL[2]], writes=[sdg])
                yield
                for pc in range(4):
                    fs = slice(pc * 128, (pc + 1) * 128)
                    rr = PR[pc]
                    kx = PR[4 + pc]
                    vv = PR[8 + pc]
                    R1 = slice(1, TE + 1)
                    psw = P.next_psum()
                    P.op("pe", lambda e, fs=fs, psw=psw: e.matmul(psw[:, :TE], lhsT=w_up[:, fs], rhs=tdw[:], start=True, stop=True),
                         reads=[w_up, tdw], writes=[psw])
                    psa = P.next_psum()
                    P.op("pe", lambda e, fs=fs, psa=psa: e.matmul(psa[:, :TE], lhsT=a_up[:, fs], rhs=dab[:], start=True, stop=True),
                         reads=[a_up, dab], writes=[psa])
                    P.op("act", lambda e, pc=pc, psw=psw: e.activation(out=lw[:], in_=psw[:, :TE], func=AF.Sigmoid,
                                                                      bias=ecols[:, 12 + pc:13 + pc]),
                         reads=[psw, ecols], writes=[lw])
                    P.op("act", lambda e, pc=pc, psa=psa: e.activation(out=aa[:], in_=psa[:, :TE], func=AF.Sigmoid,
                                                                      bias=ecols[:, 16 + pc:17 + pc]),
                         reads=[psa, ecols], writes=[aa])
                    P.op("dve", lambda e: e.tensor_scalar(out=lw[:], in0=lw[:], scalar1=LWS, scalar2=None, op0=ALU.mult),
                         reads=[lw], writes=[lw])
                    P.op("dve", lambda e, pc=pc, kx=kx: e.tensor_scalar(out=kkb[:], in0=kx[:, R1], scalar1=ecols[:, 20 + pc:21 + pc],
                                                                      scalar2=None, op0=ALU.mult), reads=[kx, ecols], writes=[kkb])
                    P.op("act", lambda e: e.activation(out=kk2[:], in_=kkb[:], func=AF.Square), reads=[kkb], writes=[kk2])
                    pss = P.next_psum()
                    P.op("pe", lambda e, pss=pss: e.matmul(pss[:, :TE], lhsT=blockones[:], rhs=kk2[:], start=True, stop=True),
                         reads=[blockones, kk2], writes=[pss])
                    P.op("act", lambda e, pss=pss: e.activation(out=rsb[:], in_=pss[:, :TE], func=AF.Ln, bias=ccon[:, 0:1]),
                         reads=[pss, ccon], writes=[rsb])
                    P.op("act", lambda e: e.activation(out=rsb[:], in_=rsb[:], func=AF.Exp, scale=-0.5), reads=[rsb], writes=[rsb])
                    P.op("dve", lambda e: e.tensor_tensor(out=kkn[:], in0=kkb[:], in1=rsb[:], op=ALU.mult), reads=[kkb, rsb], writes=[kkn])
                    P.op("dve", lambda e, pc=pc: e.tensor_scalar(out=tmpk[:], in0=aa[:], scalar1=ecols[:, 24 + pc:25 + pc],
                                                                 scalar2=ecols[:, 32 + pc:33 + pc], op0=ALU.mult, op1=ALU.add),
                         reads=[aa, ecols], writes=[tmpk])
                    P.op("dve", lambda e, kx=kx: e.tensor_tensor(out=kx[:, R1], in0=kx[:, R1], in1=tmpk[:], op=ALU.mult),
                         reads=[kx, tmpk], writes=[kx])
                    P.op("pool", lambda e: e.tensor_tensor(out=bv[:], in0=kkn[:], in1=aa[:], op=ALU.mult), reads=[kkn, aa], writes=[bv])
                    P.op("dve", lambda e, pc=pc, rr=rr, kx=kx: e.scalar_tensor_tensor(
                        out=rkr[:], in0=rr[:, R1], scalar=ecols[:, 28 + pc:29 + pc], in1=kx[:, R1], op0=ALU.mult, op1=ALU.mult),
                        reads=[rr, ecols, kx], writes=[rkr])
                    psr = P.next_psum()
                    P.op("pe", lambda e, psr=psr: e.matmul(psr[:, :TE], lhsT=blockones[:], rhs=rkr[:], start=True, stop=True),
                         reads=[blockones, rkr], writes=[psr])
                    P.op("dve", lambda e, pc=pc, vv=vv, psr=psr: e.tensor_tensor(out=bonus[:, pc, :], in0=psr[:, :TE], in1=vv[:, R1],
                                                                               op=ALU.mult), reads=[psr, vv], writes=[bonus])
                    yield
                    P.op("dve", lambda e: e.tensor_tensor_scan(out=cc[:], data0=rmask[:], data1=lw[:], initial=0.0,
                                                               op0=ALU.mult, op1=ALU.add), reads=[rmask, lw], writes=[cc])
                    P.op("pool", lambda e: e.tensor_tensor(out=cp[:], in0=cc[:], in1=lw[:], op=ALU.subtract), reads=[cc, lw], writes=[cp])
                    for ch in range(NCHT):
                        P.op("dve", lambda e, ch=ch: e.tensor_scalar(out=cd[:, ch * 64:(ch + 1) * 64], in0=cc[:, ch * 64:(ch + 1) * 64],
                                                                     scalar1=cc[:, ch * 64 + 63:ch * 64 + 64], scalar2=None,
                                                                     op0=ALU.subtract), reads=[cc], writes=[cd])
                    P.op("act", lambda e: e.activation(out=ec[:], in_=cc[:], func=AF.Exp), reads=[cc], writes=[ec])
                    P.op("act", lambda e: e.activation(out=eci[:], in_=cc[:], func=AF.Exp, scale=-1.0), reads=[cc], writes=[eci])
                    P.op("act", lambda e: e.activation(out=ecp[:], in_=cp[:], func=AF.Exp), reads=[cp], writes=[ecp])
                    P.op("act", lambda e: e.activation(out=eCc[:], in_=cd[:], func=AF.Exp, scale=-1.0), reads=[cd], writes=[eCc])
                    P.op("act", lambda e, pc=pc: e.copy(out=wCfm[:, pc, :], in_=ec[:].rearrange("p (c t) -> p c t", t=64)[:, :, 63]),
                         reads=[ec], writes=[wCfm])
                    yield
                    v3 = lambda b: b[:].rearrange("p (c t) -> p c t", t=64)
                    P.op("dve", lambda e, pc=pc: e.scalar_tensor_tensor(out=AR[:, pc, :, 0, :], in0=v3(kkn), scalar=-1.0, in1=v3(ecp),
                                                                        op0=ALU.mult, op1=ALU.mult), reads=[kkn, ecp], writes=[AR])
                    P.op("pool", lambda e, pc=pc, rr=rr: e.tensor_tensor(out=AR[:, pc, :, 1, :],
                                                                       in0=rr[:, R1].rearrange("p (c t) -> p c t", t=64),
                                                                       in1=v3(ec), op=ALU.mult), reads=[rr, ec], writes=[AR])
                    P.op("dve", lambda e, pc=pc: e.tensor_tensor(out=Bt[:, pc, :], in0=bv[:], in1=eci[:], op=ALU.mult),
                         reads=[bv, eci], writes=[Bt])
                    P.op("pool", lambda e, pc=pc, kx=kx: e.tensor_tensor(out=Kt[:, pc, :], in0=kx[:, R1], in1=eci[:], op=ALU.mult),
                         reads=[kx, eci], writes=[Kt])
                    P.op("dve", lambda e, pc=pc: e.tensor_tensor(out=BhT[:, pc, :], in0=bv[:], in1=eCc[:], op=ALU.mult),
                         reads=[bv, eCc], writes=[BhT])
                    P.op("pool", lambda e, pc=pc, kx=kx: e.tensor_tensor(out=KhT[:, pc, :], in0=kx[:, R1], in1=eCc[:], op=ALU.mult),
                         reads=[kx, eCc], writes=[KhT])
                yield
                for par in range(2):
                    pb = par * 64
                    psc = P.next_psum()
                    P.op("pe", lambda e, pb=pb, psc=psc: e.matmul(psc[:64, 0:4 * NCHT], lhsT=ident[pb:pb + 64, pb:pb + 64],
                                                                  rhs=wCfm[pb:pb + 64].rearrange("p a c -> p (a c)"), start=True, stop=True),
                         reads=[ident, wCfm], writes=[psc])
                    P.op("dve", lambda e, par=par, psc=psc: e.tensor_copy(
                        out=wCT[:].rearrange("p (a q) c -> p a q c", q=2)[:, :, par, :],
                        in_=psc[:64, 0:4 * NCHT].rearrange("p (a c) -> p a c", c=NCHT)),
                        reads=[psc], writes=[wCT])
                    psr2 = P.next_psum()
                    for pc in range(4):
                        P.op("pe", lambda e, pc=pc, pb=pb, psr2=psr2: e.matmul(
                            psr2[:64, pc * TE:(pc + 1) * TE].rearrange("p (c t) -> p c t", t=64),
                            lhsT=(ident if SDT == F32 else ident_bf)[pb:pb + 64, pb:pb + 64],
                            rhs=AR[pb:pb + 64, pc, :, 1, :], start=True, stop=True), reads=[ident, ident_bf, AR], writes=[psr2])
                    P.op("act", lambda e, par=par, psr2=psr2: e.copy(
                        out=RtL[:].rearrange("p (a q) t -> p a q t", q=2)[:, :, par, :],
                        in_=psr2[:64, 0:4 * TE].rearrange("p (a t) -> p a t", t=TE)), reads=[psr2], writes=[RtL])
                yield

            def phase_b(te):
                ts = slice(te * TE, (te + 1) * TE)
                xt = xt_tiles[te % 2]
                tt = te // (TT // TE)
                gens = [chunk_pipeline(ch, g, te) for ch in range(NCHT) for g in range(2)]
                alive = True
                while alive:
                    alive = False
                    for gi in gens:
                        try:
                            next(gi)
                            alive = True
                        except StopIteration:
                            pass
                    yield
                scur[0] += NCHT
                P.op("act", lambda e: e.activation(out=ysq[:], in_=Yb[:], func=AF.Square), reads=[Yb], writes=[ysq])
                NH = 512 // TE
                ps1 = [P.next_psum() for _ in range(8 // NH)]
                for hd in range(8):
                    P.op("pe", lambda e, hd=hd: e.matmul(ps1[hd // NH][:64, (hd % NH) * TE:(hd % NH + 1) * TE], lhsT=ones_f[0:64, 0:64],
                                                        rhs=Yb[:, hd, :], start=True, stop=True), reads=[ones_f, Yb], writes=[ps1[hd // NH]])
                for b in range(8 // NH):
                    P.op("act", lambda e, b=b: e.activation(out=mean[:, b * NH:(b + 1) * NH, :],
                                                            in_=ps1[b][:64, :].rearrange("p (h t) -> p h t", t=TE),
                                                            func=AF.Copy, scale=1.0 / 64), reads=[ps1[b]], writes=[mean])
                yield
                ps2 = [P.next_psum() for _ in range(8 // NH)]
                for hd in range(8):
                    P.op("pe", lambda e, hd=hd: e.matmul(ps2[hd // NH][:64, (hd % NH) * TE:(hd % NH + 1) * TE], lhsT=ones_f[0:64, 0:64],
                                                        rhs=ysq[:, hd, :], start=True, stop=True), reads=[ones_f, ysq], writes=[ps2[hd // NH]])
                P.op("act", lambda e: e.activation(out=ysq[:], in_=mean[:], func=AF.Square), reads=[mean], writes=[ysq])
                for b in range(8 // NH):
                    P.op("dve", lambda e, b=b: e.scalar_tensor_tensor(
                        out=var[:, b * NH:(b + 1) * NH, :], in0=ps2[b][:64, :].rearrange("p (h t) -> p h t", t=TE), scalar=1.0 / 64,
                        in1=ysq[:, b * NH:(b + 1) * NH, :], op0=ALU.mult, op1=ALU.subtract), reads=[ps2[b], ysq], writes=[var])
                P.op("act", lambda e: e.activation(out=var[:], in_=var[:], func=AF.Ln, bias=ccon[:64, 1:2]), reads=[var, ccon], writes=[var])
                P.op("act", lambda e: e.activation(out=var[:], in_=var[:], func=AF.Exp, scale=-0.5), reads=[var], writes=[var])
                P.op("pool", lambda e: e.tensor_tensor(out=ycb[:], in0=Yb[:], in1=mean[:], op=ALU.subtract), reads=[Yb, mean], writes=[ycb])
                P.op("pool", lambda e: e.tensor_tensor(out=ycb[:], in0=ycb[:], in1=var[:], op=ALU.mult), reads=[ycb, var], writes=[ycb])
                for hd in range(8):
                    P.op("dve", lambda e, hd=hd: e.tensor_scalar(out=ycb[:, hd, :], in0=ycb[:, hd, :], scalar1=hcols[:, hd:hd + 1],
                                                                 scalar2=hcols[:, 8 + hd:9 + hd], op0=ALU.mult, op1=ALU.add),
                         reads=[ycb, hcols], writes=[ycb])
                for par in range(2):
                    pb = par * 64
                    psbb = P.next_psum()
                    for pc in range(4):
                        P.op("pe", lambda e, pc=pc, pb=pb, psbb=psbb: e.matmul(
                            psbb[:64, pc * TE:(pc + 1) * TE], lhsT=ident[pb:pb + 64, pb:pb + 64],
                            rhs=bonus[pb:pb + 64, pc, :], start=True, stop=True), reads=[ident, bonus], writes=[psbb])
                    P.op("dve", lambda e, par=par, psbb=psbb: e.tensor_tensor(
                        out=ycb[:].rearrange("p (a q) t -> p a q t", q=2)[:, :, par, :],
                        in0=psbb[:64, 0:4 * TE].rearrange("p (a t) -> p a t", t=TE),
                        in1=ycb[:].rearrange("p (a q) t -> p a q t", q=2)[:, :, par, :], op=ALU.add),
                        reads=[psbb, ycb], writes=[ycb])
                yield
                psg = [P.next_psum() for _ in range(8 // NH)]
                for hd in range(8):
                    P.op("pe", lambda e, hd=hd: e.matmul(psg[hd // NH][:64, (hd % NH) * TE:(hd % NH + 1) * TE],
                                                        lhsT=g_up[:, hd * 64:(hd + 1) * 64], rhs=sdg[:], start=True, stop=True),
                         reads=[g_up, sdg], writes=[psg[hd // NH]])
                for b in range(8 // NH):
                    P.op("dve", lambda e, b=b: e.tensor_tensor(out=yfin[:, b * NH:(b + 1) * NH, :],
                                                               in0=psg[b][:64, :].rearrange("p (h t) -> p h t", t=TE),
                                                               in1=ycb[:, b * NH:(b + 1) * NH, :], op=ALU.mult),
                         reads=[psg[b], ycb], writes=[yfin])
                yield
                for o in range(NC8):
                    wc = woc_t[ocnt[0] % 2]
                    wr = wor_t[ocnt[0] % 2]
                    ocnt[0] += 1
                    P.dma("sp", wc, wc[:], woc_bf, woc_bf.t[o])
                    P.dma("sp", wr, wr[:], wor_bf, wor_bf.t[o])
                    ps = P.next_psum()
                    for pc in range(4):
                        P.op("pe", lambda e, pc=pc, wc=wc, ps=ps: e.matmul(ps[:, :TE], lhsT=wc[:, pc, :], rhs=yconv[:, pc, :],
                                                                          start=(pc == 0), stop=False), reads=[wc, yconv], writes=[ps])
                    for hd in range(8):
                        P.op("pe", lambda e, hd=hd, wr=wr, ps=ps: e.matmul(ps[:, :TE], lhsT=wr[:, hd, :], rhs=yfin[:, hd, :],
                                                                          start=False, stop=(hd == 7)), reads=[wr, yfin], writes=[ps])
                    P.op("dve", lambda e, o=o, ps=ps, xt=xt: e.tensor_tensor(out=xt[:, o, :], in0=ps[:, :TE], in1=xt[:, o, :], op=ALU.add),
                         reads=[ps, xt], writes=[xt])
                    if o % 2 == 1:
                        yield
                P.dma("pool", XTb[tt], XTv[:, :, ts], xt, xt[:])

            HSETS = [hset0, hset1]

            def bind(k):
                nonlocal AR, Bt, Kt, BhT, KhT, bonus, wCT, RtL, sdg, yconv, PR
                (AR, Bt, Kt, BhT, KhT, bonus, wCT, RtL, sdg, yconv, PR) = HSETS[k]

            def step(gen, k):
                bind(k)
                try:
                    next(gen)
                    return True
                except StopIteration:
                    return False

            ga = phase_a(0)
            while step(ga, 0):
                pass
            for te in range(NTE):
                gb_ = phase_b(te)
                ga = phase_a(te + 1) if te + 1 < NTE else None
                alive_b = True
                alive_a = ga is not None
                it = 0
                while alive_b or alive_a:
                    if alive_b:
                        alive_b = step(gb_, te % 2)
                    it += 1
                    if alive_a and (it % cfg.a_every == 0 or not alive_b):
                        alive_a = step(ga, (te + 1) % 2)
            P.barrier()
            A.reset(m0)

        seq = []
        for layer in range(cfg.layers):
            seq.append(("ffn", layer, 0))
            seq.append(("mix", layer, 0))
            seq.append(("ffn", layer, 1))
        if cfg.stop is not None:
            seq = seq[:seq.index(cfg.stop) + 1]
        if cfg.skip_ffn:
            seq = [s for s in seq if s[0] != "ffn"]
        if cfg.seq is not None:
            seq = list(cfg.seq)
        ffns = [s for s in seq if s[0] == "ffn"]
        if ffns:
            cast_weights(ffns[0][1], ffns[0][2])
        rope_tables()
        transpose_in()
        evens = [s for s in seq if s[0] == "mix" and s[1] % 2 == 0]
        if evens:
            cast_even(evens[0][1] // 2)
        evens_pending = evens[1:]
        for s in seq:
            if s[0] == "ffn":
                k = ffns.index(s)
                if k + 1 < len(ffns):
                    cast_weights(ffns[k + 1][1], ffns[k + 1][2])
                ffn_phase(s[1], s[2])
            else:
                if s[1] % 2 == 1:
                    if evens_pending:
                        cast_even(evens_pending.pop(0)[1] // 2)
                    mla_layer(s[1])
                else:
                    while evens_pending and evens_pending[0][1] <= s[1]:
                        cast_even(evens_pending.pop(0)[1] // 2)
                    even_layer(s[1])

        P.pspool = None
        yin_tiles = [A.alloc("yin%d" % i, [NC8, 128], F32) for i in range(2)]
        yo_tiles = [A.alloc("yo%d" % i, [D], F32) for i in range(2)]
        for b in range(NB):
            yi = yin_tiles[b % 2]
            yo = yo_tiles[b % 2]
            P.dma("sp", yi, yi[:], XTb[b // 4], XTv[:, :, b * 128:(b + 1) * 128])
            for half in range(2):
                ps = P.next_psum()
                for jj in range(4):
                    c = half * 4 + jj
                    P.op("pe", lambda e, ps=ps, yi=yi, c=c, jj=jj: e.transpose(
                        out=ps[:, jj * 128:(jj + 1) * 128], in_=yi[:, c, :], identity=ident[:]),
                        reads=[yi, ident], writes=[ps])
                if half:
                    P.op("act", lambda e, ps=ps, yo=yo, half=half: e.copy(
                        out=yo[:, half * 512:(half + 1) * 512], in_=ps[:]), reads=[ps], writes=[yo])
                else:
                    P.op("dve", lambda e, ps=ps, yo=yo, half=half: e.tensor_copy(
                        out=yo[:, half * 512:(half + 1) * 512], in_=ps[:]), reads=[ps], writes=[yo])
            P.dma("pool", y_out, y_out.t[b * 128:(b + 1) * 128, :], yo, yo[:])
        P.wait_all("pool", [y_out])
        P.wait_all("sp", [y_out])
        P.emit()
    return nc


def _perm_swap(n=64):
    return np.concatenate([np.arange(n // 2, n), np.arange(0, n // 2)])


def host_prep(inputs):
    out = {}
    for k in ("norm_gains", "ffn_w_gate", "ffn_w_up", "ffn_w_down", "mla_w_kv_up", "odd_w_out",
              "mla_q_a_norm", "mla_kv_a_norm", "mla_q_norm", "mla_k_norm",
              "even_w_in", "even_w_out", "even_conv_w", "even_mu_shift", "rwkv_w0", "rwkv_a0", "rwkv_k_k", "rwkv_k_a",
              "rwkv_ln_w", "rwkv_ln_b", "rwkv_w_up", "rwkv_a_up", "rwkv_g_up"):
        out[k] = np.ascontiguousarray(inputs[k], dtype=np.float32)
    out["rwkv_r_k"] = np.ascontiguousarray(np.asarray(inputs["rwkv_r_k"], dtype=np.float32).reshape(2, 512))
    sw = _perm_swap(64)
    win = np.asarray(inputs["odd_w_in"], dtype=np.float32)
    out["odd_w_in_p"] = np.ascontiguousarray(np.concatenate([win, win[:, :, 640:704][:, :, sw]], axis=2))
    wq = np.asarray(inputs["mla_w_q_up"], dtype=np.float32).reshape(2, QR, 8, 192)
    wqp = np.concatenate([wq, wq[:, :, :, 128:192][:, :, :, sw]], axis=3)
    out["mla_w_q_up_p"] = np.ascontiguousarray(wqp.reshape(2, QR, 2048))
    qk = np.zeros((2, 128, 6), np.float32)
    for j in range(2):
        gq = np.asarray(inputs["mla_q_norm"][j], dtype=np.float32)
        gk = np.asarray(inputs["mla_k_norm"][j], dtype=np.float32)
        qk[j, :, 0] = gq[:128]
        qk[j, :64, 1] = gq[128:]
        qk[j, :64, 2] = gq[128:][sw]
        qk[j, :, 3] = gk[:128]
        qk[j, :64, 4] = gk[128:]
        qk[j, :64, 5] = gk[128:][sw]
    out["qk_cols"] = qk
    inv_freq = (np.float32(10000.0) ** (-np.arange(0, 64, 2, dtype=np.float32) / np.float32(64))).astype(np.float32)
    rc = np.zeros((64, 2), np.float32)
    rc[:, 0] = np.concatenate([inv_freq, inv_freq])
    rc[:32, 1] = -1.0
    rc[32:, 1] = 1.0
    out["rope_c"] = rc
    return out


def make_in_maps(inputs, ncores, T):
    shared = host_prep(inputs)
    maps = []
    for i in range(ncores):
        m = dict(shared)
        m["x"] = np.ascontiguousarray(inputs["x"][i, :T], dtype=np.float32)
        m["positions"] = np.ascontiguousarray(inputs["positions"][i:i + 1, :T]).astype(np.int32)
        maps.append(m)
    return maps


def kernel(**inputs):
    cfg = Cfg()
    nc = build(cfg)
    in_maps = make_in_maps(inputs, 8, cfg.T)
    res = run_bass_kernel_spmd(nc, in_maps, core_ids=list(range(8)))
    return np.stack([np.asarray(r["y"]) for r in res.results], axis=0).astype(np.float32)
```

```python
import contextlib
import numpy as np
import concourse.bass as bass
import concourse.mybir as mybir
from concourse.bass_utils import run_bass_kernel_spmd

F32 = mybir.dt.float32
BF16 = mybir.dt.bfloat16
I32 = mybir.dt.int32
AF = mybir.ActivationFunctionType
ALU = mybir.AluOpType

D = 1024
DFF = 2816
NM = DFF // 128
NC8 = D // 128
EPS = 1e-6
TT = 512


class Buf:
    __slots__ = ("name", "w", "r", "dsem", "dcnt", "t", "kind")

    def __init__(self, name, t=None, kind="x"):
        self.name = name
        self.kind = kind
        self.w = None
        self.r = {}
        self.dsem = None
        self.dcnt = 0
        self.t = t

    def __getitem__(self, idx):
        return self.t[idx]


ENGS = ("pe", "act", "dve", "pool", "sp")


class _Rec:
    def __init__(self):
        self.call = None

    def __getattr__(self, name):
        def f(*a, **k):
            assert self.call is None
            self.call = (name, a, k)
            return None
        return f


class Prog:
    def __init__(self, nc, stack):
        self.nc = nc
        self.stack = stack
        self.q = {e: [] for e in ENGS}
        self.sems = []
        self.esem = {}
        for e in ENGS:
            self.esem[e] = self.new_sem("e_" + e)
        self.cnt = {e: 0 for e in ENGS}
        self.seen = {e: {} for e in ENGS}
        self.nbuf = 0
        self.psum_rr = 0
        self.psums = []
        self.same_eng_sync = True
        self.dbufs = []
        self.free_dsems = [[], []]
        self.pe_mode = None
        self.pe_drain = False
        self.dsem_issued = {}
        self.pspool = None

    def new_sem(self, name):
        s = self.stack.enter_context(self.nc.semaphore(name))
        self.sems.append(s)
        return len(self.sems) - 1

    def sb(self, name, shape, dtype):
        t = self.stack.enter_context(self.nc.sbuf_tensor(name, list(shape), dtype))
        return Buf(name, t, "sb")

    def ps(self, name, shape, dtype=F32):
        t = self.stack.enter_context(self.nc.psum_tensor(name, list(shape), dtype))
        return Buf(name, t, "ps")

    def dram(self, name, shape, dtype, kind="Internal"):
        t = self.nc.dram_tensor(name, list(shape), dtype, kind=kind)
        return Buf(name, t.ap(), "dram")

    def view(self, b, name):
        return Buf(name, b.t, b.kind)

    def op(self, eng, fn, reads=(), writes=(), dma=None):
        waits = {}
        seen = self.seen[eng]

        def need(tok):
            if tok is None:
                return
            s, v = tok
            if eng == "pe" and s == self.esem["pe"]:
                return
            if (not self.same_eng_sync) and s == self.esem[eng]:
                return
            if s in self.dsem_issued:
                v = max(v, self.dsem_issued[s])
            if seen.get(s, 0) >= v:
                return
            if waits.get(s, 0) < v:
                waits[s] = v

        for b in reads:
            need(b.w)
            if b.kind == "ps":
                for s, v in b.r.items():
                    if s != self.esem[eng]:
                        need((s, v))
        for b in writes:
            need(b.w)
            for s, v in b.r.items():
                need((s, v))
        for s, v in waits.items():
            seen[s] = v
        if dma is not None:
            kind = 1 if eng == "pool" else 0
            if dma.dsem is None:
                dma.dsem = [None, None]
                dma.dcnt = [0, 0]
                self.dbufs.append(dma)
            if dma.dsem[kind] is None:
                if self.free_dsems[kind]:
                    dma.dsem[kind], dma.dcnt[kind] = self.free_dsems[kind].pop()
                else:
                    dma.dsem[kind] = self.new_sem("d%d_%s" % (len(self.sems), dma.name))
            dma.dcnt[kind] += 16
            self.dsem_issued[dma.dsem[kind]] = dma.dcnt[kind]
            tok = (dma.dsem[kind], dma.dcnt[kind])
            inc = 16
        else:
            self.cnt[eng] += 1
            tok = (self.esem[eng], self.cnt[eng])
            inc = 1
        for b in reads:
            if b.r.get(tok[0], 0) < tok[1]:
                b.r[tok[0]] = tok[1]
        for b in writes:
            b.w = tok
            b.r = {}
        rec = _Rec()
        fn(rec)
        assert rec.call is not None
        if eng == "pe":
            st = rec.call[2].get("lhsT", rec.call[2].get("in_"))
            shp = list(st.shape)
            rnd = lambda v: 32 if v <= 32 else (64 if v <= 64 else 128)
            mode = (rnd(shp[0]), rnd(int(np.prod(shp[1:]))))
            if mode != self.pe_mode:
                if self.pe_mode is not None and self.pe_drain:
                    self.q[eng].append(([], ("drain", (), {}), None, 0))
                self.pe_mode = mode
        self.q[eng].append((list(waits.items()), rec.call, tok[0], inc))
        return tok

    def wait_all(self, eng, bufs):
        waits = {}
        for b in bufs:
            toks = [b.w] + list(b.r.items())
            for tok in toks:
                if tok is None:
                    continue
                s, v = tok
                if waits.get(s, 0) < v:
                    waits[s] = v
        self.q[eng].append((list(waits.items()), None, None, 0))

    def emit(self):
        nc = self.nc
        with nc.allow_non_contiguous_dma(reason="small param loads"), nc.Block() as block:
            def run(ename):
                def body(eng):
                    for waits, fn, s, inc in self.q[ename]:
                        for ws, wv in waits:
                            eng.wait_ge(self.sems[ws], wv)
                        if fn is not None:
                            name, a, k = fn
                            ins = getattr(eng, name)(*a, **k)
                            if s is not None:
                                ins.then_inc(self.sems[s], inc)
                return body

            block.tensor(run("pe"))
            block.scalar(run("act"))
            block.vector(run("dve"))
            block.gpsimd(run("pool"))
            block.sync(run("sp"))

    def dma(self, eng, out_b, out_ap, in_b, in_ap, sem_b=None):
        if sem_b is None:
            sem_b = out_b if out_b.kind == "sb" else in_b
            assert sem_b.kind == "sb"
        return self.op(eng, lambda e: e.dma_start(out=out_ap, in_=in_ap),
                       reads=[in_b], writes=[out_b], dma=sem_b)

    def next_psum(self):
        pool = self.pspool if self.pspool is not None else self.psums
        b = pool[self.psum_rr % len(pool)]
        self.psum_rr += 1
        return b

    def barrier(self):
        snap = {}
        for e in ENGS:
            if self.cnt[e] > 0:
                snap[self.esem[e]] = self.cnt[e]
        for b in self.dbufs:
            for kind in (0, 1):
                if b.dsem[kind] is not None:
                    snap[b.dsem[kind]] = b.dcnt[kind]
        for e in ENGS:
            waits = [(s, v) for s, v in snap.items() if self.seen[e].get(s, 0) < v]
            for s, v in waits:
                self.seen[e][s] = v
            self.q[e].append((waits, None, None, 0))


class Arena:
    def __init__(self, P, words):
        self.P = P
        self.t = P.stack.enter_context(P.nc.sbuf_tensor("arena", [128, words], F32))
        self.off = 0
        self.words = words
        self.peak = 0
        self.live = []

    def mark(self):
        return self.off

    def reset(self, m):
        keep = []
        for off, b in self.live:
            if off >= m:
                if b.dsem is not None:
                    for kind in (0, 1):
                        if b.dsem[kind] is not None:
                            self.P.free_dsems[kind].append((b.dsem[kind], b.dcnt[kind]))
                    self.P.dbufs.remove(b)
                    b.dsem = None
            else:
                keep.append((off, b))
        self.live = keep
        self.off = m

    def alloc(self, name, free, dtype=F32, parts=128):
        free = list(free)
        n = int(np.prod(free))
        four = dtype in (F32, I32)
        w = n if four else (n + 1) // 2
        wal = (w + 7) // 8 * 8
        assert self.off + wal <= self.words, "arena overflow %s: %d + %d > %d" % (name, self.off, wal, self.words)
        a = self.t[0:parts, self.off:self.off + w]
        if dtype != F32:
            a = a.bitcast(dtype)
        if len(free) > 1:
            names = ["a%d" % i for i in range(len(free))]
            a = a.rearrange("p (%s) -> p %s" % (" ".join(names), " ".join(names)),
                            **{nm: v for nm, v in zip(names, free)})
        b = Buf(name, a, "sb")
        self.live.append((self.off, b))
        self.off += wal
        self.peak = max(self.peak, self.off)
        return b


QR = 384
KVR = 256
ATTN_SCALE = 192 ** -0.5
ARENA_WORDS = 48000


class Cfg:
    def __init__(self, T=4096, layers=4, stop=None, skip_ffn=False, seq=None, upto=None):
        self.seq = seq
        self.upto = upto
        self.a_every = 1
        self.T = T
        self.layers = layers
        self.stop = stop
        self.skip_ffn = skip_ffn


def build(cfg):
    T = cfg.T
    NT = T // TT
    NB = T // 128
    nc = bass.Bass("TRN2", target_bir_lowering=False)
    stack = contextlib.ExitStack()
    with stack:
        P = Prog(nc, stack)
        A = Arena(P, ARENA_WORDS)

        def dram_in(name, shape, dt=F32):
            return P.dram(name, shape, dt, kind="ExternalInput")

        x_in = dram_in("x", [T, D])
        pos_in = dram_in("positions", [1, T], I32)
        gains_in = dram_in("norm_gains", [4, 3, D])
        wg_in = dram_in("ffn_w_gate", [4, 2, D, DFF])
        wu_in = dram_in("ffn_w_up", [4, 2, D, DFF])
        wd_in = dram_in("ffn_w_down", [4, 2, DFF, D])
        owin_in = dram_in("odd_w_in_p", [2, D, 768])
        wq_in = dram_in("mla_w_q_up_p", [2, QR, 2048])
        wkv_in = dram_in("mla_w_kv_up", [2, KVR, 2048])
        owout_in = dram_in("odd_w_out", [2, D, D])
        qa_in = dram_in("mla_q_a_norm", [2, QR])
        kva_in = dram_in("mla_kv_a_norm", [2, KVR])
        qkc_in = dram_in("qk_cols", [2, 128, 6])
        qn_in = dram_in("mla_q_norm", [2, 192])
        kn_in = dram_in("mla_k_norm", [2, 192])
        ropec_in = dram_in("rope_c", [64, 2])
        ewin_in = dram_in("even_w_in", [2, D, 3232])
        ewout_in = dram_in("even_w_out", [2, D, D])
        cw_in = dram_in("even_conv_w", [2, 3, 512])
        emu_in = dram_in("even_mu_shift", [2, 1696])
        w0_in = dram_in("rwkv_w0", [2, 512])
        a0_in = dram_in("rwkv_a0", [2, 512])
        kk_in = dram_in("rwkv_k_k", [2, 512])
        ka_in = dram_in("rwkv_k_a", [2, 512])
        rk_in = dram_in("rwkv_r_k", [2, 512])
        lnw_in = dram_in("rwkv_ln_w", [2, 512])
        lnb_in = dram_in("rwkv_ln_b", [2, 512])
        wup_in = dram_in("rwkv_w_up", [2, 32, 512])
        aup_in = dram_in("rwkv_a_up", [2, 32, 512])
        gup_in = dram_in("rwkv_g_up", [2, 96, 512])
        y_out = P.dram("y", [T, D], F32, kind="ExternalOutput")

        XT = P.dram("XT", [D, T], F32)
        XTv = XT.t.rearrange("(c p) t -> p c t", p=128)
        XTb = [P.view(XT, "XT%d" % i) for i in range(NT)]
        wg_bfs = [P.dram("wg_bf%d" % i, [NM, 128, NC8, 128], BF16) for i in range(2)]
        wu_bfs = [P.dram("wu_bf%d" % i, [NM, 128, NC8, 128], BF16) for i in range(2)]
        wd_bfs = [P.dram("wd_bf%d" % i, [NC8, 128, NM, 128], BF16) for i in range(2)]
        castsems = [Buf("castsem%d" % i) for i in range(2)]
        we_bfs = [P.dram("we_bf%d" % i, [24, 128, NC8, 128], BF16) for i in range(2)]
        wl_bfs = [P.dram("wl_bf%d" % i, [128, NC8, 160], BF16) for i in range(2)]
        woc_bfs = [P.dram("woc_bf%d" % i, [NC8, 128, 4, 128], BF16) for i in range(2)]
        wor_bfs = [P.dram("wor_bf%d" % i, [NC8, 64, 8, 128], BF16) for i in range(2)]
        ecastsem = [Buf("ecastsem%d" % i) for i in range(2)]
        OTd = P.dram("OTd", [8, 128, T], BF16)
        OTb = [P.view(OTd, "OT%d" % i) for i in range(NT)]

        for i in range(8):
            P.psums.append(P.ps("ps%d" % i, [128, 512], F32))

        ident = A.alloc("ident", [128], F32)
        ones_bf = A.alloc("ones_bf", [128], BF16)
        ident_bf = A.alloc("ident_bf", [128], BF16)
        ones_f = A.alloc("ones_f", [128], F32)
        tri_bf = A.alloc("tri_bf", [128], BF16)
        gains_sb = A.alloc("gains_sb", [12, NC8], F32)
        epsb = A.alloc("epsb", [1], F32)
        ropec = A.alloc("ropec", [2], F32, parts=64)
        ropeD = P.dram("ropeD", [2, 64, T], F32)

        P.op("pool", lambda e: e.memset(ones_bf[:], 1.0), writes=[ones_bf])
        P.op("pool", lambda e: e.memset(ones_f[:], 1.0), writes=[ones_f])
        P.op("pool", lambda e: e.memset(epsb[:], EPS), writes=[epsb])
        P.op("pool", lambda e: e.memset(ident[:], 0.0), writes=[ident])
        P.op("pool", lambda e: e.affine_select(out=ident[:], in_=ident[:], pattern=[[-1, 128]],
                                               compare_op=ALU.not_equal, fill=1.0, base=0,
                                               channel_multiplier=1),
             reads=[ident], writes=[ident])
        P.op("pool", lambda e: e.tensor_copy(out=ident_bf[:], in_=ident[:]), reads=[ident], writes=[ident_bf])
        P.op("pool", lambda e: e.memset(tri_bf[:], 1.0), writes=[tri_bf])
        P.op("pool", lambda e: e.affine_select(out=tri_bf[:], in_=tri_bf[:], pattern=[[1, 128]],
                                               compare_op=ALU.is_ge, fill=0.0, base=0,
                                               channel_multiplier=-1),
             reads=[tri_bf], writes=[tri_bf])
        P.dma("sp", gains_sb, gains_sb[:], gains_in, gains_in.t.rearrange("l s (c p) -> p (l s) c", p=128))
        P.dma("sp", ropec, ropec[:], ropec_in, ropec_in.t)

        def rope_tables():
            m = A.mark()
            cos2 = A.alloc("cos2t", [T], F32, parts=64)
            ssin2 = A.alloc("ssin2t", [T], F32, parts=64)
            posi = A.alloc("posi", [T], I32, parts=64)
            ang = A.alloc("ang", [T], F32, parts=64)
            kf = A.alloc("kf", [T], F32, parts=64)
            ki = A.alloc("ki", [T], I32, parts=64)
            rr = A.alloc("rr", [T], F32, parts=64)
            msk = A.alloc("msk", [T], F32, parts=64)
            P.dma("sp", posi, posi[:], pos_in, pos_in.t.partition_broadcast(64))
            P.op("dve", lambda e: e.tensor_copy(out=ang[:], in_=posi[:]), reads=[posi], writes=[ang])
            P.op("dve", lambda e: e.tensor_scalar(out=ang[:], in0=ang[:], scalar1=ropec[:, 0:1], scalar2=None,
                                                  op0=ALU.mult), reads=[ang, ropec], writes=[ang])
            TWO_PI = 2.0 * np.pi
            c1 = float(np.float32(6.28125))
            c2 = float(np.float32(TWO_PI - 6.28125))
            c3 = float(TWO_PI - c1 - c2)
            P.op("dve", lambda e: e.tensor_scalar(out=kf[:], in0=ang[:], scalar1=float(1.0 / TWO_PI), scalar2=None,
                                                  op0=ALU.mult), reads=[ang], writes=[kf])
            P.op("dve", lambda e: e.tensor_copy(out=ki[:], in_=kf[:]), reads=[kf], writes=[ki])
            P.op("dve", lambda e: e.tensor_copy(out=kf[:], in_=ki[:]), reads=[ki], writes=[kf])
            for cc in (c1, c2, c3):
                P.op("dve", lambda e, cc=cc: e.scalar_tensor_tensor(out=ang[:], in0=kf[:], scalar=-cc, in1=ang[:],
                                                                   op0=ALU.mult, op1=ALU.add),
                     reads=[kf, ang], writes=[ang])

            def wrap(dst, src, shift):
                P.op("dve", lambda e: e.tensor_scalar(out=dst[:], in0=src[:], scalar1=float(shift), scalar2=None,
                                                      op0=ALU.add), reads=[src], writes=[dst])
                P.op("dve", lambda e: e.tensor_single_scalar(out=msk[:], in_=dst[:], scalar=float(np.pi), op=ALU.is_gt),
                     reads=[dst], writes=[msk])
                P.op("dve", lambda e: e.scalar_tensor_tensor(out=dst[:], in0=msk[:], scalar=-TWO_PI, in1=dst[:],
                                                             op0=ALU.mult, op1=ALU.add), reads=[msk, dst], writes=[dst])
                P.op("dve", lambda e: e.tensor_single_scalar(out=msk[:], in_=dst[:], scalar=float(-np.pi), op=ALU.is_lt),
                     reads=[dst], writes=[msk])
                P.op("dve", lambda e: e.scalar_tensor_tensor(out=dst[:], in0=msk[:], scalar=TWO_PI, in1=dst[:],
                                                             op0=ALU.mult, op1=ALU.add), reads=[msk, dst], writes=[dst])
                P.op("dve", lambda e: e.tensor_scalar(out=dst[:], in0=dst[:], scalar1=float(np.pi), scalar2=float(-np.pi),
                                                      op0=ALU.min, op1=ALU.max), reads=[dst], writes=[dst])

            wrap(rr, ang, 0.0)
            P.op("act", lambda e: e.activation(out=ssin2[:], in_=rr[:], func=AF.Sin), reads=[rr], writes=[ssin2])
            P.op("dve", lambda e: e.tensor_scalar(out=ssin2[:], in0=ssin2[:], scalar1=ropec[:, 1:2], scalar2=None,
                                                  op0=ALU.mult), reads=[ssin2, ropec], writes=[ssin2])
            wrap(kf, rr, np.pi / 2)
            P.op("act", lambda e: e.activation(out=cos2[:], in_=kf[:], func=AF.Sin), reads=[kf], writes=[cos2])
            P.dma("sp", ropeD, ropeD.t[0], cos2, cos2[:])
            P.dma("sp", ropeD, ropeD.t[1], ssin2, ssin2[:])
            P.barrier()
            A.reset(m)


        def transpose_in():
            m = A.mark()
            xin_tiles = [A.alloc("xin%d" % i, [D], F32) for i in range(2)]
            xtr_tiles = [A.alloc("xtr%d" % i, [NC8, 128], F32) for i in range(2)]
            for b in range(NB):
                xi = xin_tiles[b % 2]
                xo = xtr_tiles[b % 2]
                P.dma("sp", xi, xi[:], x_in, x_in.t[b * 128:(b + 1) * 128, :])
                for half in range(2):
                    ps = P.next_psum()
                    for j in range(4):
                        c = half * 4 + j
                        P.op("pe", lambda e, ps=ps, xi=xi, c=c, j=j: e.transpose(
                            out=ps[:, j * 128:(j + 1) * 128], in_=xi[:, c * 128:(c + 1) * 128], identity=ident[:]),
                            reads=[xi, ident], writes=[ps])
                    if half:
                        P.op("act", lambda e, ps=ps, xo=xo, half=half: e.copy(
                            out=xo[:, half * 4:(half + 1) * 4, :], in_=ps[:].rearrange("p (j t) -> p j t", j=4)),
                            reads=[ps], writes=[xo])
                    else:
                        P.op("dve", lambda e, ps=ps, xo=xo, half=half: e.tensor_copy(
                            out=xo[:, half * 4:(half + 1) * 4, :], in_=ps[:].rearrange("p (j t) -> p j t", j=4)),
                            reads=[ps], writes=[xo])
                P.dma("pool", XTb[b // 4], XTv[:, :, b * 128:(b + 1) * 128], xo, xo[:])
            P.barrier()
            A.reset(m)


        def rmsnorm_tile(xt, h, gidx, sq_tile, rstd_t, w=TT):
            P.op("act", lambda e: e.activation(out=sq_tile[:], in_=xt[:], func=AF.Square),
                 reads=[xt], writes=[sq_tile])
            ps = P.next_psum()
            for c in range(NC8):
                P.op("pe", lambda e, c=c: e.matmul(ps[:, :w], lhsT=ones_bf[:], rhs=sq_tile[:, c, :],
                                                  start=(c == 0), stop=(c == NC8 - 1)),
                     reads=[ones_bf, sq_tile], writes=[ps])
            P.op("act", lambda e: e.activation(out=rstd_t[:], in_=ps[:, :w], func=AF.Sqrt,
                                               bias=epsb[:], scale=1.0 / D),
                 reads=[ps, epsb], writes=[rstd_t])
            P.op("dve", lambda e: e.reciprocal(out=rstd_t[:], in_=rstd_t[:]), reads=[rstd_t], writes=[rstd_t])
            for c in range(NC8):
                P.op("dve", lambda e, c=c: e.scalar_tensor_tensor(
                    out=h[:, c, :], in0=xt[:, c, :], scalar=gains_sb[:, gidx, c:c + 1], in1=rstd_t[:],
                    op0=ALU.mult, op1=ALU.mult), reads=[xt, gains_sb, rstd_t], writes=[h])

        def rstd_from_ss(dst, ps, n, parts=128):
            P.op("act", lambda e: e.activation(out=dst[:parts], in_=ps[:parts, :TT], func=AF.Ln,
                                               bias=epsb[:parts], scale=1.0 / n),
                 reads=[ps, epsb], writes=[dst])
            P.op("act", lambda e: e.activation(out=dst[:parts], in_=dst[:parts], func=AF.Exp, scale=-0.5),
                 reads=[dst], writes=[dst])

        def cast_weights(layer, which):
            par = (layer * 2 + which) % 2
            wg_bf, wu_bf, wd_bf, castsem = wg_bfs[par], wu_bfs[par], wd_bfs[par], castsems[par]
            for m in range(NM):
                P.op("pool", lambda e, m=m: e.dma_start(
                    out=wg_bf.t[m], in_=wg_in.t[layer, which].rearrange("(c p) f -> p c f", p=128)[:, :, m * 128:(m + 1) * 128]),
                    reads=[wg_in], writes=[wg_bf], dma=castsem)
                P.op("pool", lambda e, m=m: e.dma_start(
                    out=wu_bf.t[m], in_=wu_in.t[layer, which].rearrange("(c p) f -> p c f", p=128)[:, :, m * 128:(m + 1) * 128]),
                    reads=[wu_in], writes=[wu_bf], dma=castsem)
            for o in range(NC8):
                P.op("pool", lambda e, o=o: e.dma_start(
                    out=wd_bf.t[o], in_=wd_in.t[layer, which].rearrange("(m p) f -> p m f", p=128)[:, :, o * 128:(o + 1) * 128]),
                    reads=[wd_in], writes=[wd_bf], dma=castsem)

        TF = 1024

        def ffn_phase(layer, which):
            m0 = A.mark()
            P.pspool = None
            NTF = T // TF
            NH2 = TF // 512
            xt_tiles = [A.alloc("fxt%d" % i, [NC8, TF], F32) for i in range(2)]
            sqr = [A.alloc("fsq%d" % i, [TF], BF16) for i in range(2)]
            h_tiles = [A.alloc("fh0", [NC8, TF], BF16)] * 2
            hid = A.alloc("hid", [NM, TF], BF16)
            rstd_t = A.alloc("frstd", [TF], F32)
            sg_tiles = [A.alloc("sg%d" % i, [TF], F32) for i in range(2)]
            NWB = 4
            wgu_tiles = [A.alloc("wgu%d" % i, [2, NC8, 128], BF16) for i in range(NWB)]
            wdn_tiles = [A.alloc("wdn%d" % i, [NM, 128], BF16) for i in range(2)]
            par = (layer * 2 + which) % 2
            wg_bf, wu_bf, wd_bf = wg_bfs[par], wu_bfs[par], wd_bfs[par]
            gidx = layer * 3 + (0 if which == 0 else 2)
            wcount = 0
            dcount = 0
            def f_load(tf):
                xt = xt_tiles[tf % 2]
                for hf in range(NH2):
                    tt = tf * NH2 + hf
                    P.dma("sp", xt, xt[:, :, hf * 512:(hf + 1) * 512], XTb[tt], XTv[:, :, tt * 512:(tt + 1) * 512])

            def f_norm(tf):
                xt = xt_tiles[tf % 2]
                h = h_tiles[tf % 2]
                pss = [P.next_psum() for _ in range(NH2)]
                for c in range(NC8):
                    sq = sqr[c % 2]
                    P.op("act", lambda e, c=c, sq=sq: e.activation(out=sq[:], in_=xt[:, c, :], func=AF.Square),
                         reads=[xt], writes=[sq])
                    for hf in range(NH2):
                        P.op("pe", lambda e, c=c, hf=hf, sq=sq: e.matmul(pss[hf][:, :512], lhsT=ones_bf[:], rhs=sq[:, hf * 512:(hf + 1) * 512],
                                                                        start=(c == 0), stop=(c == NC8 - 1)),
                             reads=[ones_bf, sq], writes=[pss[hf]])
                for hf in range(NH2):
                    P.op("act", lambda e, hf=hf: e.activation(out=rstd_t[:, hf * 512:(hf + 1) * 512], in_=pss[hf][:, :512], func=AF.Sqrt,
                                                              bias=epsb[:], scale=1.0 / D), reads=[pss[hf], epsb], writes=[rstd_t])
                P.op("dve", lambda e: e.reciprocal(out=rstd_t[:], in_=rstd_t[:]), reads=[rstd_t], writes=[rstd_t])
                for c in range(NC8):
                    P.op("dve", lambda e, c=c: e.scalar_tensor_tensor(
                        out=h[:, c, :], in0=xt[:, c, :], scalar=gains_sb[:, gidx, c:c + 1], in1=rstd_t[:],
                        op0=ALU.mult, op1=ALU.mult), reads=[xt, gains_sb, rstd_t], writes=[h])

            f_load(0)
            f_norm(0)
            for tf in range(NTF):
                xt = xt_tiles[tf % 2]
                h = h_tiles[tf % 2]
                if tf + 1 < NTF:
                    f_load(tf + 1)
                for m in range(NM):
                    w = wgu_tiles[wcount % NWB]
                    wcount += 1
                    P.dma("sp", w, w[:, 0], wg_bf, wg_bf.t[m])
                    P.dma("sp", w, w[:, 1], wu_bf, wu_bf.t[m])
                    psg = [P.next_psum() for _ in range(NH2)]
                    psu = [P.next_psum() for _ in range(NH2)]
                    for kind, pst in ((0, psg), (1, psu)):
                        for c in range(NC8):
                            for hf in range(NH2):
                                P.op("pe", lambda e, c=c, w=w, hf=hf, kind=kind, pst=pst, h=h: e.matmul(
                                    pst[hf][:, :512], lhsT=w[:, kind, c, :], rhs=h[:, c, hf * 512:(hf + 1) * 512],
                                    start=(c == 0), stop=(c == NC8 - 1)), reads=[w, h], writes=[pst[hf]])
                    sg = sg_tiles[m % 2]
                    for hf in range(NH2):
                        P.op("act", lambda e, sg=sg, hf=hf, psg=psg: e.activation(out=sg[:, hf * 512:(hf + 1) * 512], in_=psg[hf][:, :512],
                                                                                 func=AF.Silu), reads=[psg[hf]], writes=[sg])
                    for hf in range(NH2):
                        P.op("dve", lambda e, sg=sg, hf=hf, psu=psu, m=m: e.tensor_tensor(
                            out=hid[:, m, hf * 512:(hf + 1) * 512], in0=psu[hf][:, :512], in1=sg[:, hf * 512:(hf + 1) * 512], op=ALU.mult),
                            reads=[psu[hf], sg], writes=[hid])
                if tf + 1 < NTF:
                    f_norm(tf + 1)
                for o in range(NC8):
                    wd = wdn_tiles[dcount % 2]
                    dcount += 1
                    P.dma("sp", wd, wd[:], wd_bf, wd_bf.t[o])
                    pso = [P.next_psum() for _ in range(NH2)]
                    for m in range(NM):
                        for hf in range(NH2):
                            P.op("pe", lambda e, m=m, wd=wd, hf=hf, pso=pso: e.matmul(
                                pso[hf][:, :512], lhsT=wd[:, m, :], rhs=hid[:, m, hf * 512:(hf + 1) * 512],
                                start=(m == 0), stop=(m == NM - 1)), reads=[wd, hid], writes=[pso[hf]])
                    for hf in range(NH2):
                        P.op("dve", lambda e, o=o, hf=hf, pso=pso, xt=xt: e.scalar_tensor_tensor(
                            out=xt[:, o, hf * 512:(hf + 1) * 512], in0=pso[hf][:, :512], scalar=0.5, in1=xt[:, o, hf * 512:(hf + 1) * 512],
                            op0=ALU.mult, op1=ALU.add), reads=[pso[hf], xt], writes=[xt])
                for hf in range(NH2):
                    tt = tf * NH2 + hf
                    P.dma("pool", XTb[tt], XTv[:, :, tt * 512:(tt + 1) * 512], xt, xt[:, :, hf * 512:(hf + 1) * 512])
            P.barrier()
            A.reset(m0)

        def mla_layer(layer):
            j = layer // 2
            gidx = layer * 3 + 1
            m0 = A.mark()
            P.pspool = P.psums[0:4]
            cos2 = A.alloc("cos2", [T], F32, parts=64)
            ssin2 = A.alloc("ssin2", [T], F32, parts=64)
            P.dma("sp", cos2, cos2[:], ropeD, ropeD.t[0])
            P.dma("sp", ssin2, ssin2[:], ropeD, ropeD.t[1])
            cqn = A.alloc("cqn", [3, T], BF16)
            ckvn = A.alloc("ckvn", [2, T], BF16)
            kper = A.alloc("kper", [T], BF16, parts=64)
            kpsq = A.alloc("kpsq", [T], BF16, parts=64)
            wq = A.alloc("wq", [3, 2048], BF16)
            wkv = A.alloc("wkv", [2, 2048], BF16)
            cols = A.alloc("mcols", [16], F32)
            grow = A.alloc("grow", [2, 192], F32, parts=1)
            gmax = A.alloc("gmax", [4], F32, parts=1)
            P.dma("pool", wq, wq[:], wq_in, wq_in.t[j].rearrange("(c p) f -> p c f", p=128))
            P.dma("pool", wkv, wkv[:], wkv_in, wkv_in.t[j].rearrange("(c p) f -> p c f", p=128))
            P.dma("sp", cols, cols[:, 0:3], qa_in, qa_in.t[j].rearrange("(c p) -> p c", p=128))
            P.dma("sp", cols, cols[:, 3:5], kva_in, kva_in.t[j].rearrange("(c p) -> p c", p=128))
            P.dma("sp", cols, cols[:, 5:11], qkc_in, qkc_in.t[j])
            P.dma("sp", grow, grow[:, 0, :], qn_in, qn_in.t[j:j + 1, :])
            P.dma("sp", grow, grow[:, 1, :], kn_in, kn_in.t[j:j + 1, :])
            P.op("dve", lambda e: e.tensor_reduce(out=gmax[:, 0:2], in_=grow[:], axis=mybir.AxisListType.X,
                                                  op=ALU.max, apply_absolute_value=True),
                 reads=[grow], writes=[gmax])
            P.op("dve", lambda e: e.scalar_tensor_tensor(out=gmax[:, 2:3], in0=gmax[:, 0:1], scalar=-ATTN_SCALE * 192.0,
                                                         in1=gmax[:, 1:2], op0=ALU.mult, op1=ALU.mult),
                 reads=[gmax], writes=[gmax])
            P.op("dve", lambda e: e.tensor_copy(out=gmax[:, 3:4], in_=gmax[:, 2:3]), reads=[gmax], writes=[gmax])
            psb = P.next_psum()
            P.op("pe", lambda e: e.matmul(psb[:, 0:2], lhsT=ones_f[0:1, :], rhs=gmax[:, 2:4], start=True, stop=True),
                 reads=[ones_f, gmax], writes=[psb])
            P.op("dve", lambda e: e.tensor_copy(out=cols[:, 12:13], in_=psb[:, 0:1]), reads=[psb], writes=[cols])

            if cfg.upto == "m0":
                P.barrier()
                A.reset(m0)
                return
            m1 = A.mark()
            win = A.alloc("win", [NC8, 768], BF16)
            P.dma("pool", win, win[:], owin_in, owin_in.t[j].rearrange("(c p) f -> p c f", p=128))
            xt_tiles = [A.alloc("xt%d" % i, [NC8, TT], F32) for i in range(1)] * 2
            sq_tile = A.alloc("sq", [NC8, TT], BF16)
            h_tiles = [A.alloc("h%d" % i, [NC8, TT], BF16) for i in range(1)] * 2
            rstd_t = A.alloc("rstd", [TT], F32)
            c32 = A.alloc("c32", [5, TT], F32)
            csq = A.alloc("csq", [5, TT], BF16)
            rl = [A.alloc("rl%d" % i, [TT], F32) for i in range(2)]
            tA = A.alloc("tA", [TT], F32, parts=64)
            tB = A.alloc("tB", [TT], F32, parts=64)
            for tt in range(NT):
                ts = slice(tt * TT, (tt + 1) * TT)
                xt = xt_tiles[tt % 2]
                h = h_tiles[tt % 2]
                P.dma("sp", xt, xt[:], XTb[tt], XTv[:, :, ts])
                rmsnorm_tile(xt, h, gidx, sq_tile, rstd_t)
                for m in range(5):
                    if cfg.upto == "m1a0":
                        continue
                    ps = P.next_psum()
                    for c in range(NC8):
                        P.op("pe", lambda e, c=c, m=m, ps=ps, h=h: e.matmul(
                            ps[:, :TT], lhsT=win[:, c, m * 128:(m + 1) * 128], rhs=h[:, c, :],
                            start=(c == 0), stop=(c == NC8 - 1)), reads=[win, h], writes=[ps])
                    P.op("act", lambda e, m=m, ps=ps: e.activation(out=csq[:, m, :], in_=ps[:, :TT], func=AF.Square),
                         reads=[ps], writes=[csq])
                    if cfg.upto != "m1a1":
                        P.op("dve", lambda e, m=m, ps=ps: e.tensor_copy(out=c32[:, m, :], in_=ps[:, :TT]),
                             reads=[ps], writes=[c32])
                if cfg.upto in ("m1a", "m1a0", "m1a1"):
                    continue
                for (lo, hi, n, dst, cbase, ri) in ((0, 3, QR, cqn, 0, 0), (3, 5, KVR, ckvn, 3, 1)):
                    ps = P.next_psum()
                    for m in range(lo, hi):
                        P.op("pe", lambda e, m=m, ps=ps, lo=lo, hi=hi: e.matmul(
                            ps[:, :TT], lhsT=ones_bf[:], rhs=csq[:, m, :], start=(m == lo), stop=(m == hi - 1)),
                            reads=[ones_bf, csq], writes=[ps])
                    rstd_from_ss(rl[ri], ps, n)
                    for m in range(lo, hi):
                        P.op("dve", lambda e, m=m, lo=lo, dst=dst, cbase=cbase, ri=ri: e.scalar_tensor_tensor(
                            out=dst[:, m - lo, ts], in0=c32[:, m, :], scalar=cols[:, cbase + m - lo:cbase + m - lo + 1],
                            in1=rl[ri][:], op0=ALU.mult, op1=ALU.mult), reads=[c32, cols, rl[ri]], writes=[dst])
                if cfg.upto == "m1b":
                    continue
                pk = P.next_psum()
                pw = P.next_psum()
                for c in range(NC8):
                    P.op("pe", lambda e, c=c, pk=pk, h=h: e.matmul(
                        pk[:64, :TT], lhsT=win[:, c, 640:704], rhs=h[:, c, :], start=(c == 0), stop=(c == NC8 - 1)),
                        reads=[win, h], writes=[pk])
                for c in range(NC8):
                    P.op("pe", lambda e, c=c, pw=pw, h=h: e.matmul(
                        pw[:64, :TT], lhsT=win[:, c, 704:768], rhs=h[:, c, :], start=(c == 0), stop=(c == NC8 - 1)),
                        reads=[win, h], writes=[pw])
                P.op("act", lambda e, pk=pk: e.activation(out=kpsq[:, ts], in_=pk[:64, :TT], func=AF.Square),
                     reads=[pk], writes=[kpsq])
                P.op("dve", lambda e, pk=pk: e.scalar_tensor_tensor(
                    out=tA[:], in0=pk[:64, :TT], scalar=cols[:64, 9:10], in1=cos2[:, ts], op0=ALU.mult, op1=ALU.mult),
                    reads=[pk, cols, cos2], writes=[tA])
                P.op("dve", lambda e, pw=pw: e.scalar_tensor_tensor(
                    out=tB[:], in0=pw[:64, :TT], scalar=cols[:64, 10:11], in1=ssin2[:, ts], op0=ALU.mult, op1=ALU.mult),
                    reads=[pw, cols, ssin2], writes=[tB])
                P.op("dve", lambda e: e.tensor_tensor(out=kper[:, ts], in0=tA[:], in1=tB[:], op=ALU.add),
                     reads=[tA, tB], writes=[kper])
            P.barrier()
            A.reset(m1)
            if cfg.upto in ("m1", "m1a", "m1b", "m1a0", "m1a1"):
                A.reset(m0)
                return

            QTn = A.alloc("QTn", [T], BF16)
            QTr = A.alloc("QTr", [T], BF16, parts=64)
            KTn = A.alloc("KTn", [T], BF16)
            KTr = A.alloc("KTr", [T], BF16, parts=64)
            Vh = A.alloc("Vh", [NB, 128], BF16)
            sqn = A.alloc("sqn", [TT], BF16)
            sqr = A.alloc("sqr", [TT], BF16, parts=64)
            rq = A.alloc("rq", [TT], F32)
            rk = A.alloc("rk", [TT], F32)
            t1 = A.alloc("t1", [TT], F32, parts=64)
            t2 = A.alloc("t2", [TT], F32, parts=64)
            pts = [A.alloc("pt%d" % i, [TT], BF16) for i in range(4)]
            rden = A.alloc("rden", [TT], F32)
            obf = [A.alloc("obf%d" % i, [TT], BF16) for i in range(2)]
            oacc = [P.psums[4], P.psums[5]]
            dacc = [P.psums[6], P.psums[7]]
            ptc = [0]
            for hd in range(8):
                qb = hd * 256
                kb = hd * 256
                for tt in range(NT):
                    ts = slice(tt * TT, (tt + 1) * TT)
                    pqn = P.next_psum()
                    for c in range(3):
                        P.op("pe", lambda e, c=c, pqn=pqn, ts=ts, qb=qb: e.matmul(
                            pqn[:, :TT], lhsT=wq[:, c, qb:qb + 128], rhs=cqn[:, c, ts], start=(c == 0), stop=(c == 2)),
                            reads=[wq, cqn], writes=[pqn])
                    pqr = P.next_psum()
                    for c in range(3):
                        P.op("pe", lambda e, c=c, pqr=pqr, ts=ts, qb=qb: e.matmul(
                            pqr[:64, :TT], lhsT=wq[:, c, qb + 128:qb + 192], rhs=cqn[:, c, ts], start=(c == 0), stop=(c == 2)),
                            reads=[wq, cqn], writes=[pqr])
                    pqs = P.next_psum()
                    for c in range(3):
                        P.op("pe", lambda e, c=c, pqs=pqs, ts=ts, qb=qb: e.matmul(
                            pqs[:64, :TT], lhsT=wq[:, c, qb + 192:qb + 256], rhs=cqn[:, c, ts], start=(c == 0), stop=(c == 2)),
                            reads=[wq, cqn], writes=[pqs])
                    P.op("act", lambda e, pqn=pqn: e.activation(out=sqn[:], in_=pqn[:, :TT], func=AF.Square),
                         reads=[pqn], writes=[sqn])
                    P.op("act", lambda e, pqr=pqr: e.activation(out=sqr[:], in_=pqr[:64, :TT], func=AF.Square),
                         reads=[pqr], writes=[sqr])
                    pss = P.next_psum()
                    P.op("pe", lambda e, pss=pss: e.matmul(pss[:, :TT], lhsT=ones_bf[:], rhs=sqn[:], start=True, stop=False),
                         reads=[ones_bf, sqn], writes=[pss])
                    P.op("pe", lambda e, pss=pss: e.matmul(pss[:, :TT], lhsT=ones_bf[:64, :], rhs=sqr[:], start=False, stop=True),
                         reads=[ones_bf, sqr], writes=[pss])
                    rstd_from_ss(rq, pss, 192)
                    P.op("dve", lambda e, pqn=pqn, ts=ts: e.scalar_tensor_tensor(
                        out=QTn[:, ts], in0=pqn[:, :TT], scalar=cols[:, 5:6], in1=rq[:], op0=ALU.mult, op1=ALU.mult),
                        reads=[pqn, cols, rq], writes=[QTn])
                    P.op("dve", lambda e, pqr=pqr, ts=ts: e.scalar_tensor_tensor(
                        out=t1[:], in0=pqr[:64, :TT], scalar=cols[:64, 6:7], in1=cos2[:, ts], op0=ALU.mult, op1=ALU.mult),
                        reads=[pqr, cols, cos2], writes=[t1])
                    P.op("dve", lambda e, pqs=pqs, ts=ts: e.scalar_tensor_tensor(
                        out=t2[:], in0=pqs[:64, :TT], scalar=cols[:64, 7:8], in1=ssin2[:, ts], op0=ALU.mult, op1=ALU.mult),
                        reads=[pqs, cols, ssin2], writes=[t2])
                    P.op("dve", lambda e: e.tensor_tensor(out=t1[:], in0=t1[:], in1=t2[:], op=ALU.add),
                         reads=[t1, t2], writes=[t1])
                    P.op("dve", lambda e, ts=ts: e.tensor_tensor(out=QTr[:, ts], in0=t1[:], in1=rq[:64], op=ALU.mult),
                         reads=[t1, rq], writes=[QTr])
                    pkn = P.next_psum()
                    for c in range(2):
                        P.op("pe", lambda e, c=c, pkn=pkn, ts=ts, kb=kb: e.matmul(
                            pkn[:, :TT], lhsT=wkv[:, c, kb:kb + 128], rhs=ckvn[:, c, ts], start=(c == 0), stop=(c == 1)),
                            reads=[wkv, ckvn], writes=[pkn])
                    P.op("act", lambda e, pkn=pkn: e.activation(out=sqn[:], in_=pkn[:, :TT], func=AF.Square),
                         reads=[pkn], writes=[sqn])
                    pss2 = P.next_psum()
                    P.op("pe", lambda e, pss2=pss2: e.matmul(pss2[:, :TT], lhsT=ones_bf[:], rhs=sqn[:], start=True, stop=False),
                         reads=[ones_bf, sqn], writes=[pss2])
                    P.op("pe", lambda e, pss2=pss2, ts=ts: e.matmul(pss2[:, :TT], lhsT=ones_bf[:64, :], rhs=kpsq[:, ts],
                                                                   start=False, stop=True),
                         reads=[ones_bf, kpsq], writes=[pss2])
                    rstd_from_ss(rk, pss2, 192)
                    P.op("dve", lambda e, pkn=pkn, ts=ts: e.scalar_tensor_tensor(
                        out=KTn[:, ts], in0=pkn[:, :TT], scalar=cols[:, 8:9], in1=rk[:], op0=ALU.mult, op1=ALU.mult),
                        reads=[pkn, cols, rk], writes=[KTn])
                    P.op("dve", lambda e, ts=ts: e.tensor_tensor(out=KTr[:, ts], in0=kper[:, ts], in1=rk[:64], op=ALU.mult),
                         reads=[kper, rk], writes=[KTr])
                    pv = P.next_psum()
                    for blk in range(4):
                        tb = slice(tt * TT + blk * 128, tt * TT + (blk + 1) * 128)
                        for c in range(2):
                            P.op("pe", lambda e, c=c, pv=pv, tb=tb, blk=blk, kb=kb: e.matmul(
                                pv[:, blk * 128:(blk + 1) * 128], lhsT=ckvn[:, c, tb], rhs=wkv[:, c, kb + 128:kb + 256],
                                start=(c == 0), stop=(c == 1)), reads=[ckvn, wkv], writes=[pv])
                    P.op("act", lambda e, pv=pv, tt=tt: e.copy(
                        out=Vh[:, tt * 4:(tt + 1) * 4, :], in_=pv[:].rearrange("p (b d) -> p b d", b=4)),
                        reads=[pv], writes=[Vh])

                if cfg.upto == "m2p":
                    continue
                units = [(i, jb) for i in range(NT) for jb in range(4 * i + 4)]

                def emit_S(u):
                    i, jb = u
                    q0 = max(i * TT, jb * 128)
                    n = (i + 1) * TT - q0
                    ps = P.next_psum()
                    ks = slice(jb * 128, (jb + 1) * 128)
                    qs = slice(q0, q0 + n)
                    P.op("pe", lambda e: e.matmul(ps[:, :n], lhsT=KTn[:, ks], rhs=QTn[:, qs], start=True, stop=False),
                         reads=[KTn, QTn], writes=[ps])
                    P.op("pe", lambda e: e.matmul(ps[:, :n], lhsT=KTr[:, ks], rhs=QTr[:, qs], start=False, stop=True),
                         reads=[KTr, QTr], writes=[ps])
                    pt = pts[ptc[0] % 4]
                    ptc[0] += 1
                    P.op("act", lambda e: e.activation(out=pt[:, :n], in_=ps[:, :n], func=AF.Exp,
                                                       bias=cols[:, 12:13], scale=ATTN_SCALE),
                         reads=[ps, cols], writes=[pt])
                    if jb >= 4 * i:
                        P.op("pool", lambda e: e.tensor_tensor(out=pt[:, 0:128], in0=pt[:, 0:128], in1=tri_bf[:], op=ALU.mult),
                             reads=[pt, tri_bf], writes=[pt])
                    return (i, jb, pt, q0 - i * TT, n)

                def emit_PV(s):
                    i, jb, pt, c0, n = s
                    po = oacc[i % 2]
                    pd = dacc[i % 2]
                    last = (jb == 4 * i + 3)
                    P.op("pe", lambda e: e.matmul(po[:, c0:c0 + n], lhsT=Vh[:, jb, :], rhs=pt[:, :n],
                                                  start=(jb == 0), stop=last), reads=[Vh, pt], writes=[po])
                    P.op("pe", lambda e: e.matmul(pd[:, c0:c0 + n], lhsT=ones_bf[:], rhs=pt[:, :n],
                                                  start=(jb == 0), stop=last), reads=[ones_bf, pt], writes=[pd])
                    if last:
                        P.op("act", lambda e: e.activation(out=rden[:], in_=pd[:, :TT], func=AF.Ln),
                             reads=[pd], writes=[rden])
                        P.op("act", lambda e: e.activation(out=rden[:], in_=rden[:], func=AF.Exp, scale=-1.0),
                             reads=[rden], writes=[rden])
                        ob = obf[i % 2]
                        P.op("dve", lambda e: e.tensor_tensor(out=ob[:], in0=po[:, :TT], in1=rden[:], op=ALU.mult),
                             reads=[po, rden], writes=[ob])
                        P.dma("pool", OTb[i], OTd.t[hd, :, i * TT:(i + 1) * TT], ob, ob[:])

                pend = []
                for u in units:
                    pend.append(emit_S(u))
                    if len(pend) > 2:
                        emit_PV(pend.pop(0))
                while pend:
                    emit_PV(pend.pop(0))
            P.barrier()
            A.reset(m1)

            if cfg.upto in ("m2p", "m2"):
                A.reset(m0)
                return
            P.pspool = None
            wo = A.alloc("wo", [8, D], BF16)
            P.dma("pool", wo, wo[:], owout_in, owout_in.t[j].rearrange("(h p) f -> p h f", p=128))
            xt_tiles = [A.alloc("xt%d" % i, [NC8, TT], F32) for i in range(2)]
            ot_tiles = [A.alloc("ot%d" % i, [8, TT], BF16) for i in range(2)]
            for tt in range(NT):
                ts = slice(tt * TT, (tt + 1) * TT)
                xt = xt_tiles[tt % 2]
                ot = ot_tiles[tt % 2]
                P.dma("sp", xt, xt[:], XTb[tt], XTv[:, :, ts])
                P.dma("sp", ot, ot[:], OTb[tt], OTd.t.rearrange("h p t -> p h t")[:, :, ts])
                for o in range(NC8):
                    ps = P.next_psum()
                    for hh in range(8):
                        P.op("pe", lambda e, hh=hh, o=o, ps=ps, ot=ot: e.matmul(
                            ps[:, :TT], lhsT=wo[:, hh, o * 128:(o + 1) * 128], rhs=ot[:, hh, :],
                            start=(hh == 0), stop=(hh == 7)), reads=[wo, ot], writes=[ps])
                    P.op("dve", lambda e, o=o, ps=ps, xt=xt: e.tensor_tensor(
                        out=xt[:, o, :], in0=ps[:, :TT], in1=xt[:, o, :], op=ALU.add), reads=[ps, xt], writes=[xt])
                P.dma("pool", XTb[tt], XTv[:, :, ts], xt, xt[:])
            P.barrier()
            A.reset(m0)

        TE = 128
        SDT = BF16
        NCHT = TE // 64
        HG = 4
        LWS = -float(np.exp(-0.5))

        def cast_even(i):
            for jj in range(24):
                P.op("pool", lambda e, jj=jj: e.dma_start(
                    out=we_bfs[i].t[jj], in_=ewin_in.t[i].rearrange("(c p) f -> p c f", p=128)[:, :, jj * 128:(jj + 1) * 128]),
                    reads=[ewin_in], writes=[we_bfs[i]], dma=ecastsem[i])
            P.op("pool", lambda e: e.dma_start(
                out=wl_bfs[i].t, in_=ewin_in.t[i].rearrange("(c p) f -> p c f", p=128)[:, :, 3072:3232]),
                reads=[ewin_in], writes=[wl_bfs[i]], dma=ecastsem[i])
            for o in range(NC8):
                P.op("pool", lambda e, o=o: e.dma_start(
                    out=woc_bfs[i].t[o], in_=ewout_in.t[i, 0:512, :].rearrange("(c p) f -> p c f", p=128)[:, :, o * 128:(o + 1) * 128]),
                    reads=[ewout_in], writes=[woc_bfs[i]], dma=ecastsem[i])
                P.op("pool", lambda e, o=o: e.dma_start(
                    out=wor_bfs[i].t[o], in_=ewout_in.t[i, 512:1024, :].rearrange("(h v) f -> v h f", v=64)[:, :, o * 128:(o + 1) * 128]),
                    reads=[ewout_in], writes=[wor_bfs[i]], dma=ecastsem[i])

        def even_layer(layer):
            i = layer // 2
            gidx = layer * 3 + 1
            NTE = T // TE
            m0 = A.mark()
            P.pspool = None
            we_bf, wl_bf, woc_bf, wor_bf = we_bfs[i], wl_bfs[i], woc_bfs[i], wor_bfs[i]
            m_xm = A.alloc("m_xm", [HG, 128], F32, parts=64)
            m_xt = A.alloc("m_xt", [HG, 64], F32, parts=64)
            identg = A.alloc("identg", [HG, 64], F32, parts=64)
            mtmp = A.alloc("mtmp", [3, 64], F32, parts=64)
            blockones = A.alloc("blockones", [128], BF16)
            rmask = A.alloc("rmask", [TE], F32)
            ecols = A.alloc("ecols", [48], F32)
            lcols = A.alloc("lcols", [4], F32)
            hcols = A.alloc("hcols", [16], F32, parts=64)
            ccon = A.alloc("ccon", [4], F32)
            w_up = A.alloc("w_up", [512], BF16, parts=32)
            a_up = A.alloc("a_up", [512], BF16, parts=32)
            g_up = A.alloc("g_up", [512], BF16, parts=96)
            wlora = A.alloc("wlora", [NC8, 160], BF16)
            Ss = [A.alloc("S%d" % k, [8, 64], F32, parts=64) for k in range(2)]
            for k in range(3):
                P.op("pool", lambda e, k=k: e.memset(mtmp[:, k, :], 1.0), writes=[mtmp])
            P.op("pool", lambda e: e.affine_select(out=mtmp[:, 0, :], in_=mtmp[:, 0, :], pattern=[[1, 64]],
                                                   compare_op=ALU.is_gt, fill=0.0, base=0, channel_multiplier=-1),
                 reads=[mtmp], writes=[mtmp])
            P.op("pool", lambda e: e.affine_select(out=mtmp[:, 1, :], in_=mtmp[:, 1, :], pattern=[[1, 64]],
                                                   compare_op=ALU.is_ge, fill=0.0, base=0, channel_multiplier=-1),
                 reads=[mtmp], writes=[mtmp])
            P.op("pool", lambda e: e.affine_select(out=mtmp[:, 2, :], in_=mtmp[:, 2, :], pattern=[[-1, 64]],
                                                   compare_op=ALU.is_gt, fill=0.0, base=0, channel_multiplier=1),
                 reads=[mtmp], writes=[mtmp])
            for hh in range(HG):
                P.op("pool", lambda e, hh=hh: e.tensor_copy(out=m_xm[:, hh, 0:64], in_=mtmp[:, 0, :]), reads=[mtmp], writes=[m_xm])
                P.op("pool", lambda e, hh=hh: e.tensor_copy(out=m_xm[:, hh, 64:128], in_=mtmp[:, 1, :]), reads=[mtmp], writes=[m_xm])
                P.op("pool", lambda e, hh=hh: e.tensor_copy(out=m_xt[:, hh, :], in_=mtmp[:, 2, :]), reads=[mtmp], writes=[m_xt])
                P.op("pool", lambda e, hh=hh: e.tensor_copy(out=identg[:, hh, :], in_=ident[0:64, 0:64]), reads=[ident], writes=[identg])
            P.op("pool", lambda e: e.memset(blockones[:], 1.0), writes=[blockones])
            P.op("pool", lambda e: e.memset(blockones[0:64, 64:128], 0.0), reads=[blockones], writes=[blockones])
            P.op("pool", lambda e: e.memset(blockones[64:128, 0:64], 0.0), reads=[blockones], writes=[blockones])
            P.op("pool", lambda e: e.memset(rmask[:], 1.0), writes=[rmask])
            P.op("pool", lambda e: e.memset(rmask[:].rearrange("p (c t) -> p c t", t=64)[:, :, 0:1], 0.0),
                 reads=[rmask], writes=[rmask])
            P.op("pool", lambda e: e.memset(ccon[:, 0:1], 1e-24), writes=[ccon])
            P.op("pool", lambda e: e.memset(ccon[:, 1:2], 64e-5), reads=[ccon], writes=[ccon])
            P.op("pool", lambda e: e.memset(Ss[0][:], 0.0), writes=[Ss[0]])
            P.dma("sp", ecols, ecols[:, 0:12], emu_in, emu_in.t[i, 0:1536].rearrange("(j p) -> p j", p=128))
            for k, src in enumerate((w0_in, a0_in, kk_in, ka_in, rk_in)):
                P.dma("sp", ecols, ecols[:, 12 + 4 * k:16 + 4 * k], src, src.t[i].rearrange("(j p) -> p j", p=128))
            P.dma("sp", ecols, ecols[:, 36:48].rearrange("p (j c) -> p j c", j=3), cw_in,
                  cw_in.t[i].rearrange("j (c p) -> p j c", p=128))
            P.op("dve", lambda e: e.tensor_scalar(out=ecols[:, 32:36], in0=ecols[:, 24:28], scalar1=-1.0, scalar2=1.0,
                                                  op0=ALU.mult, op1=ALU.add), reads=[ecols], writes=[ecols])
            P.dma("sp", lcols, lcols[0:32, 0:1], emu_in, emu_in.t[i, 1536:1568].rearrange("(p o) -> p o", o=1))
            P.dma("sp", lcols, lcols[0:32, 1:2], emu_in, emu_in.t[i, 1568:1600].rearrange("(p o) -> p o", o=1))
            P.dma("sp", lcols, lcols[0:96, 2:3], emu_in, emu_in.t[i, 1600:1696].rearrange("(p o) -> p o", o=1))
            P.dma("sp", hcols, hcols[:, 0:8], lnw_in, lnw_in.t[i].rearrange("(h v) -> v h", v=64))
            P.dma("sp", hcols, hcols[:, 8:16], lnb_in, lnb_in.t[i].rearrange("(h v) -> v h", v=64))
            P.dma("pool", w_up, w_up[:], wup_in, wup_in.t[i])
            P.dma("pool", a_up, a_up[:], aup_in, aup_in.t[i])
            P.dma("pool", g_up, g_up[:], gup_in, gup_in.t[i])
            P.dma("sp", wlora, wlora[:], wl_bf, wl_bf.t)

            xt_tiles = [A.alloc("ext%d" % k, [NC8, TE], F32) for k in range(2)]
            sq_tile = A.alloc("esq", [NC8, TE], BF16)
            h_tiles = [A.alloc("eh%d" % k, [NC8, TE], BF16) for k in range(2)]
            rstd_t = A.alloc("erstd", [TE], F32)
            NWB = 4
            we_tiles = [A.alloc("we%d" % k, [NC8, 128], BF16) for k in range(NWB)]
            gc = A.alloc("gc", [4, TE], F32)
            gb = A.alloc("gb", [4, TE], F32)
            ub = A.alloc("ub", [4, TE + 2], F32)
            cacc = [A.alloc("cacc%d" % k, [TE], F32) for k in range(2)]
            yconv = A.alloc("yconv", [4, TE], BF16)
            PR = [A.alloc("PR%d" % k, [TE + 1], F32) for k in range(12)]
            PL = [A.alloc("PL%d" % k, [TE + 1], F32) for k in range(3)]
            PRh = A.alloc("PRh", [16], F32)
            dtmp = [A.alloc("dtmp%d" % k, [TE], F32) for k in range(3)]
            tdw = A.alloc("tdw", [TE], BF16, parts=32)
            dab = A.alloc("dab", [TE], BF16, parts=32)
            sdg = A.alloc("sdg", [TE], BF16, parts=96)
            lw = A.alloc("lw", [TE], F32)
            aa = A.alloc("aa", [TE], F32)
            kkb = A.alloc("kk", [TE], F32)
            kk2 = A.alloc("kk2", [TE], BF16)
            rsb = A.alloc("rs", [TE], F32)
            kkn = A.alloc("kkn", [TE], F32)
            tmpk = A.alloc("tmpk", [TE], F32)
            bv = A.alloc("bv", [TE], F32)
            rkr = A.alloc("rkr", [TE], BF16)
            cc = A.alloc("cc", [TE], F32)
            cp = A.alloc("cp", [TE], F32)
            cd = A.alloc("cd", [TE], F32)
            ec = A.alloc("ec", [TE], F32)
            eci = A.alloc("eci", [TE], F32)
            ecp = A.alloc("ecp", [TE], F32)
            eCc = A.alloc("eCc", [TE], F32)
            AR = A.alloc("AR", [4, NCHT, 2, 64], SDT)
            Bt = A.alloc("Bt", [4, TE], SDT)
            Kt = A.alloc("Kt", [4, TE], SDT)
            BhT = A.alloc("BhT", [4, TE], SDT)
            KhT = A.alloc("KhT", [4, TE], SDT)
            bonus = A.alloc("bonus", [4, TE], F32)
            wCfm = A.alloc("wCfm", [4, NCHT], F32)
            wCT = A.alloc("wCT", [8, NCHT], F32, parts=64)
            Yb = A.alloc("Yb", [8, TE], F32, parts=64)
            RtL = A.alloc("RtL", [8, TE], F32, parts=64)
            ysq = A.alloc("ysq", [8, TE], F32, parts=64)
            mean = A.alloc("mean", [8, TE], F32, parts=64)
            var = A.alloc("var", [8, TE], F32, parts=64)
            ycb = A.alloc("ycb", [8, TE], F32, parts=64)
            yfin = A.alloc("yfin", [8, TE], BF16, parts=64)
            woc_t = [A.alloc("woc%d" % k, [4, 128], BF16) for k in range(2)]
            wor_t = [A.alloc("wor%d" % k, [8, 128], BF16, parts=64) for k in range(2)]
            G_ = []
            for g in range(2 * NCHT):
                d = {}
                for nm, shp in (("W1A", [HG, 2, 64]), ("Bh", [HG, 64]), ("Kh", [HG, 64]), ("Vt", [HG, 64]),
                                ("XM", [HG, 2, 64]), ("LM", [HG, 2, 64]), ("XT", [HG, 64]),
                                ("P0", [HG, 64]), ("P1", [HG, 64]), ("PT0", [HG, 64]), ("PT1", [HG, 64]),
                                ("Ac0", [HG, 64]), ("Ac1", [HG, 64]), ("UZ", [HG, 2, 64]), ("GT", [HG, 64]), ("QT", [HG, 64])):
                    d[nm] = A.alloc("%s_%d" % (nm, g), shp, F32 if nm in ("GT", "QT") else SDT, parts=64)
                G_.append(d)

            hset0 = (AR, Bt, Kt, BhT, KhT, bonus, wCT, RtL, sdg, yconv, PR)
            hset1 = (A.alloc("AR1", [4, NCHT, 2, 64], SDT), A.alloc("Bt1", [4, TE], SDT), A.alloc("Kt1", [4, TE], SDT),
                     A.alloc("BhT1", [4, TE], SDT), A.alloc("KhT1", [4, TE], SDT), A.alloc("bonus1", [4, TE], F32),
                     A.alloc("wCT1", [8, NCHT], F32, parts=64), A.alloc("RtL1", [8, TE], F32, parts=64),
                     A.alloc("sdg1", [TE], BF16, parts=96), A.alloc("yconv1", [4, TE], BF16),
                     [A.alloc("PRb%d" % k, [TE + 1], F32) for k in range(12)])
            P.op("pool", lambda e: e.memset(ub[:, :, 0:2], 0.0), writes=[ub])
            P.op("pool", lambda e: e.memset(PRh[:], 0.0), writes=[PRh])

            wcnt = [0]
            ocnt = [0]
            scur = [0]

            def proj(j_lo, ncols, h, consume):
                w = we_tiles[wcnt[0] % NWB]
                wcnt[0] += 1
                P.dma("sp", w, w[:], we_bf, we_bf.t[j_lo])
                ps = P.next_psum()
                for c in range(NC8):
                    P.op("pe", lambda e, c=c: e.matmul(ps[:ncols, :TE], lhsT=w[:, c, 0:ncols], rhs=h[:, c, :],
                                                      start=(c == 0), stop=(c == NC8 - 1)), reads=[w, h], writes=[ps])
                consume(ps)

            def chunk_pipeline(ch, g, te):
                B = G_[ch * 2 + g]
                cs = slice(ch * 64, (ch + 1) * 64)
                heads = list(range(g * HG, (g + 1) * HG))

                def hp(hd):
                    return hd // 2, (hd % 2) * 64

                for (nm, srcf, dst) in (("A", lambda pc: AR[:, pc, ch, 0, :], B["W1A"]),
                                        ("Bh", lambda pc: BhT[:, pc, cs], B["Bh"]),
                                        ("Kh", lambda pc: KhT[:, pc, cs], B["Kh"]),
                                        ("V", lambda pc: PR[8 + pc][:, 1 + ch * 64:1 + (ch + 1) * 64], B["Vt"])):
                    ps = P.next_psum()
                    for k in range(2):
                        pc = g * 2 + k
                        srcb = AR if nm == "A" else (BhT if nm == "Bh" else (KhT if nm == "Kh" else PR[8 + pc]))
                        idm = ident if (nm == "V" or SDT == F32) else ident_bf
                        P.op("pe", lambda e, k=k, pc=pc, idm=idm: e.matmul(ps[:64, k * 128:(k + 1) * 128], lhsT=srcf(pc),
                                                                           rhs=idm[:], start=True, stop=True),
                             reads=[srcb, idm], writes=[ps])
                    if nm == "A":
                        P.op("act", lambda e: e.copy(out=dst[:, :, 1, :], in_=ps[:64, 0:256].rearrange("p (h k) -> p h k", k=64)),
                             reads=[ps], writes=[dst])
                    else:
                        P.op("dve" if nm != "V" else "act",
                             (lambda e: e.tensor_copy(out=dst[:], in_=ps[:64, 0:256].rearrange("p (h k) -> p h k", k=64)))
                             if nm != "V" else
                             (lambda e: e.copy(out=dst[:], in_=ps[:64, 0:256].rearrange("p (h k) -> p h k", k=64))),
                             reads=[ps], writes=[dst])
                yield
                psx = [P.next_psum(), P.next_psum()]
                psl = [P.next_psum(), P.next_psum()]
                pst = [P.next_psum(), P.next_psum()]
                for k, hd in enumerate(heads):
                    pc, pb = hp(hd)
                    par = hd % 2
                    a = k // 2
                    P.op("pe", lambda e, a=a, par=par, pc=pc, pb=pb: e.matmul(
                        psx[par][:64, a * 128:(a + 1) * 128], lhsT=Bt[pb:pb + 64, pc, cs],
                        rhs=AR[pb:pb + 64, pc, ch].rearrange("p a t -> p (a t)"), start=True, stop=True),
                        reads=[Bt, AR], writes=[psx[par]])
                    P.op("pe", lambda e, a=a, par=par, pc=pc, pb=pb: e.matmul(
                        psl[par][:64, a * 128:(a + 1) * 128], lhsT=Kt[pb:pb + 64, pc, cs],
                        rhs=AR[pb:pb + 64, pc, ch].rearrange("p a t -> p (a t)"), start=True, stop=True),
                        reads=[Kt, AR], writes=[psl[par]])
                    P.op("pe", lambda e, a=a, par=par, pc=pc, pb=pb: e.matmul(
                        pst[par][:64, a * 64:(a + 1) * 64], lhsT=AR[pb:pb + 64, pc, ch, 0, :],
                        rhs=Bt[pb:pb + 64, pc, cs], start=True, stop=True),
                        reads=[Bt, AR], writes=[pst[par]])
                for par in range(2):
                    P.op("dve", lambda e, par=par: e.tensor_tensor(
                        out=B["XM"][:].rearrange("p (a q) x t -> p a q (x t)", q=2)[:, :, par, :],
                        in0=psx[par][:64, 0:256].rearrange("p (a x) -> p a x", x=128),
                        in1=m_xm[:, 0:2, :], op=ALU.mult), reads=[psx[par], m_xm], writes=[B["XM"]])
                    P.op("dve", lambda e, par=par: e.tensor_tensor(
                        out=B["LM"][:].rearrange("p (a q) x t -> p a q (x t)", q=2)[:, :, par, :],
                        in0=psl[par][:64, 0:256].rearrange("p (a x) -> p a x", x=128),
                        in1=m_xm[:, 0:2, :], op=ALU.mult), reads=[psl[par], m_xm], writes=[B["LM"]])
                    P.op("dve", lambda e, par=par: e.tensor_tensor(
                        out=B["XT"][:].rearrange("p (a q) t -> p a q t", q=2)[:, :, par, :],
                        in0=pst[par][:64, 0:128].rearrange("p (a x) -> p a x", x=64),
                        in1=m_xt[:, 0:2, :], op=ALU.mult), reads=[pst[par], m_xt], writes=[B["XT"]])
                P.op("pool", lambda e: e.tensor_tensor(out=B["Ac0"][:], in0=B["XM"][:, :, 0, :], in1=identg[:], op=ALU.add),
                     reads=[B["XM"], identg], writes=[B["Ac0"]])
                yield
                Pc, PTc, Ac = (B["XM"], lambda k: B["XM"][:, k, 0, :]), (B["XT"], lambda k: B["XT"][:, k, :]), B["Ac0"]
                for lvl in range(5):
                    lastl = (lvl == 4)
                    Pn = B["P%d" % (lvl % 2)]
                    PTn = B["PT%d" % (lvl % 2)]
                    Acn = B["Ac%d" % ((lvl + 1) % 2)]
                    psB = P.next_psum()
                    psA = None if lastl else P.next_psum()
                    for k in range(HG):
                        P.op("pe", lambda e, k=k: e.matmul(psB[:64, k * 64:(k + 1) * 64], lhsT=Pc[1](k), rhs=PTc[1](k),
                                                          start=True, stop=True), reads=[Pc[0], PTc[0]], writes=[psB])
                    if not lastl:
                        for k in range(HG):
                            P.op("pe", lambda e, k=k: e.matmul(psA[:64, k * 64:(k + 1) * 64], lhsT=PTc[1](k), rhs=Pc[1](k),
                                                              start=True, stop=True), reads=[Pc[0], PTc[0]], writes=[psA])
                    P.op("dve", lambda e: e.tensor_copy(out=PTn[:], in_=psB[:64, 0:256].rearrange("p (h x) -> p h x", x=64)),
                         reads=[psB], writes=[PTn])
                    if not lastl:
                        P.op("act", lambda e: e.copy(out=Pn[:], in_=psA[:64, 0:256].rearrange("p (h x) -> p h x", x=64)),
                             reads=[psA], writes=[Pn])
                    yield
                    psC = P.next_psum()
                    for k in range(HG):
                        P.op("pe", lambda e, k=k: e.matmul(psC[:64, k * 64:(k + 1) * 64], lhsT=PTn[:, k, :], rhs=Ac[:, k, :],
                                                          start=True, stop=True), reads=[PTn, Ac], writes=[psC])
                    P.op("dve", lambda e: e.tensor_tensor(out=Acn[:], in0=psC[:64, 0:256].rearrange("p (h x) -> p h x", x=64),
                                                          in1=Ac[:], op=ALU.add), reads=[psC, Ac], writes=[Acn])
                    Pc = (Pn, lambda k, Pn=Pn: Pn[:, k, :])
                    PTc = (PTn, lambda k, PTn=PTn: PTn[:, k, :])
                    Ac = Acn
                    yield
                NTb = Ac
                psW = P.next_psum()
                for k in range(HG):
                    P.op("pe", lambda e, k=k: e.matmul(psW[:64, k * 64:(k + 1) * 64], lhsT=B["LM"][:, k, 0, :], rhs=B["Vt"][:, k, :],
                                                      start=True, stop=True), reads=[B["LM"], B["Vt"]], writes=[psW])
                P.op("act", lambda e: e.copy(out=B["W1A"][:, :, 0, :], in_=psW[:64, 0:256].rearrange("p (h x) -> p h x", x=64)),
                     reads=[psW], writes=[B["W1A"]])
                yield
                psU = P.next_psum()
                for k in range(HG):
                    P.op("pe", lambda e, k=k: e.matmul(psU[:64, k * 128:(k + 1) * 128], lhsT=NTb[:, k, :],
                                                      rhs=B["W1A"][:, k].rearrange("p a t -> p (a t)"),
                                                      start=True, stop=True), reads=[NTb, B["W1A"]], writes=[psU])
                P.op("dve", lambda e: e.tensor_copy(out=B["UZ"][:].rearrange("p h a t -> p h (a t)"),
                                                    in_=psU[:64, :].rearrange("p (h x) -> p h x", x=128)),
                     reads=[psU], writes=[B["UZ"]])
                yield
                psG = P.next_psum()
                psQ = P.next_psum()
                for k, hd in enumerate(heads):
                    pc, pb = hp(hd)
                    P.op("pe", lambda e, k=k: e.matmul(psG[:64, k * 64:(k + 1) * 64], lhsT=B["UZ"][:, k, 1, :], rhs=B["Bh"][:, k, :],
                                                      start=True, stop=True), reads=[B["UZ"], B["Bh"]], writes=[psG])
                    P.op("pe", lambda e, k=k: e.matmul(psQ[:64, k * 64:(k + 1) * 64], lhsT=B["UZ"][:, k, 1, :], rhs=B["XM"][:, k, 1, :],
                                                      start=True, stop=True), reads=[B["UZ"], B["XM"]], writes=[psQ])
                P.op("act", lambda e: e.copy(out=B["GT"][:], in_=psG[:64, 0:256].rearrange("p (h x) -> p h x", x=64)),
                     reads=[psG], writes=[B["GT"]])
                P.op("dve", lambda e: e.tensor_tensor(out=B["QT"][:], in0=psQ[:64, 0:256].rearrange("p (h x) -> p h x", x=64),
                                                      in1=RtL[:, g * HG:(g + 1) * HG, cs], op=ALU.add),
                     reads=[psQ, RtL], writes=[B["QT"]])
                yield
                So = Ss[(scur[0] + ch) % 2]
                Sn = Ss[(scur[0] + ch + 1) % 2]
                psY = P.next_psum()
                psS = P.next_psum()
                for k, hd in enumerate(heads):
                    P.op("pe", lambda e, k=k: e.matmul(psY[:64, k * 64:(k + 1) * 64], lhsT=B["UZ"][:, k, 0, :], rhs=B["XM"][:, k, 1, :],
                                                      start=True, stop=False), reads=[B["UZ"], B["XM"]], writes=[psY])
                    P.op("pe", lambda e, k=k: e.matmul(psY[:64, k * 64:(k + 1) * 64], lhsT=B["Vt"][:, k, :], rhs=B["LM"][:, k, 1, :],
                                                      start=False, stop=False), reads=[B["Vt"], B["LM"]], writes=[psY])
                    P.op("pe", lambda e, k=k, hd=hd: e.matmul(psY[:64, k * 64:(k + 1) * 64], lhsT=So[:, hd, :], rhs=B["QT"][:, k, :],
                                                             start=False, stop=True), reads=[So, B["QT"]], writes=[psY])
                for k, hd in enumerate(heads):
                    P.op("pe", lambda e, k=k: e.matmul(psS[:64, k * 64:(k + 1) * 64], lhsT=B["Bh"][:, k, :], rhs=B["UZ"][:, k, 0, :],
                                                      start=True, stop=False), reads=[B["UZ"], B["Bh"]], writes=[psS])
                    P.op("pe", lambda e, k=k: e.matmul(psS[:64, k * 64:(k + 1) * 64], lhsT=B["Kh"][:, k, :], rhs=B["Vt"][:, k, :],
                                                      start=False, stop=False), reads=[B["Kh"], B["Vt"]], writes=[psS])
                    P.op("pe", lambda e, k=k, hd=hd: e.matmul(psS[:64, k * 64:(k + 1) * 64], lhsT=B["GT"][:, k, :], rhs=So[:, hd, :],
                                                             start=False, stop=True), reads=[So, B["GT"]], writes=[psS])
                P.op("act", lambda e: e.copy(out=Yb[:, g * HG:(g + 1) * HG, cs], in_=psY[:64, 0:256].rearrange("p (h x) -> p h x", x=64)),
                     reads=[psY], writes=[Yb])
                for k, hd in enumerate(heads):
                    P.op("dve", lambda e, k=k, hd=hd: e.scalar_tensor_tensor(
                        out=Sn[:, hd, :], in0=So[:, hd, :], scalar=wCT[:, hd, ch:ch + 1], in1=psS[:64, k * 64:(k + 1) * 64],
                        op0=ALU.mult, op1=ALU.add), reads=[So, wCT, psS], writes=[Sn])
                yield

            def phase_a(te):
                ts = slice(te * TE, (te + 1) * TE)
                xt = xt_tiles[te % 2]
                h = h_tiles[te % 2]
                tt = te // (TT // TE)
                P.dma("sp", xt, xt[:], XTb[tt], XTv[:, :, ts])
                rmsnorm_tile(xt, h, gidx, sq_tile, rstd_t, w=TE)
                for pc in range(4):
                    proj(4 + pc, 128, h, lambda ps, pc=pc: P.op(
                        "act", lambda e: e.copy(out=gc[:, pc, :], in_=ps[:, :TE]), reads=[ps], writes=[gc]))
                yield
                for pc in range(4):
                    proj(8 + pc, 128, h, lambda ps, pc=pc: P.op(
                        "dve", lambda e: e.tensor_tensor(out=ub[:, pc, 2:TE + 2], in0=ps[:, :TE], in1=gc[:, pc, :], op=ALU.mult),
                        reads=[ps, gc], writes=[ub]))
                yield
                for pc in range(4):
                    proj(pc, 128, h, lambda ps, pc=pc: P.op(
                        "act", lambda e: e.copy(out=gb[:, pc, :], in_=ps[:, :TE]), reads=[ps], writes=[gb]))
                yield
                for pc in range(4):
                    ca = cacc[pc % 2]
                    P.op("dve", lambda e, pc=pc, ca=ca: e.tensor_scalar(out=ca[:], in0=ub[:, pc, 2:TE + 2],
                                                                      scalar1=ecols[:, 36 + 8 + pc:36 + 8 + pc + 1], scalar2=None,
                                                                      op0=ALU.mult), reads=[ub, ecols], writes=[ca])
                    P.op("dve", lambda e, pc=pc, ca=ca: e.scalar_tensor_tensor(
                        out=ca[:], in0=ub[:, pc, 1:TE + 1], scalar=ecols[:, 36 + 4 + pc:36 + 4 + pc + 1], in1=ca[:],
                        op0=ALU.mult, op1=ALU.add), reads=[ub, ecols, ca], writes=[ca])
                    P.op("dve", lambda e, pc=pc, ca=ca: e.scalar_tensor_tensor(
                        out=ca[:], in0=ub[:, pc, 0:TE], scalar=ecols[:, 36 + pc:36 + pc + 1], in1=ca[:],
                        op0=ALU.mult, op1=ALU.add), reads=[ub, ecols, ca], writes=[ca])
                    P.op("dve", lambda e, pc=pc, ca=ca: e.tensor_tensor(out=yconv[:, pc, :], in0=ca[:], in1=gb[:, pc, :], op=ALU.mult),
                         reads=[ca, gb], writes=[yconv])
                P.op("pool", lambda e: e.tensor_copy(out=ub[:, :, 0:2], in_=ub[:, :, TE:TE + 2]), reads=[ub], writes=[ub])
                yield
                for j in range(12):
                    def cons(ps, j=j):
                        pr = PR[j]
                        P.op("act", lambda e: e.copy(out=pr[:, 0:1], in_=PRh[:, j:j + 1]), reads=[PRh], writes=[pr])
                        P.op("act", lambda e: e.copy(out=pr[:, 1:TE + 1], in_=ps[:, :TE]), reads=[ps], writes=[pr])
                        d = dtmp[j % 3]
                        P.op("pool", lambda e: e.tensor_tensor(out=d[:], in0=pr[:, 0:TE], in1=pr[:, 1:TE + 1], op=ALU.subtract),
                             reads=[pr], writes=[d])
                        P.op("pool", lambda e: e.tensor_copy(out=PRh[:, j:j + 1], in_=pr[:, TE:TE + 1]), reads=[pr], writes=[PRh])
                        P.op("dve", lambda e: e.scalar_tensor_tensor(out=pr[:, 1:TE + 1], in0=d[:], scalar=ecols[:, j:j + 1],
                                                                     in1=pr[:, 1:TE + 1], op0=ALU.mult, op1=ALU.add),
                             reads=[d, ecols, pr], writes=[pr])
                    proj(12 + j, 128, h, cons)
                    if j % 3 == 2:
                        yield
                for li, (lo, n) in enumerate(((0, 32), (32, 32), (64, 96))):
                    ps = P.next_psum()
                    for c in range(NC8):
                        P.op("pe", lambda e, c=c, lo=lo, n=n: e.matmul(ps[:n, :TE], lhsT=wlora[:, c, lo:lo + n], rhs=h[:, c, :],
                                                                      start=(c == 0), stop=(c == NC8 - 1)),
                             reads=[wlora, h], writes=[ps])
                    pl = PL[li]
                    P.op("act", lambda e, li=li, n=n, pl=pl: e.copy(out=pl[:n, 0:1], in_=PRh[:n, 12 + li:13 + li]),
                         reads=[PRh], writes=[pl])
                    P.op("act", lambda e, n=n, pl=pl, ps=ps: e.copy(out=pl[:n, 1:TE + 1], in_=ps[:n, :TE]), reads=[ps], writes=[pl])
                    d = dtmp[li % 3]
                    P.op("pool", lambda e, n=n, pl=pl, d=d: e.tensor_tensor(out=d[:n], in0=pl[:n, 0:TE], in1=pl[:n, 1:TE + 1],
                                                                           op=ALU.subtract), reads=[pl], writes=[d])
                    P.op("pool", lambda e, n=n, pl=pl, li=li: e.tensor_copy(out=PRh[:n, 12 + li:13 + li], in_=pl[:n, TE:TE + 1]),
                         reads=[pl], writes=[PRh])
                    P.op("dve", lambda e, n=n, pl=pl, d=d, li=li: e.scalar_tensor_tensor(
                        out=pl[:n, 1:TE + 1], in0=d[:n], scalar=lcols[:n, li:li + 1], in1=pl[:n, 1:TE + 1],
                        op0=ALU.mult, op1=ALU.add), reads=[d, lcols, pl], writes=[pl])
                P.op("act", lambda e: e.activation(out=tdw[:], in_=PL[0][:32, 1:TE + 1], func=AF.Tanh), reads=[PL[0]], writes=[tdw])
                P.op("act", lambda e: e.copy(out=dab[:], in_=PL[1][:32, 1:TE + 1]), reads=[PL[1]], writes=[dab])
                P.op("act", lambda e: e.activation(out=sdg[:], in_=PL[2][:96, 1:TE + 1], func=AF.Sigmoid), reads=[PL[2]], writes=[sdg])
                yield
                for pc in range(4):
                    fs = slice(pc * 128, (pc + 1) * 128)
                    rr = PR[pc]
                    kx = PR[4 + pc]
                    vv = PR[8 + pc]
                    R1 = slice(1, TE + 1)
                    psw = P.next_psum()
                    P.op("pe", lambda e, fs=fs, psw=psw: e.matmul(psw[:, :TE], lhsT=w_up[:, fs], rhs=tdw[:], start=True, stop=True),
                         reads=[w_up, tdw], writes=[psw])
                    psa = P.next_psum()
                    P.op("pe", lambda e, fs=fs, psa=psa: e.matmul(psa[:, :TE], lhsT=a_up[:, fs], rhs=dab[:], start=True, stop=True),
                         reads=[a_up, dab], writes=[psa])
                    P.op("act", lambda e, pc=pc, psw=psw: e.activation(out=lw[:], in_=psw[:, :TE], func=AF.Sigmoid,
                                                                      bias=ecols[:, 12 + pc:13 + pc]),
                         reads=[psw, ecols], writes=[lw])
                    P.op("act", lambda e, pc=pc, psa=psa: e.activation(out=aa[:], in_=psa[:, :TE], func=AF.Sigmoid,
                                                                      bias=ecols[:, 16 + pc:17 + pc]),
                         reads=[psa, ecols], writes=[aa])
                    P.op("dve", lambda e: e.tensor_scalar(out=lw[:], in0=lw[:], scalar1=LWS, scalar2=None, op0=ALU.mult),
                         reads=[lw], writes=[lw])
                    P.op("dve", lambda e, pc=pc, kx=kx: e.tensor_scalar(out=kkb[:], in0=kx[:, R1], scalar1=ecols[:, 20 + pc:21 + pc],
                                                                      scalar2=None, op0=ALU.mult), reads=[kx, ecols], writes=[kkb])
                    P.op("act", lambda e: e.activation(out=kk2[:], in_=kkb[:], func=AF.Square), reads=[kkb], writes=[kk2])
                    pss = P.next_psum()
                    P.op("pe", lambda e, pss=pss: e.matmul(pss[:, :TE], lhsT=blockones[:], rhs=kk2[:], start=True, stop=True),
                         reads=[blockones, kk2], writes=[pss])
                    P.op("act", lambda e, pss=pss: e.activation(out=rsb[:], in_=pss[:, :TE], func=AF.Ln, bias=ccon[:, 0:1]),
                         reads=[pss, ccon], writes=[rsb])
                    P.op("act", lambda e: e.activation(out=rsb[:], in_=rsb[:], func=AF.Exp, scale=-0.5), reads=[rsb], writes=[rsb])
                    P.op("dve", lambda e: e.tensor_tensor(out=kkn[:], in0=kkb[:], in1=rsb[:], op=ALU.mult), reads=[kkb, rsb], writes=[kkn])
                    P.op("dve", lambda e, pc=pc: e.tensor_scalar(out=tmpk[:], in0=aa[:], scalar1=ecols[:, 24 + pc:25 + pc],
                                                                 scalar2=ecols[:, 32 + pc:33 + pc], op0=ALU.mult, op1=ALU.add),
                         reads=[aa, ecols], writes=[tmpk])
                    P.op("dve", lambda e, kx=kx: e.tensor_tensor(out=kx[:, R1], in0=kx[:, R1], in1=tmpk[:], op=ALU.mult),
                         reads=[kx, tmpk], writes=[kx])
                    P.op("pool", lambda e: e.tensor_tensor(out=bv[:], in0=kkn[:], in1=aa[:], op=ALU.mult), reads=[kkn, aa], writes=[bv])
                    P.op("dve", lambda e, pc=pc, rr=rr, kx=kx: e.scalar_tensor_tensor(
                        out=rkr[:], in0=rr[:, R1], scalar=ecols[:, 28 + pc:29 + pc], in1=kx[:, R1], op0=ALU.mult, op1=ALU.mult),
                        reads=[rr, ecols, kx], writes=[rkr])
                    psr = P.next_psum()
                    P.op("pe", lambda e, psr=psr: e.matmul(psr[:, :TE], lhsT=blockones[:], rhs=rkr[:], start=True, stop=True),
                         reads=[blockones, rkr], writes=[psr])
                    P.op("dve", lambda e, pc=pc, vv=vv, psr=psr: e.tensor_tensor(out=bonus[:, pc, :], in0=psr[:, :TE], in1=vv[:, R1],
                                                                               op=ALU.mult), reads=[psr, vv], writes=[bonus])
                    yield
                    P.op("dve", lambda e: e.tensor_tensor_scan(out=cc[:], data0=rmask[:], data1=lw[:], initial=0.0,
                                                               op0=ALU.mult, op1=ALU.add), reads=[rmask, lw], writes=[cc])
                    P.op("pool", lambda e: e.tensor_tensor(out=cp[:], in0=cc[:], in1=lw[:], op=ALU.subtract), reads=[cc, lw], writes=[cp])
                    for ch in range(NCHT):
                        P.op("dve", lambda e, ch=ch: e.tensor_scalar(out=cd[:, ch * 64:(ch + 1) * 64], in0=cc[:, ch * 64:(ch + 1) * 64],
                                                                     scalar1=cc[:, ch * 64 + 63:ch * 64 + 64], scalar2=None,
                                                                     op0=ALU.subtract), reads=[cc], writes=[cd])
                    P.op("act", lambda e: e.activation(out=ec[:], in_=cc[:], func=AF.Exp), reads=[cc], writes=[ec])
                    P.op("act", lambda e: e.activation(out=eci[:], in_=cc[:], func=AF.Exp, scale=-1.0), reads=[cc], writes=[eci])
                    P.op("act", lambda e: e.activation(out=ecp[:], in_=cp[:], func=AF.Exp), reads=[cp], writes=[ecp])
                    P.op("act", lambda e: e.activation(out=eCc[:], in_=cd[:], func=AF.Exp, scale=-1.0), reads=[cd], writes=[eCc])
                    P.op("act", lambda e, pc=pc: e.copy(out=wCfm[:, pc, :], in_=ec[:].rearrange("p (c t) -> p c t", t=64)[:, :, 63]),
                         reads=[ec], writes=[wCfm])
                    yield
                    v3 = lambda b: b[:].rearrange("p (c t) -> p c t", t=64)
                    P.op("dve", lambda e, pc=pc: e.scalar_tensor_tensor(out=AR[:, pc, :, 0, :], in0=v3(kkn), scalar=-1.0, in1=v3(ecp),
                                                                        op0=ALU.mult, op1=ALU.mult), reads=[kkn, ecp], writes=[AR])
                    P.op("pool", lambda e, pc=pc, rr=rr: e.tensor_tensor(out=AR[:, pc, :, 1, :],
                                                                       in0=rr[:, R1].rearrange("p (c t) -> p c t", t=64),
                                                                       in1=v3(ec), op=ALU.mult), reads=[rr, ec], writes=[AR])
                    P.op("dve", lambda e, pc=pc: e.tensor_tensor(out=Bt[:, pc, :], in0=bv[:], in1=eci[:], op=ALU.mult),
                         reads=[bv, eci], writes=[Bt])
                    P.op("pool", lambda e, pc=pc, kx=kx: e.tensor_tensor(out=Kt[:, pc, :], in0=kx[:, R1], in1=eci[:], op=ALU.mult),
                         reads=[kx, eci], writes=[Kt])
                    P.op("dve", lambda e, pc=pc: e.tensor_tensor(out=BhT[:, pc, :], in0=bv[:], in1=eCc[:], op=ALU.mult),
                         reads=[bv, eCc], writes=[BhT])
                    P.op("pool", lambda e, pc=pc, kx=kx: e.tensor_tensor(out=KhT[:, pc, :], in0=kx[:, R1], in1=eCc[:], op=ALU.mult),
                         reads=[kx, eCc], writes=[KhT])
                yield
                for par in range(2):
                    pb = par * 64
                    psc = P.next_psum()
                    P.op("pe", lambda e, pb=pb, psc=psc: e.matmul(psc[:64, 0:4 * NCHT], lhsT=ident[pb:pb + 64, pb:pb + 64],
                                                                  rhs=wCfm[pb:pb + 64].rearrange("p a c -> p (a c)"), start=True, stop=True),
                         reads=[ident, wCfm], writes=[psc])
                    P.op("dve", lambda e, par=par, psc=psc: e.tensor_copy(
                        out=wCT[:].rearrange("p (a q) c -> p a q c", q=2)[:, :, par, :],
                        in_=psc[:64, 0:4 * NCHT].rearrange("p (a c) -> p a c", c=NCHT)),
                        reads=[psc], writes=[wCT])
                    psr2 = P.next_psum()
                    for pc in range(4):
                        P.op("pe", lambda e, pc=pc, pb=pb, psr2=psr2: e.matmul(
                            psr2[:64, pc * TE:(pc + 1) * TE].rearrange("p (c t) -> p c t", t=64),
                            lhsT=(ident if SDT == F32 else ident_bf)[pb:pb + 64, pb:pb + 64],
                            rhs=AR[pb:pb + 64, pc, :, 1, :], start=True, stop=True), reads=[ident, ident_bf, AR], writes=[psr2])
                    P.op("act", lambda e, par=par, psr2=psr2: e.copy(
                        out=RtL[:].rearrange("p (a q) t -> p a q t", q=2)[:, :, par, :],
                        in_=psr2[:64, 0:4 * TE].rearrange("p (a t) -> p a t", t=TE)), reads=[psr2], writes=[RtL])
                yield

            def phase_b(te):
                ts = slice(te * TE, (te + 1) * TE)
                xt = xt_tiles[te % 2]
                tt = te // (TT // TE)
                gens = [chunk_pipeline(ch, g, te) for ch in range(NCHT) for g in range(2)]
                alive = True
                while alive:
                    alive = False
                    for gi in gens:
                        try:
                            next(gi)
                            alive = True
                        except StopIteration:
                            pass
                    yield
                scur[0] += NCHT
                P.op("act", lambda e: e.activation(out=ysq[:], in_=Yb[:], func=AF.Square), reads=[Yb], writes=[ysq])
                NH = 512 // TE
                ps1 = [P.next_psum() for _ in range(8 // NH)]
                for hd in range(8):
                    P.op("pe", lambda e, hd=hd: e.matmul(ps1[hd // NH][:64, (hd % NH) * TE:(hd % NH + 1) * TE], lhsT=ones_f[0:64, 0:64],
                                                        rhs=Yb[:, hd, :], start=True, stop=True), reads=[ones_f, Yb], writes=[ps1[hd // NH]])
                for b in range(8 // NH):
                    P.op("act", lambda e, b=b: e.activation(out=mean[:, b * NH:(b + 1) * NH, :],
                                                            in_=ps1[b][:64, :].rearrange("p (h t) -> p h t", t=TE),
                                                            func=AF.Copy, scale=1.0 / 64), reads=[ps1[b]], writes=[mean])
                yield
                ps2 = [P.next_psum() for _ in range(8 // NH)]
                for hd in range(8):
                    P.op("pe", lambda e, hd=hd: e.matmul(ps2[hd // NH][:64, (hd % NH) * TE:(hd % NH + 1) * TE], lhsT=ones_f[0:64, 0:64],
                                                        rhs=ysq[:, hd, :], start=True, stop=True), reads=[ones_f, ysq], writes=[ps2[hd // NH]])
                P.op("act", lambda e: e.activation(out=ysq[:], in_=mean[:], func=AF.Square), reads=[mean], writes=[ysq])
                for b in range(8 // NH):
                    P.op("dve", lambda e, b=b: e.scalar_tensor_tensor(
                        out=var[:, b * NH:(b + 1) * NH, :], in0=ps2[b][:64, :].rearrange("p (h t) -> p h t", t=TE), scalar=1.0 / 64,
                        in1=ysq[:, b * NH:(b + 1) * NH, :], op0=ALU.mult, op1=ALU.subtract), reads=[ps2[b], ysq], writes=[var])
                P.op("act", lambda e: e.activation(out=var[:], in_=var[:], func=AF.Ln, bias=ccon[:64, 1:2]), reads=[var, ccon], writes=[var])
                P.op("act", lambda e: e.activation(out=var[:], in_=var[:], func=AF.Exp, scale=-0.5), reads=[var], writes=[var])
                P.op("pool", lambda e: e.tensor_tensor(out=ycb[:], in0=Yb[:], in1=mean[:], op=ALU.subtract), reads=[Yb, mean], writes=[ycb])
                P.op("pool", lambda e: e.tensor_tensor(out=ycb[:], in0=ycb[:], in1=var[:], op=ALU.mult), reads=[ycb, var], writes=[ycb])
                for hd in range(8):
                    P.op("dve", lambda e, hd=hd: e.tensor_scalar(out=ycb[:, hd, :], in0=ycb[:, hd, :], scalar1=hcols[:, hd:hd + 1],
                                                                 scalar2=hcols[:, 8 + hd:9 + hd], op0=ALU.mult, op1=ALU.add),
                         reads=[ycb, hcols], writes=[ycb])
                for par in range(2):
                    pb = par * 64
                    psbb = P.next_psum()
                    for pc in range(4):
                        P.op("pe", lambda e, pc=pc, pb=pb, psbb=psbb: e.matmul(
                            psbb[:64, pc * TE:(pc + 1) * TE], lhsT=ident[pb:pb + 64, pb:pb + 64],
                            rhs=bonus[pb:pb + 64, pc, :], start=True, stop=True), reads=[ident, bonus], writes=[psbb])
                    P.op("dve", lambda e, par=par, psbb=psbb: e.tensor_tensor(
                        out=ycb[:].rearrange("p (a q) t -> p a q t", q=2)[:, :, par, :],
                        in0=psbb[:64, 0:4 * TE].rearrange("p (a t) -> p a t", t=TE),
                        in1=ycb[:].rearrange("p (a q) t -> p a q t", q=2)[:, :, par, :], op=ALU.add),
                        reads=[psbb, ycb], writes=[ycb])
                yield
                psg = [P.next_psum() for _ in range(8 // NH)]
                for hd in range(8):
                    P.op("pe", lambda e, hd=hd: e.matmul(psg[hd // NH][:64, (hd % NH) * TE:(hd % NH + 1) * TE],
                                                        lhsT=g_up[:, hd * 64:(hd + 1) * 64], rhs=sdg[:], start=True, stop=True),
                         reads=[g_up, sdg], writes=[psg[hd // NH]])
                for b in range(8 // NH):
                    P.op("dve", lambda e, b=b: e.tensor_tensor(out=yfin[:, b * NH:(b + 1) * NH, :],
                                                               in0=psg[b][:64, :].rearrange("p (h t) -> p h t", t=TE),
                                                               in1=ycb[:, b * NH:(b + 1) * NH, :], op=ALU.mult),
                         reads=[psg[b], ycb], writes=[yfin])
                yield
                for o in range(NC8):
                    wc = woc_t[ocnt[0] % 2]
                    wr = wor_t[ocnt[0] % 2]
                    ocnt[0] += 1
                    P.dma("sp", wc, wc[:], woc_bf, woc_bf.t[o])
                    P.dma("sp", wr, wr[:], wor_bf, wor_bf.t[o])
                    ps = P.next_psum()
                    for pc in range(4):
                        P.op("pe", lambda e, pc=pc, wc=wc, ps=ps: e.matmul(ps[:, :TE], lhsT=wc[:, pc, :], rhs=yconv[:, pc, :],
                                                                          start=(pc == 0), stop=False), reads=[wc, yconv], writes=[ps])
                    for hd in range(8):
                        P.op("pe", lambda e, hd=hd, wr=wr, ps=ps: e.matmul(ps[:, :TE], lhsT=wr[:, hd, :], rhs=yfin[:, hd, :],
                                                                          start=False, stop=(hd == 7)), reads=[wr, yfin], writes=[ps])
                    P.op("dve", lambda e, o=o, ps=ps, xt=xt: e.tensor_tensor(out=xt[:, o, :], in0=ps[:, :TE], in1=xt[:, o, :], op=ALU.add),
                         reads=[ps, xt], writes=[xt])
                    if o % 2 == 1:
                        yield
                P.dma("pool", XTb[tt], XTv[:, :, ts], xt, xt[:])

            HSETS = [hset0, hset1]

            def bind(k):
                nonlocal AR, Bt, Kt, BhT, KhT, bonus, wCT, RtL, sdg, yconv, PR
                (AR, Bt, Kt, BhT, KhT, bonus, wCT, RtL, sdg, yconv, PR) = HSETS[k]

            def step(gen, k):
                bind(k)
                try:
                    next(gen)
                    return True
                except StopIteration:
                    return False

            ga = phase_a(0)
            while step(ga, 0):
                pass
            for te in range(NTE):
                gb_ = phase_b(te)
                ga = phase_a(te + 1) if te + 1 < NTE else None
                alive_b = True
                alive_a = ga is not None
                it = 0
                while alive_b or alive_a:
                    if alive_b:
                        alive_b = step(gb_, te % 2)
                    it += 1
                    if alive_a and (it % cfg.a_every == 0 or not alive_b):
                        alive_a = step(ga, (te + 1) % 2)
            P.barrier()
            A.reset(m0)

        seq = []
        for layer in range(cfg.layers):
            seq.append(("ffn", layer, 0))
            seq.append(("mix", layer, 0))
            seq.append(("ffn", layer, 1))
        if cfg.stop is not None:
            seq = seq[:seq.index(cfg.stop) + 1]
        if cfg.skip_ffn:
            seq = [s for s in seq if s[0] != "ffn"]
        if cfg.seq is not None:
            seq = list(cfg.seq)
        ffns = [s for s in seq if s[0] == "ffn"]
        if ffns:
            cast_weights(ffns[0][1], ffns[0][2])
        rope_tables()
        transpose_in()
        evens = [s for s in seq if s[0] == "mix" and s[1] % 2 == 0]
        if evens:
            cast_even(evens[0][1] // 2)
        evens_pending = evens[1:]
        for s in seq:
            if s[0] == "ffn":
                k = ffns.index(s)
                if k + 1 < len(ffns):
                    cast_weights(ffns[k + 1][1], ffns[k + 1][2])
                ffn_phase(s[1], s[2])
            else:
                if s[1] % 2 == 1:
                    if evens_pending:
                        cast_even(evens_pending.pop(0)[1] // 2)
                    mla_layer(s[1])
                else:
                    while evens_pending and evens_pending[0][1] <= s[1]:
                        cast_even(evens_pending.pop(0)[1] // 2)
                    even_layer(s[1])

        P.pspool = None
        yin_tiles = [A.alloc("yin%d" % i, [NC8, 128], F32) for i in range(2)]
        yo_tiles = [A.alloc("yo%d" % i, [D], F32) for i in range(2)]
        for b in range(NB):
            yi = yin_tiles[b % 2]
            yo = yo_tiles[b % 2]
            P.dma("sp", yi, yi[:], XTb[b // 4], XTv[:, :, b * 128:(b + 1) * 128])
            for half in range(2):
                ps = P.next_psum()
                for jj in range(4):
                    c = half * 4 + jj
                    P.op("pe", lambda e, ps=ps, yi=yi, c=c, jj=jj: e.transpose(
                        out=ps[:, jj * 128:(jj + 1) * 128], in_=yi[:, c, :], identity=ident[:]),
                        reads=[yi, ident], writes=[ps])
                if half:
                    P.op("act", lambda e, ps=ps, yo=yo, half=half: e.copy(
                        out=yo[:, half * 512:(half + 1) * 512], in_=ps[:]), reads=[ps], writes=[yo])
                else:
                    P.op("dve", lambda e, ps=ps, yo=yo, half=half: e.tensor_copy(
                        out=yo[:, half * 512:(half + 1) * 512], in_=ps[:]), reads=[ps], writes=[yo])
            P.dma("pool", y_out, y_out.t[b * 128:(b + 1) * 128, :], yo, yo[:])
        P.wait_all("pool", [y_out])
        P.wait_all("sp", [y_out])
        P.emit()
    return nc


def _perm_swap(n=64):
    return np.concatenate([np.arange(n // 2, n), np.arange(0, n // 2)])


def host_prep(inputs):
    out = {}
    for k in ("norm_gains", "ffn_w_gate", "ffn_w_up", "ffn_w_down", "mla_w_kv_up", "odd_w_out",
              "mla_q_a_norm", "mla_kv_a_norm", "mla_q_norm", "mla_k_norm",
              "even_w_in", "even_w_out", "even_conv_w", "even_mu_shift", "rwkv_w0", "rwkv_a0", "rwkv_k_k", "rwkv_k_a",
              "rwkv_ln_w", "rwkv_ln_b", "rwkv_w_up", "rwkv_a_up", "rwkv_g_up"):
        out[k] = np.ascontiguousarray(inputs[k], dtype=np.float32)
    out["rwkv_r_k"] = np.ascontiguousarray(np.asarray(inputs["rwkv_r_k"], dtype=np.float32).reshape(2, 512))
    sw = _perm_swap(64)
    win = np.asarray(inputs["odd_w_in"], dtype=np.float32)
    out["odd_w_in_p"] = np.ascontiguousarray(np.concatenate([win, win[:, :, 640:704][:, :, sw]], axis=2))
    wq = np.asarray(inputs["mla_w_q_up"], dtype=np.float32).reshape(2, QR, 8, 192)
    wqp = np.concatenate([wq, wq[:, :, :, 128:192][:, :, :, sw]], axis=3)
    out["mla_w_q_up_p"] = np.ascontiguousarray(wqp.reshape(2, QR, 2048))
    qk = np.zeros((2, 128, 6), np.float32)
    for j in range(2):
        gq = np.asarray(inputs["mla_q_norm"][j], dtype=np.float32)
        gk = np.asarray(inputs["mla_k_norm"][j], dtype=np.float32)
        qk[j, :, 0] = gq[:128]
        qk[j, :64, 1] = gq[128:]
        qk[j, :64, 2] = gq[128:][sw]
        qk[j, :, 3] = gk[:128]
        qk[j, :64, 4] = gk[128:]
        qk[j, :64, 5] = gk[128:][sw]
    out["qk_cols"] = qk
    inv_freq = (np.float32(10000.0) ** (-np.arange(0, 64, 2, dtype=np.float32) / np.float32(64))).astype(np.float32)
    rc = np.zeros((64, 2), np.float32)
    rc[:, 0] = np.concatenate([inv_freq, inv_freq])
    rc[:32, 1] = -1.0
    rc[32:, 1] = 1.0
    out["rope_c"] = rc
    return out


def make_in_maps(inputs, ncores, T):
    shared = host_prep(inputs)
    maps = []
    for i in range(ncores):
        m = dict(shared)
        m["x"] = np.ascontiguousarray(inputs["x"][i, :T], dtype=np.float32)
        m["positions"] = np.ascontiguousarray(inputs["positions"][i:i + 1, :T]).astype(np.int32)
        maps.append(m)
    return maps


def kernel(**inputs):
    cfg = Cfg()
    nc = build(cfg)
    in_maps = make_in_maps(inputs, 8, cfg.T)
    res = run_bass_kernel_spmd(nc, in_maps, core_ids=list(range(8)))
    return np.stack([np.asarray(r["y"]) for r in res.results], axis=0).astype(np.float32)
```

```python
import contextlib
import numpy as np
import concourse.bass as bass
import concourse.mybir as mybir
from concourse.bass_utils import run_bass_kernel_spmd

F32 = mybir.dt.float32
BF16 = mybir.dt.bfloat16
I32 = mybir.dt.int32
AF = mybir.ActivationFunctionType
ALU = mybir.AluOpType

D = 1024
DFF = 2816
NM = DFF // 128
NC8 = D // 128
EPS = 1e-6
TT = 512


class Buf:
    __slots__ = ("name", "w", "r", "dsem", "dcnt", "t", "kind")

    def __init__(self, name, t=None, kind="x"):
        self.name = name
        self.kind = kind
        self.w = None
        self.r = {}
        self.dsem = None
        self.dcnt = 0
        self.t = t

    def __getitem__(self, idx):
        return self.t[idx]


ENGS = ("pe", "act", "dve", "pool", "sp")


class _Rec:
    def __init__(self):
        self.call = None

    def __getattr__(self, name):
        def f(*a, **k):
            assert self.call is None
            self.call = (name, a, k)
            return None
        return f


class Prog:
    def __init__(self, nc, stack):
        self.nc = nc
        self.stack = stack
        self.q = {e: [] for e in ENGS}
        self.sems = []
        self.esem = {}
        for e in ENGS:
            self.esem[e] = self.new_sem("e_" + e)
        self.cnt = {e: 0 for e in ENGS}
        self.seen = {e: {} for e in ENGS}
        self.nbuf = 0
        self.psum_rr = 0
        self.psums = []
        self.same_eng_sync = True
        self.dbufs = []
        self.free_dsems = [[], []]
        self.pe_mode = None
        self.pe_drain = False
        self.dsem_issued = {}
        self.pspool = None

    def new_sem(self, name):
        s = self.stack.enter_context(self.nc.semaphore(name))
        self.sems.append(s)
        return len(self.sems) - 1

    def sb(self, name, shape, dtype):
        t = self.stack.enter_context(self.nc.sbuf_tensor(name, list(shape), dtype))
        return Buf(name, t, "sb")

    def ps(self, name, shape, dtype=F32):
        t = self.stack.enter_context(self.nc.psum_tensor(name, list(shape), dtype))
        return Buf(name, t, "ps")

    def dram(self, name, shape, dtype, kind="Internal"):
        t = self.nc.dram_tensor(name, list(shape), dtype, kind=kind)
        return Buf(name, t.ap(), "dram")

    def view(self, b, name):
        return Buf(name, b.t, b.kind)

    def op(self, eng, fn, reads=(), writes=(), dma=None):
        waits = {}
        seen = self.seen[eng]

        def need(tok):
            if tok is None:
                return
            s, v = tok
            if eng == "pe" and s == self.esem["pe"]:
                return
            if (not self.same_eng_sync) and s == self.esem[eng]:
                return
            if s in self.dsem_issued:
                v = max(v, self.dsem_issued[s])
            if seen.get(s, 0) >= v:
                return
            if waits.get(s, 0) < v:
                waits[s] = v

        for b in reads:
            need(b.w)
            if b.kind == "ps":
                for s, v in b.r.items():
                    if s != self.esem[eng]:
                        need((s, v))
        for b in writes:
            need(b.w)
            for s, v in b.r.items():
                need((s, v))
        for s, v in waits.items():
            seen[s] = v
        if dma is not None:
            kind = 1 if eng == "pool" else 0
            if dma.dsem is None:
                dma.dsem = [None, None]
                dma.dcnt = [0, 0]
                self.dbufs.append(dma)
            if dma.dsem[kind] is None:
                if self.free_dsems[kind]:
                    dma.dsem[kind], dma.dcnt[kind] = self.free_dsems[kind].pop()
                else:
                    dma.dsem[kind] = self.new_sem("d%d_%s" % (len(self.sems), dma.name))
            dma.dcnt[kind] += 16
            self.dsem_issued[dma.dsem[kind]] = dma.dcnt[kind]
            tok = (dma.dsem[kind], dma.dcnt[kind])
            inc = 16
        else:
            self.cnt[eng] += 1
            tok = (self.esem[eng], self.cnt[eng])
            inc = 1
        for b in reads:
            if b.r.get(tok[0], 0) < tok[1]:
                b.r[tok[0]] = tok[1]
        for b in writes:
            b.w = tok
            b.r = {}
        rec = _Rec()
        fn(rec)
        assert rec.call is not None
        if eng == "pe":
            st = rec.call[2].get("lhsT", rec.call[2].get("in_"))
            shp = list(st.shape)
            rnd = lambda v: 32 if v <= 32 else (64 if v <= 64 else 128)
            mode = (rnd(shp[0]), rnd(int(np.prod(shp[1:]))))
            if mode != self.pe_mode:
                if self.pe_mode is not None and self.pe_drain:
                    self.q[eng].append(([], ("drain", (), {}), None, 0))
                self.pe_mode = mode
        self.q[eng].append((list(waits.items()), rec.call, tok[0], inc))
        return tok

    def wait_all(self, eng, bufs):
        waits = {}
        for b in bufs:
            toks = [b.w] + list(b.r.items())
            for tok in toks:
                if tok is None:
                    continue
                s, v = tok
                if waits.get(s, 0) < v:
                    waits[s] = v
        self.q[eng].append((list(waits.items()), None, None, 0))

    def emit(self):
        nc = self.nc
        with nc.allow_non_contiguous_dma(reason="small param loads"), nc.Block() as block:
            def run(ename):
                def body(eng):
                    for waits, fn, s, inc in self.q[ename]:
                        for ws, wv in waits:
                            eng.wait_ge(self.sems[ws], wv)
                        if fn is not None:
                            name, a, k = fn
                            ins = getattr(eng, name)(*a, **k)
                            if s is not None:
                                ins.then_inc(self.sems[s], inc)
                return body

            block.tensor(run("pe"))
            block.scalar(run("act"))
            block.vector(run("dve"))
            block.gpsimd(run("pool"))
            block.sync(run("sp"))

    def dma(self, eng, out_b, out_ap, in_b, in_ap, sem_b=None):
        if sem_b is None:
            sem_b = out_b if out_b.kind == "sb" else in_b
            assert sem_b.kind == "sb"
        return self.op(eng, lambda e: e.dma_start(out=out_ap, in_=in_ap),
                       reads=[in_b], writes=[out_b], dma=sem_b)

    def next_psum(self):
        pool = self.pspool if self.pspool is not None else self.psums
        b = pool[self.psum_rr % len(pool)]
        self.psum_rr += 1
        return b

    def barrier(self):
        snap = {}
        for e in ENGS:
            if self.cnt[e] > 0:
                snap[self.esem[e]] = self.cnt[e]
        for b in self.dbufs:
            for kind in (0, 1):
                if b.dsem[kind] is not None:
                    snap[b.dsem[kind]] = b.dcnt[kind]
        for e in ENGS:
            waits = [(s, v) for s, v in snap.items() if self.seen[e].get(s, 0) < v]
            for s, v in waits:
                self.seen[e][s] = v
            self.q[e].append((waits, None, None, 0))


class Arena:
    def __init__(self, P, words):
        self.P = P
        self.t = P.stack.enter_context(P.nc.sbuf_tensor("arena", [128, words], F32))
        self.off = 0
        self.words = words
        self.peak = 0
        self.live = []

    def mark(self):
        return self.off

    def reset(self, m):
        keep = []
        for off, b in self.live:
            if off >= m:
                if b.dsem is not None:
                    for kind in (0, 1):
                        if b.dsem[kind] is not None:
                            self.P.free_dsems[kind].append((b.dsem[kind], b.dcnt[kind]))
                    self.P.dbufs.remove(b)
                    b.dsem = None
            else:
                keep.append((off, b))
        self.live = keep
        self.off = m

    def alloc(self, name, free, dtype=F32, parts=128):
        free = list(free)
        n = int(np.prod(free))
        four = dtype in (F32, I32)
        w = n if four else (n + 1) // 2
        wal = (w + 7) // 8 * 8
        assert self.off + wal <= self.words, "arena overflow %s: %d + %d > %d" % (name, self.off, wal, self.words)
        a = self.t[0:parts, self.off:self.off + w]
        if dtype != F32:
            a = a.bitcast(dtype)
        if len(free) > 1:
            names = ["a%d" % i for i in range(len(free))]
            a = a.rearrange("p (%s) -> p %s" % (" ".join(names), " ".join(names)),
                            **{nm: v for nm, v in zip(names, free)})
        b = Buf(name, a, "sb")
        self.live.append((self.off, b))
        self.off += wal
        self.peak = max(self.peak, self.off)
        return b


QR = 384
KVR = 256
ATTN_SCALE = 192 ** -0.5
ARENA_WORDS = 48000


class Cfg:
    def __init__(self, T=4096, layers=4, stop=None, skip_ffn=False, seq=None, upto=None):
        self.seq = seq
        self.upto = upto
        self.T = T
        self.layers = layers
        self.stop = stop
        self.skip_ffn = skip_ffn


def build(cfg):
    T = cfg.T
    NT = T // TT
    NB = T // 128
    nc = bass.Bass("TRN2", target_bir_lowering=False)
    stack = contextlib.ExitStack()
    with stack:
        P = Prog(nc, stack)
        A = Arena(P, ARENA_WORDS)

        def dram_in(name, shape, dt=F32):
            return P.dram(name, shape, dt, kind="ExternalInput")

        x_in = dram_in("x", [T, D])
        pos_in = dram_in("positions", [1, T], I32)
        gains_in = dram_in("norm_gains", [4, 3, D])
        wg_in = dram_in("ffn_w_gate", [4, 2, D, DFF])
        wu_in = dram_in("ffn_w_up", [4, 2, D, DFF])
        wd_in = dram_in("ffn_w_down", [4, 2, DFF, D])
        owin_in = dram_in("odd_w_in_p", [2, D, 768])
        wq_in = dram_in("mla_w_q_up_p", [2, QR, 2048])
        wkv_in = dram_in("mla_w_kv_up", [2, KVR, 2048])
        owout_in = dram_in("odd_w_out", [2, D, D])
        qa_in = dram_in("mla_q_a_norm", [2, QR])
        kva_in = dram_in("mla_kv_a_norm", [2, KVR])
        qkc_in = dram_in("qk_cols", [2, 128, 6])
        qn_in = dram_in("mla_q_norm", [2, 192])
        kn_in = dram_in("mla_k_norm", [2, 192])
        ropec_in = dram_in("rope_c", [64, 2])
        ewin_in = dram_in("even_w_in", [2, D, 3232])
        ewout_in = dram_in("even_w_out", [2, D, D])
        cw_in = dram_in("even_conv_w", [2, 3, 512])
        emu_in = dram_in("even_mu_shift", [2, 1696])
        w0_in = dram_in("rwkv_w0", [2, 512])
        a0_in = dram_in("rwkv_a0", [2, 512])
        kk_in = dram_in("rwkv_k_k", [2, 512])
        ka_in = dram_in("rwkv_k_a", [2, 512])
        rk_in = dram_in("rwkv_r_k", [2, 512])
        lnw_in = dram_in("rwkv_ln_w", [2, 512])
        lnb_in = dram_in("rwkv_ln_b", [2, 512])
        wup_in = dram_in("rwkv_w_up", [2, 32, 512])
        aup_in = dram_in("rwkv_a_up", [2, 32, 512])
        gup_in = dram_in("rwkv_g_up", [2, 96, 512])
        y_out = P.dram("y", [T, D], F32, kind="ExternalOutput")

        XT = P.dram("XT", [D, T], F32)
        XTv = XT.t.rearrange("(c p) t -> p c t", p=128)
        XTb = [P.view(XT, "XT%d" % i) for i in range(NT)]
        wg_bfs = [P.dram("wg_bf%d" % i, [NM, 128, NC8, 128], BF16) for i in range(2)]
        wu_bfs = [P.dram("wu_bf%d" % i, [NM, 128, NC8, 128], BF16) for i in range(2)]
        wd_bfs = [P.dram("wd_bf%d" % i, [NC8, 128, NM, 128], BF16) for i in range(2)]
        castsems = [Buf("castsem%d" % i) for i in range(2)]
        we_bfs = [P.dram("we_bf%d" % i, [24, 128, NC8, 128], BF16) for i in range(2)]
        wl_bfs = [P.dram("wl_bf%d" % i, [128, NC8, 160], BF16) for i in range(2)]
        woc_bfs = [P.dram("woc_bf%d" % i, [NC8, 128, 4, 128], BF16) for i in range(2)]
        wor_bfs = [P.dram("wor_bf%d" % i, [NC8, 64, 8, 128], BF16) for i in range(2)]
        ecastsem = [Buf("ecastsem%d" % i) for i in range(2)]
        OTd = P.dram("OTd", [8, 128, T], BF16)
        OTb = [P.view(OTd, "OT%d" % i) for i in range(NT)]

        for i in range(8):
            P.psums.append(P.ps("ps%d" % i, [128, 512], F32))

        ident = A.alloc("ident", [128], F32)
        ones_bf = A.alloc("ones_bf", [128], BF16)
        ident_bf = A.alloc("ident_bf", [128], BF16)
        ones_f = A.alloc("ones_f", [128], F32)
        tri_bf = A.alloc("tri_bf", [128], BF16)
        gains_sb = A.alloc("gains_sb", [12, NC8], F32)
        epsb = A.alloc("epsb", [1], F32)
        ropec = A.alloc("ropec", [2], F32, parts=64)
        ropeD = P.dram("ropeD", [2, 64, T], F32)

        P.op("pool", lambda e: e.memset(ones_bf[:], 1.0), writes=[ones_bf])
        P.op("pool", lambda e: e.memset(ones_f[:], 1.0), writes=[ones_f])
        P.op("pool", lambda e: e.memset(epsb[:], EPS), writes=[epsb])
        P.op("pool", lambda e: e.memset(ident[:], 0.0), writes=[ident])
        P.op("pool", lambda e: e.affine_select(out=ident[:], in_=ident[:], pattern=[[-1, 128]],
                                               compare_op=ALU.not_equal, fill=1.0, base=0,
                                               channel_multiplier=1),
             reads=[ident], writes=[ident])
        P.op("pool", lambda e: e.tensor_copy(out=ident_bf[:], in_=ident[:]), reads=[ident], writes=[ident_bf])
        P.op("pool", lambda e: e.memset(tri_bf[:], 1.0), writes=[tri_bf])
        P.op("pool", lambda e: e.affine_select(out=tri_bf[:], in_=tri_bf[:], pattern=[[1, 128]],
                                               compare_op=ALU.is_ge, fill=0.0, base=0,
                                               channel_multiplier=-1),
             reads=[tri_bf], writes=[tri_bf])
        P.dma("sp", gains_sb, gains_sb[:], gains_in, gains_in.t.rearrange("l s (c p) -> p (l s) c", p=128))
        P.dma("sp", ropec, ropec[:], ropec_in, ropec_in.t)

        def rope_tables():
            m = A.mark()
            cos2 = A.alloc("cos2t", [T], F32, parts=64)
            ssin2 = A.alloc("ssin2t", [T], F32, parts=64)
            posi = A.alloc("posi", [T], I32, parts=64)
            ang = A.alloc("ang", [T], F32, parts=64)
            kf = A.alloc("kf", [T], F32, parts=64)
            ki = A.alloc("ki", [T], I32, parts=64)
            rr = A.alloc("rr", [T], F32, parts=64)
            msk = A.alloc("msk", [T], F32, parts=64)
            P.dma("sp", posi, posi[:], pos_in, pos_in.t.partition_broadcast(64))
            P.op("dve", lambda e: e.tensor_copy(out=ang[:], in_=posi[:]), reads=[posi], writes=[ang])
            P.op("dve", lambda e: e.tensor_scalar(out=ang[:], in0=ang[:], scalar1=ropec[:, 0:1], scalar2=None,
                                                  op0=ALU.mult), reads=[ang, ropec], writes=[ang])
            TWO_PI = 2.0 * np.pi
            c1 = float(np.float32(6.28125))
            c2 = float(np.float32(TWO_PI - 6.28125))
            c3 = float(TWO_PI - c1 - c2)
            P.op("dve", lambda e: e.tensor_scalar(out=kf[:], in0=ang[:], scalar1=float(1.0 / TWO_PI), scalar2=None,
                                                  op0=ALU.mult), reads=[ang], writes=[kf])
            P.op("dve", lambda e: e.tensor_copy(out=ki[:], in_=kf[:]), reads=[kf], writes=[ki])
            P.op("dve", lambda e: e.tensor_copy(out=kf[:], in_=ki[:]), reads=[ki], writes=[kf])
            for cc in (c1, c2, c3):
                P.op("dve", lambda e, cc=cc: e.scalar_tensor_tensor(out=ang[:], in0=kf[:], scalar=-cc, in1=ang[:],
                                                                   op0=ALU.mult, op1=ALU.add),
                     reads=[kf, ang], writes=[ang])

            def wrap(dst, src, shift):
                P.op("dve", lambda e: e.tensor_scalar(out=dst[:], in0=src[:], scalar1=float(shift), scalar2=None,
                                                      op0=ALU.add), reads=[src], writes=[dst])
                P.op("dve", lambda e: e.tensor_single_scalar(out=msk[:], in_=dst[:], scalar=float(np.pi), op=ALU.is_gt),
                     reads=[dst], writes=[msk])
                P.op("dve", lambda e: e.scalar_tensor_tensor(out=dst[:], in0=msk[:], scalar=-TWO_PI, in1=dst[:],
                                                             op0=ALU.mult, op1=ALU.add), reads=[msk, dst], writes=[dst])
                P.op("dve", lambda e: e.tensor_single_scalar(out=msk[:], in_=dst[:], scalar=float(-np.pi), op=ALU.is_lt),
                     reads=[dst], writes=[msk])
                P.op("dve", lambda e: e.scalar_tensor_tensor(out=dst[:], in0=msk[:], scalar=TWO_PI, in1=dst[:],
                                                             op0=ALU.mult, op1=ALU.add), reads=[msk, dst], writes=[dst])
                P.op("dve", lambda e: e.tensor_scalar(out=dst[:], in0=dst[:], scalar1=float(np.pi), scalar2=float(-np.pi),
                                                      op0=ALU.min, op1=ALU.max), reads=[dst], writes=[dst])

            wrap(rr, ang, 0.0)
            P.op("act", lambda e: e.activation(out=ssin2[:], in_=rr[:], func=AF.Sin), reads=[rr], writes=[ssin2])
            P.op("dve", lambda e: e.tensor_scalar(out=ssin2[:], in0=ssin2[:], scalar1=ropec[:, 1:2], scalar2=None,
                                                  op0=ALU.mult), reads=[ssin2, ropec], writes=[ssin2])
            wrap(kf, rr, np.pi / 2)
            P.op("act", lambda e: e.activation(out=cos2[:], in_=kf[:], func=AF.Sin), reads=[kf], writes=[cos2])
            P.dma("sp", ropeD, ropeD.t[0], cos2, cos2[:])
            P.dma("sp", ropeD, ropeD.t[1], ssin2, ssin2[:])
            P.barrier()
            A.reset(m)

        rope_tables()

        def transpose_in():
            m = A.mark()
            xin_tiles = [A.alloc("xin%d" % i, [D], F32) for i in range(2)]
            xtr_tiles = [A.alloc("xtr%d" % i, [NC8, 128], F32) for i in range(2)]
            for b in range(NB):
                xi = xin_tiles[b % 2]
                xo = xtr_tiles[b % 2]
                P.dma("sp", xi, xi[:], x_in, x_in.t[b * 128:(b + 1) * 128, :])
                for half in range(2):
                    ps = P.next_psum()
                    for j in range(4):
                        c = half * 4 + j
                        P.op("pe", lambda e, ps=ps, xi=xi, c=c, j=j: e.transpose(
                            out=ps[:, j * 128:(j + 1) * 128], in_=xi[:, c * 128:(c + 1) * 128], identity=ident[:]),
                            reads=[xi, ident], writes=[ps])
                    if half:
                        P.op("act", lambda e, ps=ps, xo=xo, half=half: e.copy(
                            out=xo[:, half * 4:(half + 1) * 4, :], in_=ps[:].rearrange("p (j t) -> p j t", j=4)),
                            reads=[ps], writes=[xo])
                    else:
                        P.op("dve", lambda e, ps=ps, xo=xo, half=half: e.tensor_copy(
                            out=xo[:, half * 4:(half + 1) * 4, :], in_=ps[:].rearrange("p (j t) -> p j t", j=4)),
                            reads=[ps], writes=[xo])
                P.dma("pool", XTb[b // 4], XTv[:, :, b * 128:(b + 1) * 128], xo, xo[:])
            P.barrier()
            A.reset(m)

        transpose_in()

        def rmsnorm_tile(xt, h, gidx, sq_tile, rstd_t, w=TT):
            P.op("act", lambda e: e.activation(out=sq_tile[:], in_=xt[:], func=AF.Square),
                 reads=[xt], writes=[sq_tile])
            ps = P.next_psum()
            for c in range(NC8):
                P.op("pe", lambda e, c=c: e.matmul(ps[:, :w], lhsT=ones_bf[:], rhs=sq_tile[:, c, :],
                                                  start=(c == 0), stop=(c == NC8 - 1)),
                     reads=[ones_bf, sq_tile], writes=[ps])
            P.op("act", lambda e: e.activation(out=rstd_t[:], in_=ps[:, :w], func=AF.Sqrt,
                                               bias=epsb[:], scale=1.0 / D),
                 reads=[ps, epsb], writes=[rstd_t])
            P.op("dve", lambda e: e.reciprocal(out=rstd_t[:], in_=rstd_t[:]), reads=[rstd_t], writes=[rstd_t])
            for c in range(NC8):
                P.op("dve", lambda e, c=c: e.scalar_tensor_tensor(
                    out=h[:, c, :], in0=xt[:, c, :], scalar=gains_sb[:, gidx, c:c + 1], in1=rstd_t[:],
                    op0=ALU.mult, op1=ALU.mult), reads=[xt, gains_sb, rstd_t], writes=[h])

        def rstd_from_ss(dst, ps, n, parts=128):
            P.op("act", lambda e: e.activation(out=dst[:parts], in_=ps[:parts, :TT], func=AF.Ln,
                                               bias=epsb[:parts], scale=1.0 / n),
                 reads=[ps, epsb], writes=[dst])
            P.op("act", lambda e: e.activation(out=dst[:parts], in_=dst[:parts], func=AF.Exp, scale=-0.5),
                 reads=[dst], writes=[dst])

        def cast_weights(layer, which):
            par = (layer * 2 + which) % 2
            wg_bf, wu_bf, wd_bf, castsem = wg_bfs[par], wu_bfs[par], wd_bfs[par], castsems[par]
            for m in range(NM):
                P.op("pool", lambda e, m=m: e.dma_start(
                    out=wg_bf.t[m], in_=wg_in.t[layer, which].rearrange("(c p) f -> p c f", p=128)[:, :, m * 128:(m + 1) * 128]),
                    reads=[wg_in], writes=[wg_bf], dma=castsem)
                P.op("pool", lambda e, m=m: e.dma_start(
                    out=wu_bf.t[m], in_=wu_in.t[layer, which].rearrange("(c p) f -> p c f", p=128)[:, :, m * 128:(m + 1) * 128]),
                    reads=[wu_in], writes=[wu_bf], dma=castsem)
            for o in range(NC8):
                P.op("pool", lambda e, o=o: e.dma_start(
                    out=wd_bf.t[o], in_=wd_in.t[layer, which].rearrange("(m p) f -> p m f", p=128)[:, :, o * 128:(o + 1) * 128]),
                    reads=[wd_in], writes=[wd_bf], dma=castsem)

        TF = 1024

        def ffn_phase(layer, which):
            m0 = A.mark()
            P.pspool = None
            NTF = T // TF
            NH2 = TF // 512
            xt_tiles = [A.alloc("fxt%d" % i, [NC8, TF], F32) for i in range(2)]
            sqr = [A.alloc("fsq%d" % i, [TF], BF16) for i in range(2)]
            h_tiles = [A.alloc("fh0", [NC8, TF], BF16)] * 2
            hid = A.alloc("hid", [NM, TF], BF16)
            rstd_t = A.alloc("frstd", [TF], F32)
            sg_tiles = [A.alloc("sg%d" % i, [TF], F32) for i in range(2)]
            NWB = 4
            wgu_tiles = [A.alloc("wgu%d" % i, [2, NC8, 128], BF16) for i in range(NWB)]
            wdn_tiles = [A.alloc("wdn%d" % i, [NM, 128], BF16) for i in range(2)]
            par = (layer * 2 + which) % 2
            wg_bf, wu_bf, wd_bf = wg_bfs[par], wu_bfs[par], wd_bfs[par]
            gidx = layer * 3 + (0 if which == 0 else 2)
            wcount = 0
            dcount = 0
            def f_load(tf):
                xt = xt_tiles[tf % 2]
                for hf in range(NH2):
                    tt = tf * NH2 + hf
                    P.dma("sp", xt, xt[:, :, hf * 512:(hf + 1) * 512], XTb[tt], XTv[:, :, tt * 512:(tt + 1) * 512])

            def f_norm(tf):
                xt = xt_tiles[tf % 2]
                h = h_tiles[tf % 2]
                pss = [P.next_psum() for _ in range(NH2)]
                for c in range(NC8):
                    sq = sqr[c % 2]
                    P.op("act", lambda e, c=c, sq=sq: e.activation(out=sq[:], in_=xt[:, c, :], func=AF.Square),
                         reads=[xt], writes=[sq])
                    for hf in range(NH2):
                        P.op("pe", lambda e, c=c, hf=hf, sq=sq: e.matmul(pss[hf][:, :512], lhsT=ones_bf[:], rhs=sq[:, hf * 512:(hf + 1) * 512],
                                                                        start=(c == 0), stop=(c == NC8 - 1)),
                             reads=[ones_bf, sq], writes=[pss[hf]])
                for hf in range(NH2):
                    P.op("act", lambda e, hf=hf: e.activation(out=rstd_t[:, hf * 512:(hf + 1) * 512], in_=pss[hf][:, :512], func=AF.Sqrt,
                                                              bias=epsb[:], scale=1.0 / D), reads=[pss[hf], epsb], writes=[rstd_t])
                P.op("dve", lambda e: e.reciprocal(out=rstd_t[:], in_=rstd_t[:]), reads=[rstd_t], writes=[rstd_t])
                for c in range(NC8):
                    P.op("dve", lambda e, c=c: e.scalar_tensor_tensor(
                        out=h[:, c, :], in0=xt[:, c, :], scalar=gains_sb[:, gidx, c:c + 1], in1=rstd_t[:],
                        op0=ALU.mult, op1=ALU.mult), reads=[xt, gains_sb, rstd_t], writes=[h])

            f_load(0)
            f_norm(0)
            for tf in range(NTF):
                xt = xt_tiles[tf % 2]
                h = h_tiles[tf % 2]
                if tf + 1 < NTF:
                    f_load(tf + 1)
                for m in range(NM):
                    w = wgu_tiles[wcount % NWB]
                    wcount += 1
                    P.dma("sp", w, w[:, 0], wg_bf, wg_bf.t[m])
                    P.dma("sp", w, w[:, 1], wu_bf, wu_bf.t[m])
                    psg = [P.next_psum() for _ in range(NH2)]
                    psu = [P.next_psum() for _ in range(NH2)]
                    for kind, pst in ((0, psg), (1, psu)):
                        for c in range(NC8):
                            for hf in range(NH2):
                                P.op("pe", lambda e, c=c, w=w, hf=hf, kind=kind, pst=pst, h=h: e.matmul(
                                    pst[hf][:, :512], lhsT=w[:, kind, c, :], rhs=h[:, c, hf * 512:(hf + 1) * 512],
                                    start=(c == 0), stop=(c == NC8 - 1)), reads=[w, h], writes=[pst[hf]])
                    sg = sg_tiles[m % 2]
                    for hf in range(NH2):
                        P.op("act", lambda e, sg=sg, hf=hf, psg=psg: e.activation(out=sg[:, hf * 512:(hf + 1) * 512], in_=psg[hf][:, :512],
                                                                                 func=AF.Silu), reads=[psg[hf]], writes=[sg])
                    for hf in range(NH2):
                        P.op("dve", lambda e, sg=sg, hf=hf, psu=psu, m=m: e.tensor_tensor(
                            out=hid[:, m, hf * 512:(hf + 1) * 512], in0=psu[hf][:, :512], in1=sg[:, hf * 512:(hf + 1) * 512], op=ALU.mult),
                            reads=[psu[hf], sg], writes=[hid])
                if tf + 1 < NTF:
                    f_norm(tf + 1)
                for o in range(NC8):
                    wd = wdn_tiles[dcount % 2]
                    dcount += 1
                    P.dma("sp", wd, wd[:], wd_bf, wd_bf.t[o])
                    pso = [P.next_psum() for _ in range(NH2)]
                    for m in range(NM):
                        for hf in range(NH2):
                            P.op("pe", lambda e, m=m, wd=wd, hf=hf, pso=pso: e.matmul(
                                pso[hf][:, :512], lhsT=wd[:, m, :], rhs=hid[:, m, hf * 512:(hf + 1) * 512],
                                start=(m == 0), stop=(m == NM - 1)), reads=[wd, hid], writes=[pso[hf]])
                    for hf in range(NH2):
                        P.op("dve", lambda e, o=o, hf=hf, pso=pso, xt=xt: e.scalar_tensor_tensor(
                            out=xt[:, o, hf * 512:(hf + 1) * 512], in0=pso[hf][:, :512], scalar=0.5, in1=xt[:, o, hf * 512:(hf + 1) * 512],
                            op0=ALU.mult, op1=ALU.add), reads=[pso[hf], xt], writes=[xt])
                for hf in range(NH2):
                    tt = tf * NH2 + hf
                    P.dma("pool", XTb[tt], XTv[:, :, tt * 512:(tt + 1) * 512], xt, xt[:, :, hf * 512:(hf + 1) * 512])
            P.barrier()
            A.reset(m0)

        def mla_layer(layer):
            j = layer // 2
            gidx = layer * 3 + 1
            m0 = A.mark()
            P.pspool = P.psums[0:4]
            cos2 = A.alloc("cos2", [T], F32, parts=64)
            ssin2 = A.alloc("ssin2", [T], F32, parts=64)
            P.dma("sp", cos2, cos2[:], ropeD, ropeD.t[0])
            P.dma("sp", ssin2, ssin2[:], ropeD, ropeD.t[1])
            cqn = A.alloc("cqn", [3, T], BF16)
            ckvn = A.alloc("ckvn", [2, T], BF16)
            kper = A.alloc("kper", [T], BF16, parts=64)
            kpsq = A.alloc("kpsq", [T], BF16, parts=64)
            wq = A.alloc("wq", [3, 2048], BF16)
            wkv = A.alloc("wkv", [2, 2048], BF16)
            cols = A.alloc("mcols", [16], F32)
            grow = A.alloc("grow", [2, 192], F32, parts=1)
            gmax = A.alloc("gmax", [4], F32, parts=1)
            P.dma("pool", wq, wq[:], wq_in, wq_in.t[j].rearrange("(c p) f -> p c f", p=128))
            P.dma("pool", wkv, wkv[:], wkv_in, wkv_in.t[j].rearrange("(c p) f -> p c f", p=128))
            P.dma("sp", cols, cols[:, 0:3], qa_in, qa_in.t[j].rearrange("(c p) -> p c", p=128))
            P.dma("sp", cols, cols[:, 3:5], kva_in, kva_in.t[j].rearrange("(c p) -> p c", p=128))
            P.dma("sp", cols, cols[:, 5:11], qkc_in, qkc_in.t[j])
            P.dma("sp", grow, grow[:, 0, :], qn_in, qn_in.t[j:j + 1, :])
            P.dma("sp", grow, grow[:, 1, :], kn_in, kn_in.t[j:j + 1, :])
            P.op("dve", lambda e: e.tensor_reduce(out=gmax[:, 0:2], in_=grow[:], axis=mybir.AxisListType.X,
                                                  op=ALU.max, apply_absolute_value=True),
                 reads=[grow], writes=[gmax])
            P.op("dve", lambda e: e.scalar_tensor_tensor(out=gmax[:, 2:3], in0=gmax[:, 0:1], scalar=-ATTN_SCALE * 192.0,
                                                         in1=gmax[:, 1:2], op0=ALU.mult, op1=ALU.mult),
                 reads=[gmax], writes=[gmax])
            P.op("dve", lambda e: e.tensor_copy(out=gmax[:, 3:4], in_=gmax[:, 2:3]), reads=[gmax], writes=[gmax])
            psb = P.next_psum()
            P.op("pe", lambda e: e.matmul(psb[:, 0:2], lhsT=ones_f[0:1, :], rhs=gmax[:, 2:4], start=True, stop=True),
                 reads=[ones_f, gmax], writes=[psb])
            P.op("dve", lambda e: e.tensor_copy(out=cols[:, 12:13], in_=psb[:, 0:1]), reads=[psb], writes=[cols])

            if cfg.upto == "m0":
                P.barrier()
                A.reset(m0)
                return
            m1 = A.mark()
            win = A.alloc("win", [NC8, 768], BF16)
            P.dma("pool", win, win[:], owin_in, owin_in.t[j].rearrange("(c p) f -> p c f", p=128))
            xt_tiles = [A.alloc("xt%d" % i, [NC8, TT], F32) for i in range(1)] * 2
            sq_tile = A.alloc("sq", [NC8, TT], BF16)
            h_tiles = [A.alloc("h%d" % i, [NC8, TT], BF16) for i in range(1)] * 2
            rstd_t = A.alloc("rstd", [TT], F32)
            c32 = A.alloc("c32", [5, TT], F32)
            csq = A.alloc("csq", [5, TT], BF16)
            rl = [A.alloc("rl%d" % i, [TT], F32) for i in range(2)]
            tA = A.alloc("tA", [TT], F32, parts=64)
            tB = A.alloc("tB", [TT], F32, parts=64)
            for tt in range(NT):
                ts = slice(tt * TT, (tt + 1) * TT)
                xt = xt_tiles[tt % 2]
                h = h_tiles[tt % 2]
                P.dma("sp", xt, xt[:], XTb[tt], XTv[:, :, ts])
                rmsnorm_tile(xt, h, gidx, sq_tile, rstd_t)
                for m in range(5):
                    if cfg.upto == "m1a0":
                        continue
                    ps = P.next_psum()
                    for c in range(NC8):
                        P.op("pe", lambda e, c=c, m=m, ps=ps, h=h: e.matmul(
                            ps[:, :TT], lhsT=win[:, c, m * 128:(m + 1) * 128], rhs=h[:, c, :],
                            start=(c == 0), stop=(c == NC8 - 1)), reads=[win, h], writes=[ps])
                    P.op("act", lambda e, m=m, ps=ps: e.activation(out=csq[:, m, :], in_=ps[:, :TT], func=AF.Square),
                         reads=[ps], writes=[csq])
                    if cfg.upto != "m1a1":
                        P.op("dve", lambda e, m=m, ps=ps: e.tensor_copy(out=c32[:, m, :], in_=ps[:, :TT]),
                             reads=[ps], writes=[c32])
                if cfg.upto in ("m1a", "m1a0", "m1a1"):
                    continue
                for (lo, hi, n, dst, cbase, ri) in ((0, 3, QR, cqn, 0, 0), (3, 5, KVR, ckvn, 3, 1)):
                    ps = P.next_psum()
                    for m in range(lo, hi):
                        P.op("pe", lambda e, m=m, ps=ps, lo=lo, hi=hi: e.matmul(
                            ps[:, :TT], lhsT=ones_bf[:], rhs=csq[:, m, :], start=(m == lo), stop=(m == hi - 1)),
                            reads=[ones_bf, csq], writes=[ps])
                    rstd_from_ss(rl[ri], ps, n)
                    for m in range(lo, hi):
                        P.op("dve", lambda e, m=m, lo=lo, dst=dst, cbase=cbase, ri=ri: e.scalar_tensor_tensor(
                            out=dst[:, m - lo, ts], in0=c32[:, m, :], scalar=cols[:, cbase + m - lo:cbase + m - lo + 1],
                            in1=rl[ri][:], op0=ALU.mult, op1=ALU.mult), reads=[c32, cols, rl[ri]], writes=[dst])
                if cfg.upto == "m1b":
                    continue
                pk = P.next_psum()
                pw = P.next_psum()
                for c in range(NC8):
                    P.op("pe", lambda e, c=c, pk=pk, h=h: e.matmul(
                        pk[:64, :TT], lhsT=win[:, c, 640:704], rhs=h[:, c, :], start=(c == 0), stop=(c == NC8 - 1)),
                        reads=[win, h], writes=[pk])
                for c in range(NC8):
                    P.op("pe", lambda e, c=c, pw=pw, h=h: e.matmul(
                        pw[:64, :TT], lhsT=win[:, c, 704:768], rhs=h[:, c, :], start=(c == 0), stop=(c == NC8 - 1)),
                        reads=[win, h], writes=[pw])
                P.op("act", lambda e, pk=pk: e.activation(out=kpsq[:, ts], in_=pk[:64, :TT], func=AF.Square),
                     reads=[pk], writes=[kpsq])
                P.op("dve", lambda e, pk=pk: e.scalar_tensor_tensor(
                    out=tA[:], in0=pk[:64, :TT], scalar=cols[:64, 9:10], in1=cos2[:, ts], op0=ALU.mult, op1=ALU.mult),
                    reads=[pk, cols, cos2], writes=[tA])
                P.op("dve", lambda e, pw=pw: e.scalar_tensor_tensor(
                    out=tB[:], in0=pw[:64, :TT], scalar=cols[:64, 10:11], in1=ssin2[:, ts], op0=ALU.mult, op1=ALU.mult),
                    reads=[pw, cols, ssin2], writes=[tB])
                P.op("dve", lambda e: e.tensor_tensor(out=kper[:, ts], in0=tA[:], in1=tB[:], op=ALU.add),
                     reads=[tA, tB], writes=[kper])
            P.barrier()
            A.reset(m1)
            if cfg.upto in ("m1", "m1a", "m1b", "m1a0", "m1a1"):
                A.reset(m0)
                return

            QTn = A.alloc("QTn", [T], BF16)
            QTr = A.alloc("QTr", [T], BF16, parts=64)
            KTn = A.alloc("KTn", [T], BF16)
            KTr = A.alloc("KTr", [T], BF16, parts=64)
            Vh = A.alloc("Vh", [NB, 128], BF16)
            sqn = A.alloc("sqn", [TT], BF16)
            sqr = A.alloc("sqr", [TT], BF16, parts=64)
            rq = A.alloc("rq", [TT], F32)
            rk = A.alloc("rk", [TT], F32)
            t1 = A.alloc("t1", [TT], F32, parts=64)
            t2 = A.alloc("t2", [TT], F32, parts=64)
            pts = [A.alloc("pt%d" % i, [TT], BF16) for i in range(4)]
            rden = A.alloc("rden", [TT], F32)
            obf = [A.alloc("obf%d" % i, [TT], BF16) for i in range(2)]
            oacc = [P.psums[4], P.psums[5]]
            dacc = [P.psums[6], P.psums[7]]
            ptc = [0]
            for hd in range(8):
                qb = hd * 256
                kb = hd * 256
                for tt in range(NT):
                    ts = slice(tt * TT, (tt + 1) * TT)
                    pqn = P.next_psum()
                    for c in range(3):
                        P.op("pe", lambda e, c=c, pqn=pqn, ts=ts, qb=qb: e.matmul(
                            pqn[:, :TT], lhsT=wq[:, c, qb:qb + 128], rhs=cqn[:, c, ts], start=(c == 0), stop=(c == 2)),
                            reads=[wq, cqn], writes=[pqn])
                    pqr = P.next_psum()
                    for c in range(3):
                        P.op("pe", lambda e, c=c, pqr=pqr, ts=ts, qb=qb: e.matmul(
                            pqr[:64, :TT], lhsT=wq[:, c, qb + 128:qb + 192], rhs=cqn[:, c, ts], start=(c == 0), stop=(c == 2)),
                            reads=[wq, cqn], writes=[pqr])
                    pqs = P.next_psum()
                    for c in range(3):
                        P.op("pe", lambda e, c=c, pqs=pqs, ts=ts, qb=qb: e.matmul(
                            pqs[:64, :TT], lhsT=wq[:, c, qb + 192:qb + 256], rhs=cqn[:, c, ts], start=(c == 0), stop=(c == 2)),
                            reads=[wq, cqn], writes=[pqs])
                    P.op("act", lambda e, pqn=pqn: e.activation(out=sqn[:], in_=pqn[:, :TT], func=AF.Square),
                         reads=[pqn], writes=[sqn])
                    P.op("act", lambda e, pqr=pqr: e.activation(out=sqr[:], in_=pqr[:64, :TT], func=AF.Square),
                         reads=[pqr], writes=[sqr])
                    pss = P.next_psum()
                    P.op("pe", lambda e, pss=pss: e.matmul(pss[:, :TT], lhsT=ones_bf[:], rhs=sqn[:], start=True, stop=False),
                         reads=[ones_bf, sqn], writes=[pss])
                    P.op("pe", lambda e, pss=pss: e.matmul(pss[:, :TT], lhsT=ones_bf[:64, :], rhs=sqr[:], start=False, stop=True),
                         reads=[ones_bf, sqr], writes=[pss])
                    rstd_from_ss(rq, pss, 192)
                    P.op("dve", lambda e, pqn=pqn, ts=ts: e.scalar_tensor_tensor(
                        out=QTn[:, ts], in0=pqn[:, :TT], scalar=cols[:, 5:6], in1=rq[:], op0=ALU.mult, op1=ALU.mult),
                        reads=[pqn, cols, rq], writes=[QTn])
                    P.op("dve", lambda e, pqr=pqr, ts=ts: e.scalar_tensor_tensor(
                        out=t1[:], in0=pqr[:64, :TT], scalar=cols[:64, 6:7], in1=cos2[:, ts], op0=ALU.mult, op1=ALU.mult),
                        reads=[pqr, cols, cos2], writes=[t1])
                    P.op("dve", lambda e, pqs=pqs, ts=ts: e.scalar_tensor_tensor(
                        out=t2[:], in0=pqs[:64, :TT], scalar=cols[:64, 7:8], in1=ssin2[:, ts], op0=ALU.mult, op1=ALU.mult),
                        reads=[pqs, cols, ssin2], writes=[t2])
                    P.op("dve", lambda e: e.tensor_tensor(out=t1[:], in0=t1[:], in1=t2[:], op=ALU.add),
                         reads=[t1, t2], writes=[t1])
                    P.op("dve", lambda e, ts=ts: e.tensor_tensor(out=QTr[:, ts], in0=t1[:], in1=rq[:64], op=ALU.mult),
                         reads=[t1, rq], writes=[QTr])
                    pkn = P.next_psum()
                    for c in range(2):
                        P.op("pe", lambda e, c=c, pkn=pkn, ts=ts, kb=kb: e.matmul(
                            pkn[:, :TT], lhsT=wkv[:, c, kb:kb + 128], rhs=ckvn[:, c, ts], start=(c == 0), stop=(c == 1)),
                            reads=[wkv, ckvn], writes=[pkn])
                    P.op("act", lambda e, pkn=pkn: e.activation(out=sqn[:], in_=pkn[:, :TT], func=AF.Square),
                         reads=[pkn], writes=[sqn])
                    pss2 = P.next_psum()
                    P.op("pe", lambda e, pss2=pss2: e.matmul(pss2[:, :TT], lhsT=ones_bf[:], rhs=sqn[:], start=True, stop=False),
                         reads=[ones_bf, sqn], writes=[pss2])
                    P.op("pe", lambda e, pss2=pss2, ts=ts: e.matmul(pss2[:, :TT], lhsT=ones_bf[:64, :], rhs=kpsq[:, ts],
                                                                   start=False, stop=True),
                         reads=[ones_bf, kpsq], writes=[pss2])
                    rstd_from_ss(rk, pss2, 192)
                    P.op("dve", lambda e, pkn=pkn, ts=ts: e.scalar_tensor_tensor(
                        out=KTn[:, ts], in0=pkn[:, :TT], scalar=cols[:, 8:9], in1=rk[:], op0=ALU.mult, op1=ALU.mult),
                        reads=[pkn, cols, rk], writes=[KTn])
                    P.op("dve", lambda e, ts=ts: e.tensor_tensor(out=KTr[:, ts], in0=kper[:, ts], in1=rk[:64], op=ALU.mult),
                         reads=[kper, rk], writes=[KTr])
                    pv = P.next_psum()
                    for blk in range(4):
                        tb = slice(tt * TT + blk * 128, tt * TT + (blk + 1) * 128)
                        for c in range(2):
                            P.op("pe", lambda e, c=c, pv=pv, tb=tb, blk=blk, kb=kb: e.matmul(
                                pv[:, blk * 128:(blk + 1) * 128], lhsT=ckvn[:, c, tb], rhs=wkv[:, c, kb + 128:kb + 256],
                                start=(c == 0), stop=(c == 1)), reads=[ckvn, wkv], writes=[pv])
                    P.op("act", lambda e, pv=pv, tt=tt: e.copy(
                        out=Vh[:, tt * 4:(tt + 1) * 4, :], in_=pv[:].rearrange("p (b d) -> p b d", b=4)),
                        reads=[pv], writes=[Vh])

                if cfg.upto == "m2p":
                    continue
                units = [(i, jb) for i in range(NT) for jb in range(4 * i + 4)]

                def emit_S(u):
                    i, jb = u
                    q0 = max(i * TT, jb * 128)
                    n = (i + 1) * TT - q0
                    ps = P.next_psum()
                    ks = slice(jb * 128, (jb + 1) * 128)
                    qs = slice(q0, q0 + n)
                    P.op("pe", lambda e: e.matmul(ps[:, :n], lhsT=KTn[:, ks], rhs=QTn[:, qs], start=True, stop=False),
                         reads=[KTn, QTn], writes=[ps])
                    P.op("pe", lambda e: e.matmul(ps[:, :n], lhsT=KTr[:, ks], rhs=QTr[:, qs], start=False, stop=True),
                         reads=[KTr, QTr], writes=[ps])
                    pt = pts[ptc[0] % 4]
                    ptc[0] += 1
                    P.op("act", lambda e: e.activation(out=pt[:, :n], in_=ps[:, :n], func=AF.Exp,
                                                       bias=cols[:, 12:13], scale=ATTN_SCALE),
                         reads=[ps, cols], writes=[pt])
                    if jb >= 4 * i:
                        P.op("pool", lambda e: e.tensor_tensor(out=pt[:, 0:128], in0=pt[:, 0:128], in1=tri_bf[:], op=ALU.mult),
                             reads=[pt, tri_bf], writes=[pt])
                    return (i, jb, pt, q0 - i * TT, n)

                def emit_PV(s):
                    i, jb, pt, c0, n = s
                    po = oacc[i % 2]
                    pd = dacc[i % 2]
                    last = (jb == 4 * i + 3)
                    P.op("pe", lambda e: e.matmul(po[:, c0:c0 + n], lhsT=Vh[:, jb, :], rhs=pt[:, :n],
                                                  start=(jb == 0), stop=last), reads=[Vh, pt], writes=[po])
                    P.op("pe", lambda e: e.matmul(pd[:, c0:c0 + n], lhsT=ones_bf[:], rhs=pt[:, :n],
                                                  start=(jb == 0), stop=last), reads=[ones_bf, pt], writes=[pd])
                    if last:
                        P.op("act", lambda e: e.activation(out=rden[:], in_=pd[:, :TT], func=AF.Ln),
                             reads=[pd], writes=[rden])
                        P.op("act", lambda e: e.activation(out=rden[:], in_=rden[:], func=AF.Exp, scale=-1.0),
                             reads=[rden], writes=[rden])
                        ob = obf[i % 2]
                        P.op("dve", lambda e: e.tensor_tensor(out=ob[:], in0=po[:, :TT], in1=rden[:], op=ALU.mult),
                             reads=[po, rden], writes=[ob])
                        P.dma("pool", OTb[i], OTd.t[hd, :, i * TT:(i + 1) * TT], ob, ob[:])

                pend = []
                for u in units:
                    pend.append(emit_S(u))
                    if len(pend) > 2:
                        emit_PV(pend.pop(0))
                while pend:
                    emit_PV(pend.pop(0))
            P.barrier()
            A.reset(m1)

            if cfg.upto in ("m2p", "m2"):
                A.reset(m0)
                return
            P.pspool = None
            wo = A.alloc("wo", [8, D], BF16)
            P.dma("pool", wo, wo[:], owout_in, owout_in.t[j].rearrange("(h p) f -> p h f", p=128))
            xt_tiles = [A.alloc("xt%d" % i, [NC8, TT], F32) for i in range(2)]
            ot_tiles = [A.alloc("ot%d" % i, [8, TT], BF16) for i in range(2)]
            for tt in range(NT):
                ts = slice(tt * TT, (tt + 1) * TT)
                xt = xt_tiles[tt % 2]
                ot = ot_tiles[tt % 2]
                P.dma("sp", xt, xt[:], XTb[tt], XTv[:, :, ts])
                P.dma("sp", ot, ot[:], OTb[tt], OTd.t.rearrange("h p t -> p h t")[:, :, ts])
                for o in range(NC8):
                    ps = P.next_psum()
                    for hh in range(8):
                        P.op("pe", lambda e, hh=hh, o=o, ps=ps, ot=ot: e.matmul(
                            ps[:, :TT], lhsT=wo[:, hh, o * 128:(o + 1) * 128], rhs=ot[:, hh, :],
                            start=(hh == 0), stop=(hh == 7)), reads=[wo, ot], writes=[ps])
                    P.op("dve", lambda e, o=o, ps=ps, xt=xt: e.tensor_tensor(
                        out=xt[:, o, :], in0=ps[:, :TT], in1=xt[:, o, :], op=ALU.add), reads=[ps, xt], writes=[xt])
                P.dma("pool", XTb[tt], XTv[:, :, ts], xt, xt[:])
            P.barrier()
            A.reset(m0)

        TE = 128
        SDT = BF16
        NCHT = TE // 64
        HG = 4
        LWS = -float(np.exp(-0.5))

        def cast_even(i):
            for jj in range(24):
                P.op("pool", lambda e, jj=jj: e.dma_start(
                    out=we_bfs[i].t[jj], in_=ewin_in.t[i].rearrange("(c p) f -> p c f", p=128)[:, :, jj * 128:(jj + 1) * 128]),
                    reads=[ewin_in], writes=[we_bfs[i]], dma=ecastsem[i])
            P.op("pool", lambda e: e.dma_start(
                out=wl_bfs[i].t, in_=ewin_in.t[i].rearrange("(c p) f -> p c f", p=128)[:, :, 3072:3232]),
                reads=[ewin_in], writes=[wl_bfs[i]], dma=ecastsem[i])
            for o in range(NC8):
                P.op("pool", lambda e, o=o: e.dma_start(
                    out=woc_bfs[i].t[o], in_=ewout_in.t[i, 0:512, :].rearrange("(c p) f -> p c f", p=128)[:, :, o * 128:(o + 1) * 128]),
                    reads=[ewout_in], writes=[woc_bfs[i]], dma=ecastsem[i])
                P.op("pool", lambda e, o=o: e.dma_start(
                    out=wor_bfs[i].t[o], in_=ewout_in.t[i, 512:1024, :].rearrange("(h v) f -> v h f", v=64)[:, :, o * 128:(o + 1) * 128]),
                    reads=[ewout_in], writes=[wor_bfs[i]], dma=ecastsem[i])

        def even_layer(layer):
            i = layer // 2
            gidx = layer * 3 + 1
            NTE = T // TE
            m0 = A.mark()
            P.pspool = None
            we_bf, wl_bf, woc_bf, wor_bf = we_bfs[i], wl_bfs[i], woc_bfs[i], wor_bfs[i]
            m_xm = A.alloc("m_xm", [HG, 128], F32, parts=64)
            m_xt = A.alloc("m_xt", [HG, 64], F32, parts=64)
            identg = A.alloc("identg", [HG, 64], F32, parts=64)
            mtmp = A.alloc("mtmp", [3, 64], F32, parts=64)
            blockones = A.alloc("blockones", [128], BF16)
            rmask = A.alloc("rmask", [TE], F32)
            ecols = A.alloc("ecols", [48], F32)
            lcols = A.alloc("lcols", [4], F32)
            hcols = A.alloc("hcols", [16], F32, parts=64)
            ccon = A.alloc("ccon", [4], F32)
            w_up = A.alloc("w_up", [512], BF16, parts=32)
            a_up = A.alloc("a_up", [512], BF16, parts=32)
            g_up = A.alloc("g_up", [512], BF16, parts=96)
            wlora = A.alloc("wlora", [NC8, 160], BF16)
            Ss = [A.alloc("S%d" % k, [8, 64], F32, parts=64) for k in range(2)]
            for k in range(3):
                P.op("pool", lambda e, k=k: e.memset(mtmp[:, k, :], 1.0), writes=[mtmp])
            P.op("pool", lambda e: e.affine_select(out=mtmp[:, 0, :], in_=mtmp[:, 0, :], pattern=[[1, 64]],
                                                   compare_op=ALU.is_gt, fill=0.0, base=0, channel_multiplier=-1),
                 reads=[mtmp], writes=[mtmp])
            P.op("pool", lambda e: e.affine_select(out=mtmp[:, 1, :], in_=mtmp[:, 1, :], pattern=[[1, 64]],
                                                   compare_op=ALU.is_ge, fill=0.0, base=0, channel_multiplier=-1),
                 reads=[mtmp], writes=[mtmp])
            P.op("pool", lambda e: e.affine_select(out=mtmp[:, 2, :], in_=mtmp[:, 2, :], pattern=[[-1, 64]],
                                                   compare_op=ALU.is_gt, fill=0.0, base=0, channel_multiplier=1),
                 reads=[mtmp], writes=[mtmp])
            for hh in range(HG):
                P.op("pool", lambda e, hh=hh: e.tensor_copy(out=m_xm[:, hh, 0:64], in_=mtmp[:, 0, :]), reads=[mtmp], writes=[m_xm])
                P.op("pool", lambda e, hh=hh: e.tensor_copy(out=m_xm[:, hh, 64:128], in_=mtmp[:, 1, :]), reads=[mtmp], writes=[m_xm])
                P.op("pool", lambda e, hh=hh: e.tensor_copy(out=m_xt[:, hh, :], in_=mtmp[:, 2, :]), reads=[mtmp], writes=[m_xt])
                P.op("pool", lambda e, hh=hh: e.tensor_copy(out=identg[:, hh, :], in_=ident[0:64, 0:64]), reads=[ident], writes=[identg])
            P.op("pool", lambda e: e.memset(blockones[:], 1.0), writes=[blockones])
            P.op("pool", lambda e: e.memset(blockones[0:64, 64:128], 0.0), reads=[blockones], writes=[blockones])
            P.op("pool", lambda e: e.memset(blockones[64:128, 0:64], 0.0), reads=[blockones], writes=[blockones])
            P.op("pool", lambda e: e.memset(rmask[:], 1.0), writes=[rmask])
            P.op("pool", lambda e: e.memset(rmask[:].rearrange("p (c t) -> p c t", t=64)[:, :, 0:1], 0.0),
                 reads=[rmask], writes=[rmask])
            P.op("pool", lambda e: e.memset(ccon[:, 0:1], 1e-24), writes=[ccon])
            P.op("pool", lambda e: e.memset(ccon[:, 1:2], 64e-5), reads=[ccon], writes=[ccon])
            P.op("pool", lambda e: e.memset(Ss[0][:], 0.0), writes=[Ss[0]])
            P.dma("sp", ecols, ecols[:, 0:12], emu_in, emu_in.t[i, 0:1536].rearrange("(j p) -> p j", p=128))
            for k, src in enumerate((w0_in, a0_in, kk_in, ka_in, rk_in)):
                P.dma("sp", ecols, ecols[:, 12 + 4 * k:16 + 4 * k], src, src.t[i].rearrange("(j p) -> p j", p=128))
            P.dma("sp", ecols, ecols[:, 36:48].rearrange("p (j c) -> p j c", j=3), cw_in,
                  cw_in.t[i].rearrange("j (c p) -> p j c", p=128))
            P.op("dve", lambda e: e.tensor_scalar(out=ecols[:, 32:36], in0=ecols[:, 24:28], scalar1=-1.0, scalar2=1.0,
                                                  op0=ALU.mult, op1=ALU.add), reads=[ecols], writes=[ecols])
            P.dma("sp", lcols, lcols[0:32, 0:1], emu_in, emu_in.t[i, 1536:1568].rearrange("(p o) -> p o", o=1))
            P.dma("sp", lcols, lcols[0:32, 1:2], emu_in, emu_in.t[i, 1568:1600].rearrange("(p o) -> p o", o=1))
            P.dma("sp", lcols, lcols[0:96, 2:3], emu_in, emu_in.t[i, 1600:1696].rearrange("(p o) -> p o", o=1))
            P.dma("sp", hcols, hcols[:, 0:8], lnw_in, lnw_in.t[i].rearrange("(h v) -> v h", v=64))
            P.dma("sp", hcols, hcols[:, 8:16], lnb_in, lnb_in.t[i].rearrange("(h v) -> v h", v=64))
            P.dma("pool", w_up, w_up[:], wup_in, wup_in.t[i])
            P.dma("pool", a_up, a_up[:], aup_in, aup_in.t[i])
            P.dma("pool", g_up, g_up[:], gup_in, gup_in.t[i])
            P.dma("sp", wlora, wlora[:], wl_bf, wl_bf.t)

            xt_tiles = [A.alloc("ext%d" % k, [NC8, TE], F32) for k in range(2)]
            sq_tile = A.alloc("esq", [NC8, TE], BF16)
            h_tiles = [A.alloc("eh%d" % k, [NC8, TE], BF16) for k in range(2)]
            rstd_t = A.alloc("erstd", [TE], F32)
            NWB = 4
            we_tiles = [A.alloc("we%d" % k, [NC8, 128], BF16) for k in range(NWB)]
            gc = A.alloc("gc", [4, TE], F32)
            gb = A.alloc("gb", [4, TE], F32)
            ub = A.alloc("ub", [4, TE + 2], F32)
            cacc = [A.alloc("cacc%d" % k, [TE], F32) for k in range(2)]
            yconv = A.alloc("yconv", [4, TE], BF16)
            PR = [A.alloc("PR%d" % k, [TE + 1], F32) for k in range(12)]
            PL = [A.alloc("PL%d" % k, [TE + 1], F32) for k in range(3)]
            PRh = A.alloc("PRh", [16], F32)
            dtmp = [A.alloc("dtmp%d" % k, [TE], F32) for k in range(3)]
            tdw = A.alloc("tdw", [TE], BF16, parts=32)
            dab = A.alloc("dab", [TE], BF16, parts=32)
            sdg = A.alloc("sdg", [TE], BF16, parts=96)
            lw = A.alloc("lw", [TE], F32)
            aa = A.alloc("aa", [TE], F32)
            kkb = A.alloc("kk", [TE], F32)
            kk2 = A.alloc("kk2", [TE], BF16)
            rsb = A.alloc("rs", [TE], F32)
            kkn = A.alloc("kkn", [TE], F32)
            tmpk = A.alloc("tmpk", [TE], F32)
            bv = A.alloc("bv", [TE], F32)
            rkr = A.alloc("rkr", [TE], BF16)
            cc = A.alloc("cc", [TE], F32)
            cp = A.alloc("cp", [TE], F32)
            cd = A.alloc("cd", [TE], F32)
            ec = A.alloc("ec", [TE], F32)
            eci = A.alloc("eci", [TE], F32)
            ecp = A.alloc("ecp", [TE], F32)
            eCc = A.alloc("eCc", [TE], F32)
            AR = A.alloc("AR", [4, NCHT, 2, 64], SDT)
            Bt = A.alloc("Bt", [4, TE], SDT)
            Kt = A.alloc("Kt", [4, TE], SDT)
            BhT = A.alloc("BhT", [4, TE], SDT)
            KhT = A.alloc("KhT", [4, TE], SDT)
            bonus = A.alloc("bonus", [4, TE], F32)
            wCfm = A.alloc("wCfm", [4, NCHT], F32)
            wCT = A.alloc("wCT", [8, NCHT], F32, parts=64)
            Yb = A.alloc("Yb", [8, TE], F32, parts=64)
            RtL = A.alloc("RtL", [8, TE], F32, parts=64)
            ysq = A.alloc("ysq", [8, TE], F32, parts=64)
            mean = A.alloc("mean", [8, TE], F32, parts=64)
            var = A.alloc("var", [8, TE], F32, parts=64)
            ycb = A.alloc("ycb", [8, TE], F32, parts=64)
            yfin = A.alloc("yfin", [8, TE], BF16, parts=64)
            woc_t = [A.alloc("woc%d" % k, [4, 128], BF16) for k in range(2)]
            wor_t = [A.alloc("wor%d" % k, [8, 128], BF16, parts=64) for k in range(2)]
            G_ = []
            for g in range(2 * NCHT):
                d = {}
                for nm, shp in (("W1A", [HG, 2, 64]), ("Bh", [HG, 64]), ("Kh", [HG, 64]), ("Vt", [HG, 64]),
                                ("XM", [HG, 2, 64]), ("LM", [HG, 2, 64]), ("XT", [HG, 64]),
                                ("P0", [HG, 64]), ("P1", [HG, 64]), ("PT0", [HG, 64]), ("PT1", [HG, 64]),
                                ("Ac0", [HG, 64]), ("Ac1", [HG, 64]), ("UZ", [HG, 2, 64]), ("GT", [HG, 64]), ("QT", [HG, 64])):
                    d[nm] = A.alloc("%s_%d" % (nm, g), shp, F32 if nm in ("GT", "QT") else SDT, parts=64)
                G_.append(d)

            hset0 = (AR, Bt, Kt, BhT, KhT, bonus, wCT, RtL, sdg, yconv, PR)
            hset1 = (A.alloc("AR1", [4, NCHT, 2, 64], SDT), A.alloc("Bt1", [4, TE], SDT), A.alloc("Kt1", [4, TE], SDT),
                     A.alloc("BhT1", [4, TE], SDT), A.alloc("KhT1", [4, TE], SDT), A.alloc("bonus1", [4, TE], F32),
                     A.alloc("wCT1", [8, NCHT], F32, parts=64), A.alloc("RtL1", [8, TE], F32, parts=64),
                     A.alloc("sdg1", [TE], BF16, parts=96), A.alloc("yconv1", [4, TE], BF16),
                     [A.alloc("PRb%d" % k, [TE + 1], F32) for k in range(12)])
            P.op("pool", lambda e: e.memset(ub[:, :, 0:2], 0.0), writes=[ub])
            P.op("pool", lambda e: e.memset(PRh[:], 0.0), writes=[PRh])

            wcnt = [0]
            ocnt = [0]
            scur = [0]

            def proj(j_lo, ncols, h, consume):
                w = we_tiles[wcnt[0] % NWB]
                wcnt[0] += 1
                P.dma("sp", w, w[:], we_bf, we_bf.t[j_lo])
                ps = P.next_psum()
                for c in range(NC8):
                    P.op("pe", lambda e, c=c: e.matmul(ps[:ncols, :TE], lhsT=w[:, c, 0:ncols], rhs=h[:, c, :],
                                                      start=(c == 0), stop=(c == NC8 - 1)), reads=[w, h], writes=[ps])
                consume(ps)

            def chunk_pipeline(ch, g, te):
                B = G_[ch * 2 + g]
                cs = slice(ch * 64, (ch + 1) * 64)
                heads = list(range(g * HG, (g + 1) * HG))

                def hp(hd):
                    return hd // 2, (hd % 2) * 64

                for (nm, srcf, dst) in (("A", lambda pc: AR[:, pc, ch, 0, :], B["W1A"]),
                                        ("Bh", lambda pc: BhT[:, pc, cs], B["Bh"]),
                                        ("Kh", lambda pc: KhT[:, pc, cs], B["Kh"]),
                                        ("V", lambda pc: PR[8 + pc][:, 1 + ch * 64:1 + (ch + 1) * 64], B["Vt"])):
                    ps = P.next_psum()
                    for k in range(2):
                        pc = g * 2 + k
                        srcb = AR if nm == "A" else (BhT if nm == "Bh" else (KhT if nm == "Kh" else PR[8 + pc]))
                        idm = ident if (nm == "V" or SDT == F32) else ident_bf
                        P.op("pe", lambda e, k=k, pc=pc, idm=idm: e.matmul(ps[:64, k * 128:(k + 1) * 128], lhsT=srcf(pc),
                                                                           rhs=idm[:], start=True, stop=True),
                             reads=[srcb, idm], writes=[ps])
                    if nm == "A":
                        P.op("act", lambda e: e.copy(out=dst[:, :, 1, :], in_=ps[:64, 0:256].rearrange("p (h k) -> p h k", k=64)),
                             reads=[ps], writes=[dst])
                    else:
                        P.op("dve" if nm != "V" else "act",
                             (lambda e: e.tensor_copy(out=dst[:], in_=ps[:64, 0:256].rearrange("p (h k) -> p h k", k=64)))
                             if nm != "V" else
                             (lambda e: e.copy(out=dst[:], in_=ps[:64, 0:256].rearrange("p (h k) -> p h k", k=64))),
                             reads=[ps], writes=[dst])
                yield
                psx = [P.next_psum(), P.next_psum()]
                psl = [P.next_psum(), P.next_psum()]
                pst = [P.next_psum(), P.next_psum()]
                for k, hd in enumerate(heads):
                    pc, pb = hp(hd)
                    par = hd % 2
                    a = k // 2
                    P.op("pe", lambda e, a=a, par=par, pc=pc, pb=pb: e.matmul(
                        psx[par][:64, a * 128:(a + 1) * 128], lhsT=Bt[pb:pb + 64, pc, cs],
                        rhs=AR[pb:pb + 64, pc, ch].rearrange("p a t -> p (a t)"), start=True, stop=True),
                        reads=[Bt, AR], writes=[psx[par]])
                    P.op("pe", lambda e, a=a, par=par, pc=pc, pb=pb: e.matmul(
                        psl[par][:64, a * 128:(a + 1) * 128], lhsT=Kt[pb:pb + 64, pc, cs],
                        rhs=AR[pb:pb + 64, pc, ch].rearrange("p a t -> p (a t)"), start=True, stop=True),
                        reads=[Kt, AR], writes=[psl[par]])
                    P.op("pe", lambda e, a=a, par=par, pc=pc, pb=pb: e.matmul(
                        pst[par][:64, a * 64:(a + 1) * 64], lhsT=AR[pb:pb + 64, pc, ch, 0, :],
                        rhs=Bt[pb:pb + 64, pc, cs], start=True, stop=True),
                        reads=[Bt, AR], writes=[pst[par]])
                for par in range(2):
                    P.op("dve", lambda e, par=par: e.tensor_tensor(
                        out=B["XM"][:].rearrange("p (a q) x t -> p a q (x t)", q=2)[:, :, par, :],
                        in0=psx[par][:64, 0:256].rearrange("p (a x) -> p a x", x=128),
                        in1=m_xm[:, 0:2, :], op=ALU.mult), reads=[psx[par], m_xm], writes=[B["XM"]])
                    P.op("dve", lambda e, par=par: e.tensor_tensor(
                        out=B["LM"][:].rearrange("p (a q) x t -> p a q (x t)", q=2)[:, :, par, :],
                        in0=psl[par][:64, 0:256].rearrange("p (a x) -> p a x", x=128),
                        in1=m_xm[:, 0:2, :], op=ALU.mult), reads=[psl[par], m_xm], writes=[B["LM"]])
                    P.op("dve", lambda e, par=par: e.tensor_tensor(
                        out=B["XT"][:].rearrange("p (a q) t -> p a q t", q=2)[:, :, par, :],
                        in0=pst[par][:64, 0:128].rearrange("p (a x) -> p a x", x=64),
                        in1=m_xt[:, 0:2, :], op=ALU.mult), reads=[pst[par], m_xt], writes=[B["XT"]])
                P.op("pool", lambda e: e.tensor_tensor(out=B["Ac0"][:], in0=B["XM"][:, :, 0, :], in1=identg[:], op=ALU.add),
                     reads=[B["XM"], identg], writes=[B["Ac0"]])
                yield
                Pc, PTc, Ac = (B["XM"], lambda k: B["XM"][:, k, 0, :]), (B["XT"], lambda k: B["XT"][:, k, :]), B["Ac0"]
                for lvl in range(5):
                    lastl = (lvl == 4)
                    Pn = B["P%d" % (lvl % 2)]
                    PTn = B["PT%d" % (lvl % 2)]
                    Acn = B["Ac%d" % ((lvl + 1) % 2)]
                    psB = P.next_psum()
                    psA = None if lastl else P.next_psum()
                    for k in range(HG):
                        P.op("pe", lambda e, k=k: e.matmul(psB[:64, k * 64:(k + 1) * 64], lhsT=Pc[1](k), rhs=PTc[1](k),
                                                          start=True, stop=True), reads=[Pc[0], PTc[0]], writes=[psB])
                    if not lastl:
                        for k in range(HG):
                            P.op("pe", lambda e, k=k: e.matmul(psA[:64, k * 64:(k + 1) * 64], lhsT=PTc[1](k), rhs=Pc[1](k),
                                                              start=True, stop=True), reads=[Pc[0], PTc[0]], writes=[psA])
                    P.op("dve", lambda e: e.tensor_copy(out=PTn[:], in_=psB[:64, 0:256].rearrange("p (h x) -> p h x", x=64)),
                         reads=[psB], writes=[PTn])
                    if not lastl:
                        P.op("act", lambda e: e.copy(out=Pn[:], in_=psA[:64, 0:256].rearrange("p (h x) -> p h x", x=64)),
                             reads=[psA], writes=[Pn])
                    yield
                    psC = P.next_psum()
                    for k in range(HG):
                        P.op("pe", lambda e, k=k: e.matmul(psC[:64, k * 64:(k + 1) * 64], lhsT=PTn[:, k, :], rhs=Ac[:, k, :],
                                                          start=True, stop=True), reads=[PTn, Ac], writes=[psC])
                    P.op("dve", lambda e: e.tensor_tensor(out=Acn[:], in0=psC[:64, 0:256].rearrange("p (h x) -> p h x", x=64),
                                                          in1=Ac[:], op=ALU.add), reads=[psC, Ac], writes=[Acn])
                    Pc = (Pn, lambda k, Pn=Pn: Pn[:, k, :])
                    PTc = (PTn, lambda k, PTn=PTn: PTn[:, k, :])
                    Ac = Acn
                    yield
                NTb = Ac
                psW = P.next_psum()
                for k in range(HG):
                    P.op("pe", lambda e, k=k: e.matmul(psW[:64, k * 64:(k + 1) * 64], lhsT=B["LM"][:, k, 0, :], rhs=B["Vt"][:, k, :],
                                                      start=True, stop=True), reads=[B["LM"], B["Vt"]], writes=[psW])
                P.op("act", lambda e: e.copy(out=B["W1A"][:, :, 0, :], in_=psW[:64, 0:256].rearrange("p (h x) -> p h x", x=64)),
                     reads=[psW], writes=[B["W1A"]])
                yield
                psU = P.next_psum()
                for k in range(HG):
                    P.op("pe", lambda e, k=k: e.matmul(psU[:64, k * 128:(k + 1) * 128], lhsT=NTb[:, k, :],
                                                      rhs=B["W1A"][:, k].rearrange("p a t -> p (a t)"),
                                                      start=True, stop=True), reads=[NTb, B["W1A"]], writes=[psU])
                P.op("dve", lambda e: e.tensor_copy(out=B["UZ"][:].rearrange("p h a t -> p h (a t)"),
                                                    in_=psU[:64, :].rearrange("p (h x) -> p h x", x=128)),
                     reads=[psU], writes=[B["UZ"]])
                yield
                psG = P.next_psum()
                psQ = P.next_psum()
                for k, hd in enumerate(heads):
                    pc, pb = hp(hd)
                    P.op("pe", lambda e, k=k: e.matmul(psG[:64, k * 64:(k + 1) * 64], lhsT=B["UZ"][:, k, 1, :], rhs=B["Bh"][:, k, :],
                                                      start=True, stop=True), reads=[B["UZ"], B["Bh"]], writes=[psG])
                    P.op("pe", lambda e, k=k: e.matmul(psQ[:64, k * 64:(k + 1) * 64], lhsT=B["UZ"][:, k, 1, :], rhs=B["XM"][:, k, 1, :],
                                                      start=True, stop=True), reads=[B["UZ"], B["XM"]], writes=[psQ])
                P.op("act", lambda e: e.copy(out=B["GT"][:], in_=psG[:64, 0:256].rearrange("p (h x) -> p h x", x=64)),
                     reads=[psG], writes=[B["GT"]])
                P.op("dve", lambda e: e.tensor_tensor(out=B["QT"][:], in0=psQ[:64, 0:256].rearrange("p (h x) -> p h x", x=64),
                                                      in1=RtL[:, g * HG:(g + 1) * HG, cs], op=ALU.add),
                     reads=[psQ, RtL], writes=[B["QT"]])
                yield
                So = Ss[(scur[0] + ch) % 2]
                Sn = Ss[(scur[0] + ch + 1) % 2]
                psY = P.next_psum()
                psS = P.next_psum()
                for k, hd in enumerate(heads):
                    P.op("pe", lambda e, k=k: e.matmul(psY[:64, k * 64:(k + 1) * 64], lhsT=B["UZ"][:, k, 0, :], rhs=B["XM"][:, k, 1, :],
                                                      start=True, stop=False), reads=[B["UZ"], B["XM"]], writes=[psY])
                    P.op("pe", lambda e, k=k: e.matmul(psY[:64, k * 64:(k + 1) * 64], lhsT=B["Vt"][:, k, :], rhs=B["LM"][:, k, 1, :],
                                                      start=False, stop=False), reads=[B["Vt"], B["LM"]], writes=[psY])
                    P.op("pe", lambda e, k=k, hd=hd: e.matmul(psY[:64, k * 64:(k + 1) * 64], lhsT=So[:, hd, :], rhs=B["QT"][:, k, :],
                                                             start=False, stop=True), reads=[So, B["QT"]], writes=[psY])
                for k, hd in enumerate(heads):
                    P.op("pe", lambda e, k=k: e.matmul(psS[:64, k * 64:(k + 1) * 64], lhsT=B["Bh"][:, k, :], rhs=B["UZ"][:, k, 0, :],
                                                      start=True, stop=False), reads=[B["UZ"], B["Bh"]], writes=[psS])
                    P.op("pe", lambda e, k=k: e.matmul(psS[:64, k * 64:(k + 1) * 64], lhsT=B["Kh"][:, k, :], rhs=B["Vt"][:, k, :],
                                                      start=False, stop=False), reads=[B["Kh"], B["Vt"]], writes=[psS])
                    P.op("pe", lambda e, k=k, hd=hd: e.matmul(psS[:64, k * 64:(k + 1) * 64], lhsT=B["GT"][:, k, :], rhs=So[:, hd, :],
                                                             start=False, stop=True), reads=[So, B["GT"]], writes=[psS])
                P.op("act", lambda e: e.copy(out=Yb[:, g * HG:(g + 1) * HG, cs], in_=psY[:64, 0:256].rearrange("p (h x) -> p h x", x=64)),
                     reads=[psY], writes=[Yb])
                for k, hd in enumerate(heads):
                    P.op("dve", lambda e, k=k, hd=hd: e.scalar_tensor_tensor(
                        out=Sn[:, hd, :], in0=So[:, hd, :], scalar=wCT[:, hd, ch:ch + 1], in1=psS[:64, k * 64:(k + 1) * 64],
                        op0=ALU.mult, op1=ALU.add), reads=[So, wCT, psS], writes=[Sn])
                yield

            def phase_a(te):
                ts = slice(te * TE, (te + 1) * TE)
                xt = xt_tiles[te % 2]
                h = h_tiles[te % 2]
                tt = te // (TT // TE)
                P.dma("sp", xt, xt[:], XTb[tt], XTv[:, :, ts])
                rmsnorm_tile(xt, h, gidx, sq_tile, rstd_t, w=TE)
                for pc in range(4):
                    proj(4 + pc, 128, h, lambda ps, pc=pc: P.op(
                        "act", lambda e: e.copy(out=gc[:, pc, :], in_=ps[:, :TE]), reads=[ps], writes=[gc]))
                yield
                for pc in range(4):
                    proj(8 + pc, 128, h, lambda ps, pc=pc: P.op(
                        "dve", lambda e: e.tensor_tensor(out=ub[:, pc, 2:TE + 2], in0=ps[:, :TE], in1=gc[:, pc, :], op=ALU.mult),
                        reads=[ps, gc], writes=[ub]))
                yield
                for pc in range(4):
                    proj(pc, 128, h, lambda ps, pc=pc: P.op(
                        "act", lambda e: e.copy(out=gb[:, pc, :], in_=ps[:, :TE]), reads=[ps], writes=[gb]))
                yield
                for pc in range(4):
                    ca = cacc[pc % 2]
                    P.op("dve", lambda e, pc=pc, ca=ca: e.tensor_scalar(out=ca[:], in0=ub[:, pc, 2:TE + 2],
                                                                      scalar1=ecols[:, 36 + 8 + pc:36 + 8 + pc + 1], scalar2=None,
                                                                      op0=ALU.mult), reads=[ub, ecols], writes=[ca])
                    P.op("dve", lambda e, pc=pc, ca=ca: e.scalar_tensor_tensor(
                        out=ca[:], in0=ub[:, pc, 1:TE + 1], scalar=ecols[:, 36 + 4 + pc:36 + 4 + pc + 1], in1=ca[:],
                        op0=ALU.mult, op1=ALU.add), reads=[ub, ecols, ca], writes=[ca])
                    P.op("dve", lambda e, pc=pc, ca=ca: e.scalar_tensor_tensor(
                        out=ca[:], in0=ub[:, pc, 0:TE], scalar=ecols[:, 36 + pc:36 + pc + 1], in1=ca[:],
                        op0=ALU.mult, op1=ALU.add), reads=[ub, ecols, ca], writes=[ca])
                    P.op("dve", lambda e, pc=pc, ca=ca: e.tensor_tensor(out=yconv[:, pc, :], in0=ca[:], in1=gb[:, pc, :], op=ALU.mult),
                         reads=[ca, gb], writes=[yconv])
                P.op("pool", lambda e: e.tensor_copy(out=ub[:, :, 0:2], in_=ub[:, :, TE:TE + 2]), reads=[ub], writes=[ub])
                yield
                for j in range(12):
                    def cons(ps, j=j):
                        pr = PR[j]
                        P.op("act", lambda e: e.copy(out=pr[:, 0:1], in_=PRh[:, j:j + 1]), reads=[PRh], writes=[pr])
                        P.op("act", lambda e: e.copy(out=pr[:, 1:TE + 1], in_=ps[:, :TE]), reads=[ps], writes=[pr])
                        d = dtmp[j % 3]
                        P.op("pool", lambda e: e.tensor_tensor(out=d[:], in0=pr[:, 0:TE], in1=pr[:, 1:TE + 1], op=ALU.subtract),
                             reads=[pr], writes=[d])
                        P.op("pool", lambda e: e.tensor_copy(out=PRh[:, j:j + 1], in_=pr[:, TE:TE + 1]), reads=[pr], writes=[PRh])
                        P.op("dve", lambda e: e.scalar_tensor_tensor(out=pr[:, 1:TE + 1], in0=d[:], scalar=ecols[:, j:j + 1],
                                                                     in1=pr[:, 1:TE + 1], op0=ALU.mult, op1=ALU.add),
                             reads=[d, ecols, pr], writes=[pr])
                    proj(12 + j, 128, h, cons)
                    if j % 3 == 2:
                        yield
                for li, (lo, n) in enumerate(((0, 32), (32, 32), (64, 96))):
                    ps = P.next_psum()
                    for c in range(NC8):
                        P.op("pe", lambda e, c=c, lo=lo, n=n: e.matmul(ps[:n, :TE], lhsT=wlora[:, c, lo:lo + n], rhs=h[:, c, :],
                                                                      start=(c == 0), stop=(c == NC8 - 1)),
                             reads=[wlora, h], writes=[ps])
                    pl = PL[li]
                    P.op("act", lambda e, li=li, n=n, pl=pl: e.copy(out=pl[:n, 0:1], in_=PRh[:n, 12 + li:13 + li]),
                         reads=[PRh], writes=[pl])
                    P.op("act", lambda e, n=n, pl=pl, ps=ps: e.copy(out=pl[:n, 1:TE + 1], in_=ps[:n, :TE]), reads=[ps], writes=[pl])
                    d = dtmp[li % 3]
                    P.op("pool", lambda e, n=n, pl=pl, d=d: e.tensor_tensor(out=d[:n], in0=pl[:n, 0:TE], in1=pl[:n, 1:TE + 1],
                                                                           op=ALU.subtract), reads=[pl], writes=[d])
                    P.op("pool", lambda e, n=n, pl=pl, li=li: e.tensor_copy(out=PRh[:n, 12 + li:13 + li], in_=pl[:n, TE:TE + 1]),
                         reads=[pl], writes=[PRh])
                    P.op("dve", lambda e, n=n, pl=pl, d=d, li=li: e.scalar_tensor_tensor(
                        out=pl[:n, 1:TE + 1], in0=d[:n], scalar=lcols[:n, li:li + 1], in1=pl[:n, 1:TE + 1],
                        op0=ALU.mult, op1=ALU.add), reads=[d, lcols, pl], writes=[pl])
                P.op("act", lambda e: e.activation(out=tdw[:], in_=PL[0][:32, 1:TE + 1], func=AF.Tanh), reads=[PL[0]], writes=[tdw])
                P.op("act", lambda e: e.copy(out=dab[:], in_=PL[1][:32, 1:TE + 1]), reads=[PL[1]], writes=[dab])
                P.op("act", lambda e: e.activation(out=sdg[:], in_=PL[2][:96, 1:TE + 1], func=AF.Sigmoid), reads=[PL[2]], writes=[sdg])
                yield
                for pc in range(4):
                    fs = slice(pc * 128, (pc + 1) * 128)
                    rr = PR[pc]
                    kx = PR[4 + pc]
                    vv = PR[8 + pc]
                    R1 = slice(1, TE + 1)
                    psw = P.next_psum()
                    P.op("pe", lambda e, fs=fs, psw=psw: e.matmul(psw[:, :TE], lhsT=w_up[:, fs], rhs=tdw[:], start=True, stop=True),
                         reads=[w_up, tdw], writes=[psw])
                    psa = P.next_psum()
                    P.op("pe", lambda e, fs=fs, psa=psa: e.matmul(psa[:, :TE], lhsT=a_up[:, fs], rhs=dab[:], start=True, stop=True),
                         reads=[a_up, dab], writes=[psa])
                    P.op("act", lambda e, pc=pc, psw=psw: e.activation(out=lw[:], in_=psw[:, :TE], func=AF.Sigmoid,
                                                                      bias=ecols[:, 12 + pc:13 + pc]),
                         reads=[psw, ecols], writes=[lw])
                    P.op("act", lambda e, pc=pc, psa=psa: e.activation(out=aa[:], in_=psa[:, :TE], func=AF.Sigmoid,
                                                                      bias=ecols[:, 16 + pc:17 + pc]),
                         reads=[psa, ecols], writes=[aa])
                    P.op("dve", lambda e: e.tensor_scalar(out=lw[:], in0=lw[:], scalar1=LWS, scalar2=None, op0=ALU.mult),
                         reads=[lw], writes=[lw])
                    P.op("dve", lambda e, pc=pc, kx=kx: e.tensor_scalar(out=kkb[:], in0=kx[:, R1], scalar1=ecols[:, 20 + pc:21 + pc],
                                                                      scalar2=None, op0=ALU.mult), reads=[kx, ecols], writes=[kkb])
                    P.op("act", lambda e: e.activation(out=kk2[:], in_=kkb[:], func=AF.Square), reads=[kkb], writes=[kk2])
                    pss = P.next_psum()
                    P.op("pe", lambda e, pss=pss: e.matmul(pss[:, :TE], lhsT=blockones[:], rhs=kk2[:], start=True, stop=True),
                         reads=[blockones, kk2], writes=[pss])
                    P.op("act", lambda e, pss=pss: e.activation(out=rsb[:], in_=pss[:, :TE], func=AF.Ln, bias=ccon[:, 0:1]),
                         reads=[pss, ccon], writes=[rsb])
                    P.op("act", lambda e: e.activation(out=rsb[:], in_=rsb[:], func=AF.Exp, scale=-0.5), reads=[rsb], writes=[rsb])
                    P.op("dve", lambda e: e.tensor_tensor(out=kkn[:], in0=kkb[:], in1=rsb[:], op=ALU.mult), reads=[kkb, rsb], writes=[kkn])
                    P.op("dve", lambda e, pc=pc: e.tensor_scalar(out=tmpk[:], in0=aa[:], scalar1=ecols[:, 24 + pc:25 + pc],
                                                                 scalar2=ecols[:, 32 + pc:33 + pc], op0=ALU.mult, op1=ALU.add),
                         reads=[aa, ecols], writes=[tmpk])
                    P.op("dve", lambda e, kx=kx: e.tensor_tensor(out=kx[:, R1], in0=kx[:, R1], in1=tmpk[:], op=ALU.mult),
                         reads=[kx, tmpk], writes=[kx])
                    P.op("pool", lambda e: e.tensor_tensor(out=bv[:], in0=kkn[:], in1=aa[:], op=ALU.mult), reads=[kkn, aa], writes=[bv])
                    P.op("dve", lambda e, pc=pc, rr=rr, kx=kx: e.scalar_tensor_tensor(
                        out=rkr[:], in0=rr[:, R1], scalar=ecols[:, 28 + pc:29 + pc], in1=kx[:, R1], op0=ALU.mult, op1=ALU.mult),
                        reads=[rr, ecols, kx], writes=[rkr])
                    psr = P.next_psum()
                    P.op("pe", lambda e, psr=psr: e.matmul(psr[:, :TE], lhsT=blockones[:], rhs=rkr[:], start=True, stop=True),
                         reads=[blockones, rkr], writes=[psr])
                    P.op("dve", lambda e, pc=pc, vv=vv, psr=psr: e.tensor_tensor(out=bonus[:, pc, :], in0=psr[:, :TE], in1=vv[:, R1],
                                                                               op=ALU.mult), reads=[psr, vv], writes=[bonus])
                    yield
                    P.op("dve", lambda e: e.tensor_tensor_scan(out=cc[:], data0=rmask[:], data1=lw[:], initial=0.0,
                                                               op0=ALU.mult, op1=ALU.add), reads=[rmask, lw], writes=[cc])
                    P.op("pool", lambda e: e.tensor_tensor(out=cp[:], in0=cc[:], in1=lw[:], op=ALU.subtract), reads=[cc, lw], writes=[cp])
                    for ch in range(NCHT):
                        P.op("dve", lambda e, ch=ch: e.tensor_scalar(out=cd[:, ch * 64:(ch + 1) * 64], in0=cc[:, ch * 64:(ch + 1) * 64],
                                                                     scalar1=cc[:, ch * 64 + 63:ch * 64 + 64], scalar2=None,
                                                                     op0=ALU.subtract), reads=[cc], writes=[cd])
                    P.op("act", lambda e: e.activation(out=ec[:], in_=cc[:], func=AF.Exp), reads=[cc], writes=[ec])
                    P.op("act", lambda e: e.activation(out=eci[:], in_=cc[:], func=AF.Exp, scale=-1.0), reads=[cc], writes=[eci])
                    P.op("act", lambda e: e.activation(out=ecp[:], in_=cp[:], func=AF.Exp), reads=[cp], writes=[ecp])
                    P.op("act", lambda e: e.activation(out=eCc[:], in_=cd[:], func=AF.Exp, scale=-1.0), reads=[cd], writes=[eCc])
                    P.op("act", lambda e, pc=pc: e.copy(out=wCfm[:, pc, :], in_=ec[:].rearrange("p (c t) -> p c t", t=64)[:, :, 63]),
                         reads=[ec], writes=[wCfm])
                    yield
                    v3 = lambda b: b[:].rearrange("p (c t) -> p c t", t=64)
                    P.op("dve", lambda e, pc=pc: e.scalar_tensor_tensor(out=AR[:, pc, :, 0, :], in0=v3(kkn), scalar=-1.0, in1=v3(ecp),
                                                                        op0=ALU.mult, op1=ALU.mult), reads=[kkn, ecp], writes=[AR])
                    P.op("pool", lambda e, pc=pc, rr=rr: e.tensor_tensor(out=AR[:, pc, :, 1, :],
                                                                       in0=rr[:, R1].rearrange("p (c t) -> p c t", t=64),
                                                                       in1=v3(ec), op=ALU.mult), reads=[rr, ec], writes=[AR])
                    P.op("dve", lambda e, pc=pc: e.tensor_tensor(out=Bt[:, pc, :], in0=bv[:], in1=eci[:], op=ALU.mult),
                         reads=[bv, eci], writes=[Bt])
                    P.op("pool", lambda e, pc=pc, kx=kx: e.tensor_tensor(out=Kt[:, pc, :], in0=kx[:, R1], in1=eci[:], op=ALU.mult),
                         reads=[kx, eci], writes=[Kt])
                    P.op("dve", lambda e, pc=pc: e.tensor_tensor(out=BhT[:, pc, :], in0=bv[:], in1=eCc[:], op=ALU.mult),
                         reads=[bv, eCc], writes=[BhT])
                    P.op("pool", lambda e, pc=pc, kx=kx: e.tensor_tensor(out=KhT[:, pc, :], in0=kx[:, R1], in1=eCc[:], op=ALU.mult),
                         reads=[kx, eCc], writes=[KhT])
                yield
                for par in range(2):
                    pb = par * 64
                    psc = P.next_psum()
                    P.op("pe", lambda e, pb=pb, psc=psc: e.matmul(psc[:64, 0:4 * NCHT], lhsT=ident[pb:pb + 64, pb:pb + 64],
                                                                  rhs=wCfm[pb:pb + 64].rearrange("p a c -> p (a c)"), start=True, stop=True),
                         reads=[ident, wCfm], writes=[psc])
                    P.op("dve", lambda e, par=par, psc=psc: e.tensor_copy(
                        out=wCT[:].rearrange("p (a q) c -> p a q c", q=2)[:, :, par, :],
                        in_=psc[:64, 0:4 * NCHT].rearrange("p (a c) -> p a c", c=NCHT)),
                        reads=[psc], writes=[wCT])
                    psr2 = P.next_psum()
                    for pc in range(4):
                        P.op("pe", lambda e, pc=pc, pb=pb, psr2=psr2: e.matmul(
                            psr2[:64, pc * TE:(pc + 1) * TE].rearrange("p (c t) -> p c t", t=64),
                            lhsT=(ident if SDT == F32 else ident_bf)[pb:pb + 64, pb:pb + 64],
                            rhs=AR[pb:pb + 64, pc, :, 1, :], start=True, stop=True), reads=[ident, ident_bf, AR], writes=[psr2])
                    P.op("act", lambda e, par=par, psr2=psr2: e.copy(
                        out=RtL[:].rearrange("p (a q) t -> p a q t", q=2)[:, :, par, :],
                        in_=psr2[:64, 0:4 * TE].rearrange("p (a t) -> p a t", t=TE)), reads=[psr2], writes=[RtL])
                yield

            def phase_b(te):
                ts = slice(te * TE, (te + 1) * TE)
                xt = xt_tiles[te % 2]
                tt = te // (TT // TE)
                gens = [chunk_pipeline(ch, g, te) for ch in range(NCHT) for g in range(2)]
                alive = True
                while alive:
                    alive = False
                    for gi in gens:
                        try:
                            next(gi)
                            alive = True
                        except StopIteration:
                            pass
                    yield
                scur[0] += NCHT
                P.op("act", lambda e: e.activation(out=ysq[:], in_=Yb[:], func=AF.Square), reads=[Yb], writes=[ysq])
                NH = 512 // TE
                ps1 = [P.next_psum() for _ in range(8 // NH)]
                for hd in range(8):
                    P.op("pe", lambda e, hd=hd: e.matmul(ps1[hd // NH][:64, (hd % NH) * TE:(hd % NH + 1) * TE], lhsT=ones_f[0:64, 0:64],
                                                        rhs=Yb[:, hd, :], start=True, stop=True), reads=[ones_f, Yb], writes=[ps1[hd // NH]])
                for b in range(8 // NH):
                    P.op("act", lambda e, b=b: e.activation(out=mean[:, b * NH:(b + 1) * NH, :],
                                                            in_=ps1[b][:64, :].rearrange("p (h t) -> p h t", t=TE),
                                                            func=AF.Copy, scale=1.0 / 64), reads=[ps1[b]], writes=[mean])
                yield
                ps2 = [P.next_psum() for _ in range(8 // NH)]
                for hd in range(8):
                    P.op("pe", lambda e, hd=hd: e.matmul(ps2[hd // NH][:64, (hd % NH) * TE:(hd % NH + 1) * TE], lhsT=ones_f[0:64, 0:64],
                                                        rhs=ysq[:, hd, :], start=True, stop=True), reads=[ones_f, ysq], writes=[ps2[hd // NH]])
                P.op("act", lambda e: e.activation(out=ysq[:], in_=mean[:], func=AF.Square), reads=[mean], writes=[ysq])
                for b in range(8 // NH):
                    P.op("dve", lambda e, b=b: e.scalar_tensor_tensor(
                        out=var[:, b * NH:(b + 1) * NH, :], in0=ps2[b][:64, :].rearrange("p (h t) -> p h t", t=TE), scalar=1.0 / 64,
                        in1=ysq[:, b * NH:(b + 1) * NH, :], op0=ALU.mult, op1=ALU.subtract), reads=[ps2[b], ysq], writes=[var])
                P.op("act", lambda e: e.activation(out=var[:], in_=var[:], func=AF.Ln, bias=ccon[:64, 1:2]), reads=[var, ccon], writes=[var])
                P.op("act", lambda e: e.activation(out=var[:], in_=var[:], func=AF.Exp, scale=-0.5), reads=[var], writes=[var])
                P.op("pool", lambda e: e.tensor_tensor(out=ycb[:], in0=Yb[:], in1=mean[:], op=ALU.subtract), reads=[Yb, mean], writes=[ycb])
                P.op("pool", lambda e: e.tensor_tensor(out=ycb[:], in0=ycb[:], in1=var[:], op=ALU.mult), reads=[ycb, var], writes=[ycb])
                for hd in range(8):
                    P.op("dve", lambda e, hd=hd: e.tensor_scalar(out=ycb[:, hd, :], in0=ycb[:, hd, :], scalar1=hcols[:, hd:hd + 1],
                                                                 scalar2=hcols[:, 8 + hd:9 + hd], op0=ALU.mult, op1=ALU.add),
                         reads=[ycb, hcols], writes=[ycb])
                for par in range(2):
                    pb = par * 64
                    psbb = P.next_psum()
                    for pc in range(4):
                        P.op("pe", lambda e, pc=pc, pb=pb, psbb=psbb: e.matmul(
                            psbb[:64, pc * TE:(pc + 1) * TE], lhsT=ident[pb:pb + 64, pb:pb + 64],
                            rhs=bonus[pb:pb + 64, pc, :], start=True, stop=True), reads=[ident, bonus], writes=[psbb])
                    P.op("dve", lambda e, par=par, psbb=psbb: e.tensor_tensor(
                        out=ycb[:].rearrange("p (a q) t -> p a q t", q=2)[:, :, par, :],
                        in0=psbb[:64, 0:4 * TE].rearrange("p (a t) -> p a t", t=TE),
                        in1=ycb[:].rearrange("p (a q) t -> p a q t", q=2)[:, :, par, :], op=ALU.add),
                        reads=[psbb, ycb], writes=[ycb])
                yield
                psg = [P.next_psum() for _ in range(8 // NH)]
                for hd in range(8):
                    P.op("pe", lambda e, hd=hd: e.matmul(psg[hd // NH][:64, (hd % NH) * TE:(hd % NH + 1) * TE],
                                                        lhsT=g_up[:, hd * 64:(hd + 1) * 64], rhs=sdg[:], start=True, stop=True),
                         reads=[g_up, sdg], writes=[psg[hd // NH]])
                for b in range(8 // NH):
                    P.op("dve", lambda e, b=b: e.tensor_tensor(out=yfin[:, b * NH:(b + 1) * NH, :],
                                                               in0=psg[b][:64, :].rearrange("p (h t) -> p h t", t=TE),
                                                               in1=ycb[:, b * NH:(b + 1) * NH, :], op=ALU.mult),
                         reads=[psg[b], ycb], writes=[yfin])
                yield
                for o in range(NC8):
                    wc = woc_t[ocnt[0] % 2]
                    wr = wor_t[ocnt[0] % 2]
                    ocnt[0] += 1
                    P.dma("sp", wc, wc[:], woc_bf, woc_bf.t[o])
                    P.dma("sp", wr, wr[:], wor_bf, wor_bf.t[o])
                    ps = P.next_psum()
                    for pc in range(4):
                        P.op("pe", lambda e, pc=pc, wc=wc, ps=ps: e.matmul(ps[:, :TE], lhsT=wc[:, pc, :], rhs=yconv[:, pc, :],
                                                                          start=(pc == 0), stop=False), reads=[wc, yconv], writes=[ps])
                    for hd in range(8):
                        P.op("pe", lambda e, hd=hd, wr=wr, ps=ps: e.matmul(ps[:, :TE], lhsT=wr[:, hd, :], rhs=yfin[:, hd, :],
                                                                          start=False, stop=(hd == 7)), reads=[wr, yfin], writes=[ps])
                    P.op("dve", lambda e, o=o, ps=ps, xt=xt: e.tensor_tensor(out=xt[:, o, :], in0=ps[:, :TE], in1=xt[:, o, :], op=ALU.add),
                         reads=[ps, xt], writes=[xt])
                    if o % 2 == 1:
                        yield
                P.dma("pool", XTb[tt], XTv[:, :, ts], xt, xt[:])

            HSETS = [hset0, hset1]

            def bind(k):
                nonlocal AR, Bt, Kt, BhT, KhT, bonus, wCT, RtL, sdg, yconv, PR
                (AR, Bt, Kt, BhT, KhT, bonus, wCT, RtL, sdg, yconv, PR) = HSETS[k]

            def step(gen, k):
                bind(k)
                try:
                    next(gen)
                    return True
                except StopIteration:
                    return False

            ga = phase_a(0)
            while step(ga, 0):
                pass
            for te in range(NTE):
                gb_ = phase_b(te)
                ga = phase_a(te + 1) if te + 1 < NTE else None
                alive_b = True
                alive_a = ga is not None
                while alive_b or alive_a:
                    if alive_b:
                        alive_b = step(gb_, te % 2)
                    if alive_a:
                        alive_a = step(ga, (te + 1) % 2)
            P.barrier()
            A.reset(m0)

        seq = []
        for layer in range(cfg.layers):
            seq.append(("ffn", layer, 0))
            seq.append(("mix", layer, 0))
            seq.append(("ffn", layer, 1))
        if cfg.stop is not None:
            seq = seq[:seq.index(cfg.stop) + 1]
        if cfg.skip_ffn:
            seq = [s for s in seq if s[0] != "ffn"]
        if cfg.seq is not None:
            seq = list(cfg.seq)
        evens = [s for s in seq if s[0] == "mix" and s[1] % 2 == 0]
        if evens:
            cast_even(evens[0][1] // 2)
        evens_pending = evens[1:]
        ffns = [s for s in seq if s[0] == "ffn"]
        if ffns:
            cast_weights(ffns[0][1], ffns[0][2])
        for s in seq:
            if s[0] == "ffn":
                k = ffns.index(s)
                if k + 1 < len(ffns):
                    cast_weights(ffns[k + 1][1], ffns[k + 1][2])
                ffn_phase(s[1], s[2])
            else:
                if s[1] % 2 == 1:
                    if evens_pending:
                        cast_even(evens_pending.pop(0)[1] // 2)
                    mla_layer(s[1])
                else:
                    while evens_pending and evens_pending[0][1] <= s[1]:
                        cast_even(evens_pending.pop(0)[1] // 2)
                    even_layer(s[1])

        P.pspool = None
        yin_tiles = [A.alloc("yin%d" % i, [NC8, 128], F32) for i in range(2)]
        yo_tiles = [A.alloc("yo%d" % i, [D], F32) for i in range(2)]
        for b in range(NB):
            yi = yin_tiles[b % 2]
            yo = yo_tiles[b % 2]
            P.dma("sp", yi, yi[:], XTb[b // 4], XTv[:, :, b * 128:(b + 1) * 128])
            for half in range(2):
                ps = P.next_psum()
                for jj in range(4):
                    c = half * 4 + jj
                    P.op("pe", lambda e, ps=ps, yi=yi, c=c, jj=jj: e.transpose(
                        out=ps[:, jj * 128:(jj + 1) * 128], in_=yi[:, c, :], identity=ident[:]),
                        reads=[yi, ident], writes=[ps])
                if half:
                    P.op("act", lambda e, ps=ps, yo=yo, half=half: e.copy(
                        out=yo[:, half * 512:(half + 1) * 512], in_=ps[:]), reads=[ps], writes=[yo])
                else:
                    P.op("dve", lambda e, ps=ps, yo=yo, half=half: e.tensor_copy(
                        out=yo[:, half * 512:(half + 1) * 512], in_=ps[:]), reads=[ps], writes=[yo])
            P.dma("pool", y_out, y_out.t[b * 128:(b + 1) * 128, :], yo, yo[:])
        P.wait_all("pool", [y_out])
        P.wait_all("sp", [y_out])
        P.emit()
    return nc


def _perm_swap(n=64):
    return np.concatenate([np.arange(n // 2, n), np.arange(0, n // 2)])


def host_prep(inputs):
    out = {}
    for k in ("norm_gains", "ffn_w_gate", "ffn_w_up", "ffn_w_down", "mla_w_kv_up", "odd_w_out",
              "mla_q_a_norm", "mla_kv_a_norm", "mla_q_norm", "mla_k_norm",
              "even_w_in", "even_w_out", "even_conv_w", "even_mu_shift", "rwkv_w0", "rwkv_a0", "rwkv_k_k", "rwkv_k_a",
              "rwkv_ln_w", "rwkv_ln_b", "rwkv_w_up", "rwkv_a_up", "rwkv_g_up"):
        out[k] = np.ascontiguousarray(inputs[k], dtype=np.float32)
    out["rwkv_r_k"] = np.ascontiguousarray(np.asarray(inputs["rwkv_r_k"], dtype=np.float32).reshape(2, 512))
    sw = _perm_swap(64)
    win = np.asarray(inputs["odd_w_in"], dtype=np.float32)
    out["odd_w_in_p"] = np.ascontiguousarray(np.concatenate([win, win[:, :, 640:704][:, :, sw]], axis=2))
    wq = np.asarray(inputs["mla_w_q_up"], dtype=np.float32).reshape(2, QR, 8, 192)
    wqp = np.concatenate([wq, wq[:, :, :, 128:192][:, :, :, sw]], axis=3)
    out["mla_w_q_up_p"] = np.ascontiguousarray(wqp.reshape(2, QR, 2048))
    qk = np.zeros((2, 128, 6), np.float32)
    for j in range(2):
        gq = np.asarray(inputs["mla_q_norm"][j], dtype=np.float32)
        gk = np.asarray(inputs["mla_k_norm"][j], dtype=np.float32)
        qk[j, :, 0] = gq[:128]
        qk[j, :64, 1] = gq[128:]
        qk[j, :64, 2] = gq[128:][sw]
        qk[j, :, 3] = gk[:128]
        qk[j, :64, 4] = gk[128:]
        qk[j, :64, 5] = gk[128:][sw]
    out["qk_cols"] = qk
    inv_freq = (np.float32(10000.0) ** (-np.arange(0, 64, 2, dtype=np.float32) / np.float32(64))).astype(np.float32)
    rc = np.zeros((64, 2), np.float32)
    rc[:, 0] = np.concatenate([inv_freq, inv_freq])
    rc[:32, 1] = -1.0
    rc[32:, 1] = 1.0
    out["rope_c"] = rc
    return out


def make_in_maps(inputs, ncores, T):
    shared = host_prep(inputs)
    maps = []
    for i in range(ncores):
        m = dict(shared)
        m["x"] = np.ascontiguousarray(inputs["x"][i, :T], dtype=np.float32)
        m["positions"] = np.ascontiguousarray(inputs["positions"][i:i + 1, :T]).astype(np.int32)
        maps.append(m)
    return maps


def kernel(**inputs):
    cfg = Cfg()
    nc = build(cfg)
    in_maps = make_in_maps(inputs, 8, cfg.T)
    res = run_bass_kernel_spmd(nc, in_maps, core_ids=list(range(8)))
    return np.stack([np.asarray(r["y"]) for r in res.results], axis=0).astype(np.float32)
```
